# Optimizing a Trainium2 kernel written in Bass

```python
import math
import jax, jax.numpy as jnp
from jax import lax
import numpy as np

D_MODEL = 1024
BATCH = 8
SEQ = 4096
DEPTH = 1
DEC_BATCH = 4
DEC_SEQ = 8192
PAST_LEN = 128

HEAD_DIM = 64
N_HEADS_DIL = 8
DIL_WIDTH = N_HEADS_DIL * HEAD_DIM
N_HEADS_GLA = 4
GLA_DK = 64
GLA_DV = 128
GLA_KEY_WIDTH = N_HEADS_GLA * GLA_DK
GLA_WIDTH = N_HEADS_GLA * GLA_DV
GLA_RANK = 16
GLA_TAU = 16.0
GLA_CHUNK = 64
DILATED_PATTERNS = ((128, 1), (512, 4), (2048, 16))
ROT_DIM = HEAD_DIM // 4
ROPE_THETA = 500000.0
D_FF = ((8 * D_MODEL // 3 + 255) // 256) * 256
EPS = 1e-6
NEG_INF = -1e30
PROJ_SPLITS = (DIL_WIDTH, DIL_WIDTH, DIL_WIDTH, GLA_KEY_WIDTH, GLA_KEY_WIDTH, GLA_WIDTH, GLA_WIDTH, 2 * GLA_RANK)
PROJ_WIDTH = sum(PROJ_SPLITS)
PROJ_OFFSETS = tuple(int(o) for o in np.cumsum(PROJ_SPLITS)[:-1])

kernel_name = "hybrid_gla_dilated_encoder"


def _rmsnorm(x, g):
    xf = x.astype(jnp.float32)
    y = xf * lax.rsqrt(jnp.mean(xf * xf, axis=-1, keepdims=True) + EPS)
    return (y * g.astype(jnp.float32)).astype(x.dtype)


def _partial_rotary(x):
    S = x.shape[1]
    half = ROT_DIM // 2
    inv_freq = jnp.power(jnp.float32(ROPE_THETA), -jnp.arange(0, ROT_DIM, 2, dtype=jnp.float32) / ROT_DIM)
    ang = jnp.arange(S, dtype=jnp.float32)[:, None] * inv_freq[None, :]
    cos = jnp.cos(ang)[None, :, None, :]
    sin = jnp.sin(ang)[None, :, None, :]
    rot = x[..., :ROT_DIM].astype(jnp.float32)
    x1, x2 = rot[..., :half], rot[..., half:]
    rot = jnp.concatenate([x1 * cos - x2 * sin, x1 * sin + x2 * cos], axis=-1)
    return jnp.concatenate([rot.astype(x.dtype), x[..., ROT_DIM:]], axis=-1)


def _to_strided(x, dil):
    B, S = x.shape[0], x.shape[1]
    rest = x.shape[2:]
    L = S // dil
    return jnp.swapaxes(x.reshape((B, L, dil) + rest), 1, 2).reshape((B * dil, L) + rest)


def _from_strided(x, B, dil):
    L = x.shape[1]
    rest = x.shape[2:]
    return jnp.swapaxes(x.reshape((B, dil, L) + rest), 1, 2).reshape((B, L * dil) + rest)


def _banded_attention(q, k, v, half):
    N, L, H, D = q.shape
    blk = half
    nb = -(-L // blk)
    Lp = nb * blk
    qb = jnp.pad(q, ((0, 0), (0, Lp - L), (0, 0), (0, 0))).reshape(N, nb, blk, H, D)

    def windows(t):
        tp = jnp.pad(t, ((0, 0), (blk, blk + Lp - L), (0, 0), (0, 0))).reshape(N, nb + 2, blk, H, D)
        return jnp.concatenate([tp[:, :-2], tp[:, 1:-1], tp[:, 2:]], axis=2)

    kw, vw = windows(k), windows(v)
    s = jnp.einsum('nbqhd,nbkhd->nbhqk', qb, kw, preferred_element_type=jnp.float32)
    blocks = jnp.arange(nb)[:, None] * blk
    qpos = blocks + jnp.arange(blk)[None, :]
    kpos = blocks - blk + jnp.arange(3 * blk)[None, :]
    rel = kpos[:, None, :] - qpos[:, :, None]
    valid = (jnp.abs(rel) <= half) & (kpos[:, None, :] >= 0) & (kpos[:, None, :] < L)
    s = jnp.where(valid[None, :, None], s, NEG_INF)
    lse = jax.nn.logsumexp(s, axis=-1)
    p = jnp.exp(s - lse[..., None])
    o = jnp.einsum('nbhqk,nbkhd->nbqhd', p.astype(v.dtype), vw).reshape(N, Lp, H, D)[:, :L]
    lse = jnp.transpose(lse, (0, 1, 3, 2)).reshape(N, Lp, H)[:, :L]
    return o, lse


def _dilated_mixture(q, k, v):
    B = q.shape[0]
    outs, lses = [], []
    for window, dil in DILATED_PATTERNS:
        half = window // (2 * dil)
        o, l = _banded_attention(_to_strided(q, dil), _to_strided(k, dil), _to_strided(v, dil), half)
        outs.append(_from_strided(o, B, dil))
        lses.append(_from_strided(l, B, dil))
    w = jax.nn.softmax(jnp.stack(lses), axis=0)
    o = jnp.einsum('gbsh,gbshd->bshd', w, jnp.stack(outs).astype(jnp.float32))
    return o.astype(q.dtype)


def _gla_direction(q, k, v, log_a, inclusive):
    B, S, H, dk = q.shape
    dv = v.shape[-1]
    C = GLA_CHUNK
    nc = S // C
    f32 = jnp.float32
    q = q.astype(f32).reshape(B, nc, C, H, dk)
    k = k.astype(f32).reshape(B, nc, C, H, dk)
    v = v.astype(f32).reshape(B, nc, C, H, dv)
    b = jnp.cumsum(log_a.astype(f32).reshape(B, nc, C, H, dk), axis=2)
    q_in = q * jnp.exp(b)
    k_in = k * jnp.exp(-b)
    mask = jnp.tril(jnp.ones((C, C), dtype=bool), k=0 if inclusive else -1)
    att = jnp.where(mask, jnp.einsum('bnthk,bnshk->bnhts', q_in, k_in), 0.0)
    o = jnp.einsum('bnhts,bnshv->bnthv', att, v)
    b_last = b[:, :, -1:]
    kv = jnp.einsum('bnshk,bnshv->nbhkv', k * jnp.exp(b_last - b), v)
    decay = jnp.transpose(jnp.exp(b_last[:, :, 0]), (1, 0, 2, 3))

    def step(state, inp):
        d, u = inp
        return d[..., None] * state + u, state

    _, states = lax.scan(step, jnp.zeros((B, H, dk, dv), f32), (decay, kv))
    o = o + jnp.einsum('bnthk,nbhkv->bnthv', q_in, states)
    return o.reshape(B, S, H, dv)


def _gla_mixer(q, k, v, r, g_lr, w_gate_f, b_gate_f, w_gate_b, b_gate_b, g_gla):
    B, S = q.shape[0], q.shape[1]
    q = q.reshape(B, S, N_HEADS_GLA, GLA_DK) * (GLA_DK ** -0.5)
    k = k.reshape(B, S, N_HEADS_GLA, GLA_DK)
    v = v.reshape(B, S, N_HEADS_GLA, GLA_DV)
    gf, gb = g_lr[..., :GLA_RANK], g_lr[..., GLA_RANK:]
    log_af = (jax.nn.log_sigmoid((gf @ w_gate_f + b_gate_f).astype(jnp.float32)) / GLA_TAU).reshape(B, S, N_HEADS_GLA, GLA_DK)
    log_ab = (jax.nn.log_sigmoid((gb @ w_gate_b + b_gate_b).astype(jnp.float32)) / GLA_TAU).reshape(B, S, N_HEADS_GLA, GLA_DK)
    o_f = _gla_direction(q, k, v, log_af, True)
    flip = lambda t: jnp.flip(t, axis=1)
    o_b = flip(_gla_direction(flip(q), flip(k), flip(v), flip(log_ab), False))
    o = _rmsnorm((o_f + o_b).astype(r.dtype), g_gla)
    return o.reshape(B, S, GLA_WIDTH) * jax.nn.silu(r)


def _encoder_layer(x, g_mix, w_in, w_gate_f, b_gate_f, w_gate_b, b_gate_b, g_gla, w_out,
                   g_ffn, w_ffn_gate, w_ffn_up, w_ffn_down):
    B, S, _ = x.shape
    h = _rmsnorm(x, g_mix)
    proj = h @ w_in
    qd, kd, vd, qa, ka, va, ra, g_lr = jnp.split(proj, PROJ_OFFSETS, axis=-1)
    hs = (B, S, N_HEADS_DIL, HEAD_DIM)
    qd = _partial_rotary(qd.reshape(hs)) * (HEAD_DIM ** -0.5)
    kd = _partial_rotary(kd.reshape(hs))
    vd = vd.reshape(hs)
    o_dil = _dilated_mixture(qd, kd, vd).reshape(B, S, DIL_WIDTH)
    o_gla = _gla_mixer(qa, ka, va, ra, g_lr, w_gate_f, b_gate_f, w_gate_b, b_gate_b, g_gla)
    x = x + jnp.concatenate([o_dil, o_gla], axis=-1) @ w_out
    h = _rmsnorm(x, g_ffn)
    x = x + (jax.nn.silu(h @ w_ffn_gate) * (h @ w_ffn_up)) @ w_ffn_down
    return x


def _trunk(x, g_mix, w_in, w_gate_f, b_gate_f, w_gate_b, b_gate_b, g_gla, w_out,
           g_ffn, w_ffn_gate, w_ffn_up, w_ffn_down, g_final):
    for l in range(DEPTH):
        x = _encoder_layer(x, g_mix[l], w_in[l], w_gate_f[l], b_gate_f[l], w_gate_b[l], b_gate_b[l],
                           g_gla[l], w_out[l], g_ffn[l], w_ffn_gate[l], w_ffn_up[l], w_ffn_down[l])
    return _rmsnorm(x, g_final)


def setup_inputs(seed: int = 0) -> dict:
    key = jax.random.key(seed)
    ks = jax.random.split(key, 16)
    f32 = jnp.float32
    nrm = lambda k, shape, scale: jax.random.normal(k, shape, f32) * scale
    gain = lambda k, n: jnp.ones((DEPTH, n), f32) + nrm(k, (DEPTH, n), 0.02)
    return {
        "x_prompt": nrm(ks[0], (BATCH, SEQ, D_MODEL), 1.0),
        "x_sample": nrm(ks[1], (DEC_BATCH, DEC_SEQ, D_MODEL), 1.0),
        "g_mix": gain(ks[2], D_MODEL),
        "w_in": nrm(ks[3], (DEPTH, D_MODEL, PROJ_WIDTH), D_MODEL ** -0.5),
        "w_gate_f": nrm(ks[4], (DEPTH, GLA_RANK, GLA_KEY_WIDTH), GLA_RANK ** -0.5),
        "b_gate_f": nrm(ks[5], (DEPTH, GLA_KEY_WIDTH), 0.1),
        "w_gate_b": nrm(ks[6], (DEPTH, GLA_RANK, GLA_KEY_WIDTH), GLA_RANK ** -0.5),
        "b_gate_b": nrm(ks[7], (DEPTH, GLA_KEY_WIDTH), 0.1),
        "g_gla": gain(ks[8], GLA_DV),
        "w_out": nrm(ks[9], (DEPTH, D_MODEL, D_MODEL), D_MODEL ** -0.5),
        "g_ffn": gain(ks[10], D_MODEL),
        "w_ffn_gate": nrm(ks[11], (DEPTH, D_MODEL, D_FF), D_MODEL ** -0.5),
        "w_ffn_up": nrm(ks[12], (DEPTH, D_MODEL, D_FF), D_MODEL ** -0.5),
        "w_ffn_down": nrm(ks[13], (DEPTH, D_FF, D_MODEL), D_FF ** -0.5),
        "g_final": jnp.ones((D_MODEL,), f32) + nrm(ks[14], (D_MODEL,), 0.02),
    }


def reference(x_prompt, x_sample, g_mix, w_in, w_gate_f, b_gate_f, w_gate_b, b_gate_b, g_gla, w_out,
              g_ffn, w_ffn_gate, w_ffn_up, w_ffn_down, g_final):
    y_prompt = _trunk(x_prompt, g_mix, w_in, w_gate_f, b_gate_f, w_gate_b, b_gate_b, g_gla, w_out,
                      g_ffn, w_ffn_gate, w_ffn_up, w_ffn_down, g_final)
    y_sample = _trunk(x_sample, g_mix, w_in, w_gate_f, b_gate_f, w_gate_b, b_gate_b, g_gla, w_out,
                      g_ffn, w_ffn_gate, w_ffn_up, w_ffn_down, g_final)
    return (y_prompt, y_sample)
```

```python
import contextlib
import numpy as np
import ml_dtypes
import concourse.bass as bass
import concourse.mybir as mybir
from concourse.bass_utils import run_bass_kernel_spmd

F32 = mybir.dt.float32
BF = mybir.dt.bfloat16
AF = mybir.ActivationFunctionType
ALU = mybir.AluOpType
AX = mybir.AxisListType

D = 1024
DFF = 2816
PW = 3104
SEG = 4096
HALO = 1024
EXT = SEG + 2 * HALO
EPS = 1e-6
DEBUG = False
MASK_MODES = ("dve", "pool", "dve")
SAME_SYNC = ("act", "dve", "pool")
ENG_NAMES = ("pe", "act", "dve", "pool", "sp")
C_QD, C_KD, C_VD, C_QA, C_KA, C_VA, C_RA, C_G = 0, 512, 1024, 1536, 1792, 2048, 2560, 3072


class _Rec:
    def __getattr__(self, name):
        def f(*a, **k):
            self.call = (name, a, k)
            return None
        return f


class Op:
    __slots__ = ("eng", "fn", "deps", "sig", "cnt", "dsem", "dval")

    def __init__(self, eng, fn, deps, dsem=None, dval=0):
        self.eng = eng
        rec = _Rec()
        fn(rec)
        self.fn = rec.call
        self.deps = deps
        self.sig = False
        self.cnt = 0
        self.dsem = dsem
        self.dval = dval


class Buf:
    __slots__ = ("w", "r")

    def __init__(self):
        self.w = None
        self.r = []


class Sched:
    def __init__(self, nc):
        self.nc = nc
        self.ops = {e: [] for e in ENG_NAMES}
        self.dma_cnt = {}
        self.last_dma = {}
        self.bar = []

    def op(self, eng, fn, reads=(), writes=(), dsem=None):
        deps = list(self.bar)
        for b in reads:
            if b.w is not None:
                deps.append(b.w)
        for b in writes:
            if b.w is not None:
                deps.append(b.w)
            deps.extend(b.r)
        dval = 0
        if dsem is not None:
            self.dma_cnt[dsem] = self.dma_cnt.get(dsem, 0) + 16
            dval = self.dma_cnt[dsem]
        o = Op(eng, fn, deps, dsem, dval)
        if dsem is not None:
            self.last_dma[dsem] = o
        self.ops[eng].append(o)
        for b in reads:
            b.r.append(o)
        for b in writes:
            b.w = o
            b.r = []
        return o

    def barrier(self):
        bar = []
        for e in ENG_NAMES:
            for o in reversed(self.ops[e]):
                if o.dsem is None:
                    bar.append(o)
                    break
        bar.extend(self.last_dma.values())
        self.bar = bar

    def emit(self, stack):
        nc = self.nc
        for e in ENG_NAMES:
            for o in self.ops[e]:
                for d in o.deps:
                    if d.dsem is None and (d.eng != o.eng or o.eng in SAME_SYNC):
                        d.sig = True
        esem = {}
        for e in ENG_NAMES:
            c = 0
            for o in self.ops[e]:
                if o.sig:
                    c += 1
                    o.cnt = c
            esem[e] = stack.enter_context(nc.semaphore("s_" + e))
        dsems = {k: stack.enter_context(nc.semaphore("d_%s" % k)) for k in self.dma_cnt}
        block = stack.enter_context(nc.Block())
        engobj = {"pe": block.tensor, "act": block.scalar, "dve": block.vector,
                  "pool": block.gpsimd, "sp": block.sync}
        finals = list(self.last_dma.values())

        def run(ename):
            def body(eng):
                waited = {}
                for o in self.ops[ename]:
                    need = {}
                    for d in o.deps:
                        if d.dsem is not None:
                            key, sem, val = ("d", d.dsem), dsems[d.dsem], d.dval
                        else:
                            if d.eng == ename and ename not in SAME_SYNC:
                                continue
                            key, sem, val = ("e", d.eng), esem[d.eng], d.cnt
                        if waited.get(key, 0) >= val:
                            continue
                        if key not in need or need[key][1] < val:
                            need[key] = (sem, val)
                    for key, (sem, val) in need.items():
                        eng.wait_ge(sem, val)
                        waited[key] = val
                    nm_, a_, k_ = o.fn
                    inst = getattr(eng, nm_)(*a_, **k_)
                    if o.dsem is not None:
                        inst.then_inc(dsems[o.dsem], 16)
                    elif o.sig:
                        inst.then_inc(esem[ename], 1)
                if ename == "sp":
                    for d in finals:
                        if waited.get(("d", d.dsem), 0) < d.dval:
                            eng.wait_ge(dsems[d.dsem], d.dval)
            return body

        for e in ENG_NAMES:
            engobj[e](run(e))


class Ring:
    def __init__(self, items):
        self.items = [(ap, Buf()) for ap in items]
        self.i = 0
        self.base = id(self) % 100000

    def next(self):
        k = self.i % len(self.items)
        self.i += 1
        ap, b = self.items[k]
        return ap, b, k


def build_program():
    nc = bass.Bass("TRN2", target_bir_lowering=False)
    st = contextlib.ExitStack()
    S = Sched(nc)

    def din(name, shape, dt=F32):
        return nc.dram_tensor(name, shape, dt, kind="ExternalInput").ap()

    def dscr(name, shape, dt=BF):
        if DEBUG:
            return nc.dram_tensor(name, shape, dt, kind="ExternalOutput").ap()
        return nc.dram_tensor(name, shape, dt).ap()

    xin = [din("xP", [SEG, D]), din("xS", [3 * SEG, D])]
    rot = [din("rotP", [2, 128, EXT]), din("rotS", [2, 128, EXT])]
    valid = [din("validP", [128, EXT // 128]), din("validS", [128, EXT // 128])]
    g_mix = din("g_mix", [D]); g_ffn = din("g_ffn", [D]); g_gla = din("g_gla", [128]); g_final = din("g_final", [D])
    w_in = din("w_in", [D, PW]); w_out = din("w_out", [D, D])
    w_g = din("w_ffn_gate", [D, DFF]); w_u = din("w_ffn_up", [D, DFF]); w_d = din("w_ffn_down", [DFF, D])
    wgate = [din("w_gate_f", [16, 256]), din("w_gate_b", [16, 256])]
    bgate = [din("b_gate_f", [256]), din("b_gate_b", [256])]
    c_identb = din("c_identb", [128, 128], BF); c_identf = din("c_identf", [128, 128])
    c_pm = din("c_pm", [128, 128], BF); c_mask = din("c_mask", [128, 256], BF); c_maskm = din("c_maskm", [128, 256], BF)
    c_M = din("c_M", [4, 128, 128]); c_G = din("c_G", [2, 128, 512], BF)
    yout = [nc.dram_tensor("yP", [SEG, D], F32, kind="ExternalOutput").ap(),
            nc.dram_tensor("yS", [SEG, D], F32, kind="ExternalOutput").ap()]
    win_b = dscr("win_b", [D, PW]); wout_b = dscr("wout_b", [D, D])
    wg_b = dscr("wg_b", [D, DFF]); wu_b = dscr("wu_b", [D, DFF]); wd_b = dscr("wd_b", [DFF, D])
    qT_s = [dscr("qT_s%d" % j, [4, 128, SEG]) for j in range(2)]
    kT_s = [dscr("kT_s%d" % j, [4, 128, EXT]) for j in range(2)]
    v_s = [dscr("v_s%d" % j, [EXT, 520]) for j in range(2)]
    o_s = [dscr("o_s%d" % j, [SEG, D]) for j in range(2)]
    B_scr = {}

    def dbuf(key):
        if key not in B_scr:
            B_scr[key] = Buf()
        return B_scr[key]

    def sb(name, shape, dt):
        return st.enter_context(nc.sbuf_tensor(name, shape, dt))

    identb = sb("identb", [128, 128], BF); identf = sb("identf", [128, 128], F32)
    pm = sb("pm", [128, 128], BF); mask01 = sb("mask01", [128, 256], BF); mask01m = sb("mask01m", [128, 256], BF)
    Mm = sb("Mm", [128, 4, 128], F32)
    Gm = sb("Gm", [128, 2, 512], BF)
    negcol = sb("negcol", [128, 2], F32); epsc = sb("epsc", [128, 1], F32); onec = sb("onec", [128, 1], F32)
    gt = sb("gt", [128, 18], F32)
    gfin = sb("gfin", [128, D], F32)
    wgs = sb("wgs", [32, 2, 256], F32); wga = sb("wga", [32, 2, 256], BF)
    gaug = sb("gaug", [32, 2, 512], BF)
    Sst = sb("Sst", [128, 2, 2, 128], F32)
    Fsnaps = [sb("Fsnap%d" % i, [128, 4, 2, 128], BF) for i in range(2)]
    small = sb("small", [128, 64], F32)
    ARENA = 47700
    arena = sb("arena", [128, ARENA], F32)
    psb = [st.enter_context(nc.psum_tensor("pb%d" % i, [128, 512], F32)) for i in range(8)]
    bconst = Buf()
    cnt = [0]

    def dkey(tag):
        return tag

    cl = [(identb[:], c_identb[:, :]), (identf[:], c_identf[:, :]), (pm[:], c_pm[:, :]), (mask01[:], c_mask[:, :]), (mask01m[:], c_maskm[:, :]),
          (gfin[:], g_final.partition_broadcast(128))]
    for k in range(4):
        cl.append((Mm[:, k, :], c_M[k]))
    for k in range(2):
        cl.append((Gm[:, k, :], c_G[k]))
        cl.append((wgs[0:16, k, :], wgate[k][:, :]))
        cl.append((wgs[16:17, k, :], bgate[k].rearrange("(a n) -> a n", a=1)))
    cl.append((gt[:, 0:8], g_mix.rearrange("(c p) -> p c", p=128)))
    cl.append((gt[:, 8:16], g_ffn.rearrange("(c p) -> p c", p=128)))
    cl.append((gt[:, 16:17], g_gla.rearrange("(p a) -> p a", a=1)))
    S.op("pool", lambda e: e.memset(wgs[:], 0.0), writes=[bconst])
    for i, (dst, src) in enumerate(cl):
        S.op("sp", lambda e, dst=dst, src=src: e.dma_start(out=dst, in_=src, allow_slow_non_contiguous=True),
             writes=[bconst], dsem="c%d" % (i % 4))
    S.op("dve", lambda e: e.memset(negcol[:], -1.0 / 16.0), writes=[bconst])
    S.op("dve", lambda e: e.memset(epsc[:], EPS), writes=[bconst])
    S.op("dve", lambda e: e.memset(onec[:], 1.0), writes=[bconst])
    S.op("dve", lambda e: e.memset(gt[:, 17:18], 1.0), writes=[bconst])
    S.op("dve", lambda e: e.memset(gaug[:], 1.0), writes=[bconst])
    S.barrier()
    S.op("dve", lambda e: e.tensor_copy(wga[0:17], wgs[0:17]), reads=[bconst], writes=[bconst])

    class Arena:
        def __init__(self):
            self.off = 0

        def take(self, nbytes_per_part, dt, shape):
            words = (nbytes_per_part + 3) // 4
            words = (words + 7) // 8 * 8
            a = arena[:, self.off:self.off + words]
            self.off += words
            assert self.off <= ARENA, ("arena overflow", self.off)
            if dt == BF:
                a = a.bitcast(BF)
                n = int(np.prod(shape[1:]))
                a = a[:, 0:n]
            else:
                n = int(np.prod(shape[1:]))
                a = a[:, 0:n]
            if len(shape) == 3:
                a = a.rearrange("p (a b) -> p a b", b=shape[2])
            elif len(shape) == 4:
                a = a.rearrange("p (a b c) -> p a b c", b=shape[2], c=shape[3])
            return a

        def ring(self, n, dt, shape):
            esz = 2 if dt == BF else 4
            nb = int(np.prod(shape[1:])) * esz
            return Ring([self.take(nb, dt, shape) for _ in range(n)])

    PS = Ring([p[:] for p in psb])

    def psbf(ap):
        return ap.bitcast(BF)

    smallr = Ring([small[:, k * 4:k * 4 + 4] for k in range(16)])

    def smallcol(n=1):
        ap, b, _ = smallr.next()
        return ap[:, 0:n], b

    A = Arena()
    stg = A.ring(4, F32, [128, PW])
    stgb = A.ring(4, BF, [128, PW])
    wi = [0]
    for (W, Wb, K, N, g0) in [(w_in, win_b, D, PW, 0), (w_out, wout_b, D, D, None), (w_g, wg_b, D, DFF, 8),
                              (w_u, wu_b, D, DFF, 8), (w_d, wd_b, DFF, D, None)]:
        for kc in range(K // 128):
            a, ab, ak = stg.next()
            o, ob, ok = stgb.next()
            gcol = 17
            if g0 is not None:
                gcol = g0 + kc
            elif W is w_out and kc >= 4:
                gcol = 16
            S.op("sp", lambda e, a=a, W=W, kc=kc, N=N: e.dma_start(out=a[:, 0:N], in_=W[kc * 128:(kc + 1) * 128, :]),
                 writes=[ab], dsem="wl%d" % ak)
            if wi[0] % 2 == 0:
                S.op("act", lambda e, a=a, o=o, N=N, gcol=gcol: e.activation(out=o[:, 0:N], in_=a[:, 0:N], func=AF.Copy,
                                                                                 scale=gt[:, gcol:gcol + 1]),
                     reads=[ab, bconst], writes=[ob])
            else:
                S.op("dve", lambda e, a=a, o=o, N=N, gcol=gcol: e.tensor_scalar(out=o[:, 0:N], in0=a[:, 0:N],
                                                                                    scalar1=gt[:, gcol:gcol + 1], scalar2=None,
                                                                                    op0=ALU.mult),
                     reads=[ab, bconst], writes=[ob])
            wi[0] += 1
            S.op("pool", lambda e, o=o, Wb=Wb, kc=kc, N=N: e.dma_start(out=Wb[kc * 128:(kc + 1) * 128, :], in_=o[:, 0:N]),
                 reads=[ob], dsem="ws%d" % ok)
    S.barrier()

    A = Arena()
    winb = A.take(8 * PW * 2, BF, [128, 8, PW])
    Ssnap = A.take(32 * 2 * 128 * 2, BF, [128, 32, 2, 128])
    Xr = A.ring(2, F32, [128, D])
    Hr = A.ring(2, BF, [128, D])
    hTr = A.ring(3, BF, [128, 8, 512])
    junk = A.take(D * 2, BF, [128, D])
    sprs = [[[A.take(1024, F32, [128, 256]) for _ in range(2)] for _ in range(4)] for _ in range(2)]
    spbs = [[[Buf() for _ in range(2)] for _ in range(4)] for _ in range(2)]
    etmp = A.ring(1, F32, [128, 256])
    eDr = A.ring(1, F32, [128, 256])
    kdr = A.ring(8, BF, [128, 256])
    vbr = A.ring(8, BF, [128, 512])
    decr = A.ring(8, F32, [128, 4])
    Er = A.ring(4, F32, [128, 512])
    qin = [[A.take(1024, BF, [128, 512]) for _ in range(2)] for _ in range(2)]
    kin = [[A.take(1024, BF, [128, 512]) for _ in range(2)] for _ in range(2)]
    qinb = [[Buf() for _ in range(2)] for _ in range(2)]
    kinb = [[Buf() for _ in range(2)] for _ in range(2)]
    attr = A.ring(6, BF, [128, 512])
    sqr = A.ring(2, F32, [128, 512])
    onr = A.ring(2, F32, [128, 512])
    err = A.ring(2, F32, [128, 512])
    ogr = A.ring(2, BF, [128, 512])
    rotr = A.ring(1, F32, [128, 2, 512])
    qrawr = A.ring(2, BF, [128, 512])
    t1r = A.ring(1, F32, [128, 512])
    t2r = A.ring(1, F32, [128, 512])
    qfr = A.ring(2, BF, [128, 512])
    vaugr = A.ring(2, BF, [128, 8, 65])
    validt = A.take(48 * 4, F32, [128, 48])
    zt = Ssnap.rearrange("p a b c -> p (a b c)")[:, 0:1024]
    bwin = Buf(); bvalid = Buf()
    bS = [Buf(), Buf()]; bSq = [[Buf() for _ in range(4)] for _ in range(2)]; bSsnap = Buf(); bzt = bSsnap; bFsnaps = [Buf(), Buf()]; bgaug = [Buf(), Buf()]

    for kc in range(8):
        S.op("sp", lambda e, kc=kc: e.dma_start(out=winb[:, kc, :], in_=win_b[kc * 128:(kc + 1) * 128, :]),
             writes=[bwin], dsem="wi%d" % (kc % 2))
    S.op("pool", lambda e: e.memset(zt[:], 0.0), writes=[bzt])

    def mm_group(out, pairs, ps_b, reads):
        n = len(pairs)
        for i, (l, r) in enumerate(pairs):
            S.op("pe", lambda e, l=l, r=r, i=i: e.matmul(out, l, r, start=(i == 0), stop=(i == n - 1)),
                 reads=reads, writes=[ps_b])

    def rstd_of(src, src_b, width):
        ss, ssb = smallcol()
        S.op("act", lambda e: e.activation(out=junk[:, 0:width], in_=src, func=AF.Square, accum_out=ss),
             reads=[src_b], writes=[ssb, bjunk])
        ln, lnb = smallcol()
        S.op("act", lambda e: e.activation(out=ln, in_=ss, func=AF.Ln, scale=1.0 / width, bias=epsc[:, 0:1]),
             reads=[ssb, bconst], writes=[lnb])
        rs, rsb = smallcol()
        S.op("act", lambda e: e.activation(out=rs, in_=ln, func=AF.Exp, scale=-0.5), reads=[lnb], writes=[rsb])
        return rs, rsb

    bjunk = Buf()

    dbg = {}
    if DEBUG:
        dbg["x"] = nc.dram_tensor("dbgx", [32, 128, D], F32, kind="ExternalOutput").ap()
        dbg["h"] = nc.dram_tensor("dbgh", [32, 128, D], BF, kind="ExternalOutput").ap()
        dbg["r"] = nc.dram_tensor("dbgr", [32, 128, 4], F32, kind="ExternalOutput").ap()
        dbg["hT"] = nc.dram_tensor("dbghT", [8, 128, 8, 512], BF, kind="ExternalOutput").ap()

    def prep_tile(xsrc_rows, hT, hTb, col, dbi=None):
        xt, xb, xk = Xr.next()
        S.op("sp", lambda e: e.dma_start(out=xt, in_=xsrc_rows), writes=[xb], dsem="x%d" % xk)
        rs, rsb = rstd_of(xt, xb, D)
        hb, hbb, _ = Hr.next()
        S.op("dve", lambda e: e.tensor_scalar(out=hb, in0=xt, scalar1=rs, scalar2=None, op0=ALU.mult),
             reads=[xb, rsb], writes=[hbb])
        pt, ptb, _ = PS.next()
        ptv = psbf(pt)
        for kc in range(8):
            S.op("pe", lambda e, kc=kc: e.transpose(ptv[:, kc * 128:(kc + 1) * 128], hb[:, kc * 128:(kc + 1) * 128], identb[:]),
                 reads=[hbb, bconst], writes=[ptb])
        S.op("act", lambda e: e.activation(out=hT[:, :, col:col + 128], in_=ptv.rearrange("p (a b) -> p a b", b=128),
                                           func=AF.Copy),
             reads=[ptb], writes=[hTb])
        if DEBUG and dbi is not None:
            S.op("sp", lambda e: e.dma_start(out=dbg["x"][dbi], in_=xt), reads=[xb], dsem="dbg0")
            S.op("sp", lambda e: e.dma_start(out=dbg["h"][dbi], in_=hb), reads=[hbb], dsem="dbg1")
            S.op("sp", lambda e: e.dma_start(out=dbg["r"][dbi][:, 0:1], in_=rs, allow_slow_non_contiguous=True), reads=[rsb], dsem="dbg2")
            if col == 384:
                S.op("sp", lambda e: e.dma_start(out=dbg["hT"][dbi // 4], in_=hT), reads=[hTb], dsem="dbg3")

    def interleave(gens, width):
        pending = list(gens)
        active = []
        while pending or active:
            while pending and len(active) < width:
                active.append(pending.pop(0))
            for g in list(active):
                try:
                    next(g)
                except StopIteration:
                    active.remove(g)
                yield

    def run_streams(gens, width):
        for _ in interleave(gens, width):
            pass

    def gla_tokmajor(hT, hTb, ti, col, dirs, scan, snap_dst, res, gset, split_state=False):
        spr, spb = sprs[gset], spbs[gset]
        bFsnap = bFsnaps[gset]
        lh = [hT[:, kc, col:col + 128] for kc in range(8)]
        pv, pvb, _ = PS.next()
        mm_group(pv[:, :], [(lh[kc], winb[:, kc, C_VA:C_VA + 512]) for kc in range(8)], pvb, [hTb, bwin])
        vb, vbb, _ = vbr.next()
        S.op("act", lambda e: e.activation(out=vb, in_=pv[:, :], func=AF.Copy), reads=[pvb], writes=[vbb])
        res["vb"] = (vb, vbb)
        for d in dirs:
            pz, pzb, _ = PS.next()
            S.op("pe", lambda e, d=d: e.matmul(pz[:, 0:256], gaug[0:17, d, col:col + 128], wga[0:17, d, :], start=True, stop=True),
                 reads=[bgaug[d], bconst], writes=[pzb])
            et_, etb, _ = etmp.next()
            S.op("act", lambda e, et_=et_: e.activation(out=et_, in_=pz[:, 0:256], func=AF.Exp, scale=-1.0),
                 reads=[pzb], writes=[etb])
            sp_, spb_ = spr[ti][d], spb[ti][d]
            S.op("act", lambda e, et_=et_, sp_=sp_: e.activation(out=sp_, in_=et_, func=AF.Ln, bias=onec[:, 0:1]),
                 reads=[etb, bconst], writes=[spb_])
        yield
        d = scan
        sp_, spb_ = spr[ti][d], spb[ti][d]
        pd, pdb, _ = PS.next()
        S.op("pe", lambda e: e.matmul(pd[:, 0:256], Mm[:, 2 + d, :], sp_, start=True, stop=True),
             reads=[spb_, bconst], writes=[pdb])
        for j in range(2):
            S.op("pe", lambda e, j=j: e.matmul(pd[:, 256 + 2 * j:258 + 2 * j], sp_[:, j * 128:(j + 1) * 128], negcol[:, 0:2],
                                               start=True, stop=True),
                 reads=[spb_, bconst], writes=[pdb])
        eD, eDb, _ = eDr.next()
        S.op("act", lambda e: e.activation(out=eD, in_=pd[:, 0:256], func=AF.Exp), reads=[pdb], writes=[eDb])
        dec, decb, _ = decr.next()
        S.op("act", lambda e: e.activation(out=dec, in_=pd[:, 256:260], func=AF.Exp), reads=[pdb], writes=[decb])
        pk, pkb, _ = PS.next()
        mm_group(pk[:, 0:256], [(lh[kc], winb[:, kc, C_KA:C_KA + 256]) for kc in range(8)], pkb, [hTb, bwin])
        kd, kdb, _ = kdr.next()
        S.op("dve", lambda e: e.tensor_tensor(out=kd, in0=pk[:, 0:256], in1=eD, op=ALU.mult), reads=[pkb, eDb], writes=[kdb])
        yield

        def state_gen():
            if snap_dst is not None:
                S.op("act", lambda e: e.activation(out=snap_dst, in_=Sst[:, d, :, :], func=AF.Copy), reads=bSq[d],
                     writes=[bSsnap if d == 1 else bFsnap])
            pkv, pkvb, _ = PS.next()
            for j in range(2):
                S.op("pe", lambda e, j=j: e.matmul(pkv[:, j * 256:(j + 1) * 256], kd[:, j * 128:(j + 1) * 128],
                                                   vb[:, j * 256:(j + 1) * 256], start=True, stop=True),
                     reads=[kdb, vbb], writes=[pkvb])
            for j in range(2):
                for par in range(2):
                    rows = slice(par * 64, (par + 1) * 64)
                    sbuf_ = bSq[d][2 * j + par]
                    S.op("dve", lambda e, j=j, par=par, rows=rows: e.scalar_tensor_tensor(
                        out=Sst[rows, d, j, :], in0=Sst[rows, d, j, :], scalar=dec[rows, 2 * j:2 * j + 1],
                        in1=pkv[rows, j * 256 + par * 128:j * 256 + (par + 1) * 128], op0=ALU.mult, op1=ALU.add),
                        reads=[pkvb, decb, sbuf_], writes=[sbuf_])
            yield
        if split_state:
            res["state"] = state_gen()
        else:
            for _ in state_gen():
                yield

    def prep_group(job, tiles, full):
        hT, hTb, _ = hTr.next()

        def gen():
            xsrc = xin[job]
            for ti in range(4):
                et = tiles[ti]
                prep_tile(xsrc[et * 128:(et + 1) * 128, :], hT, hTb, ti * 128, dbi=(et if (full and job == 0) else None))
                yield
        return (hT, hTb), gen()

    def proc_group(job, tiles, scan, full, dilkv, gset, hTctx):
        Fsnap = Fsnaps[gset]
        base = tiles[0]
        hT, hTb = hTctx
        order = list(range(4)) if scan == 0 else [3, 2, 1, 0]
        dirs = [0, 1] if full else [scan]

        def gproj_gen():
            for d in dirs:
                pg, pgb, _ = PS.next()
                mm_group(pg[0:16, :], [(winb[:, kc, C_G + 16 * d:C_G + 16 * d + 16], hT[:, kc, :]) for kc in range(8)], pgb, [hTb, bwin])
                S.op("act", lambda e, d=d, pg=pg: e.activation(out=gaug[0:16, d, :], in_=pg[0:16, :], func=AF.Copy),
                     reads=[pgb], writes=[bgaug[d]])
            yield
        tm = {}
        gens = []
        for ti in order:
            et = tiles[ti]
            is_main = (job == 0) or (32 <= et < 64)
            mt = et - (32 if job == 1 else 0)
            if not is_main:
                snap_dst = None
            elif scan == 1:
                snap_dst = Ssnap[:, mt, :, :]
            else:
                snap_dst = Fsnap[:, ti, :, :]
            tm[ti] = {}
            gens.append(gla_tokmajor(hT, hTb, ti, ti * 128, dirs, scan, snap_dst, tm[ti], gset, split_state=(not full)))

        def dil_gen():
            det0 = (base + 8) if job == 0 else (base - 24)
            ecol = det0 * 128
            rt, rtb, rk = rotr.next()
            S.op("sp", lambda e: e.dma_start(out=rt, in_=rot[job][:, :, ecol:ecol + 512].rearrange("a p n -> p a n")),
                 writes=[rtb], dsem="rot%d" % rk)
            which = [("k", C_KD)] + ([("q", C_QD)] if full else [])

            def finish(nm, hp, qr, qrb):
                pp, ppb, _ = PS.next()
                S.op("pe", lambda e: e.matmul(pp[:, :], pm[:], qr, start=True, stop=True),
                     reads=[qrb, bconst], writes=[ppb])
                t1, t1b, _ = t1r.next()
                S.op("dve", lambda e: e.tensor_tensor(out=t1, in0=pp[:, :], in1=rt[:, 1, :], op=ALU.mult),
                     reads=[ppb, rtb], writes=[t1b])
                t2, t2b, _ = t2r.next()
                S.op("pool", lambda e: e.tensor_tensor(out=t2, in0=qr, in1=rt[:, 0, :], op=ALU.mult),
                     reads=[qrb, rtb], writes=[t2b])
                qf, qfb, qfk = qfr.next()
                S.op("dve", lambda e: e.tensor_tensor(out=qf, in0=t1, in1=t2, op=ALU.add),
                     reads=[t1b, t2b], writes=[qfb])
                if nm == "q":
                    mcol = (base - (32 if job == 1 else 0)) * 128
                    dst = qT_s[job][hp][:, mcol:mcol + 512]
                else:
                    dst = kT_s[job][hp][:, ecol:ecol + 512]
                S.op("pool", lambda e: e.dma_start(out=dst, in_=qf), reads=[qfb], dsem="qf%d" % qfk)

            prev = None
            for hp in range(4):
                for nm, c0 in which:
                    pq, pqb, _ = PS.next()
                    mm_group(pq[:, :], [(winb[:, kc, c0 + hp * 128:c0 + (hp + 1) * 128], hT[:, kc, :]) for kc in range(8)],
                             pqb, [hTb, bwin])
                    qr, qrb, _ = qrawr.next()
                    S.op("act", lambda e, qr=qr, pq=pq: e.activation(out=qr, in_=pq[:, :], func=AF.Copy), reads=[pqb], writes=[qrb])
                    if prev is not None:
                        finish(*prev)
                    prev = (nm, hp, qr, qrb)
                    yield
            for ti in range(4):
                col = ti * 128
                pv, pvb, _ = PS.next()
                mm_group(pv[:, :], [(hT[:, kc, col:col + 128], winb[:, kc, C_VD:C_VD + 512]) for kc in range(8)], pvb, [hTb, bwin])
                if prev is not None:
                    finish(*prev)
                    prev = None
                va, vab, vak = vaugr.next()
                S.op("act", lambda e, va=va, pv=pv: e.activation(out=va[:, :, 0:64], in_=pv[:, :].rearrange("p (a b) -> p a b", b=64),
                                                                   func=AF.Copy), reads=[pvb], writes=[vab])
                det = det0 + ti
                S.op("dve", lambda e, va=va, det=det: e.tensor_copy(va[:, :, 64:65],
                                                                    validt[:, det:det + 1].broadcast_to([128, 8]).rearrange("p (a b) -> p a b", b=1)),
                     reads=[bvalid], writes=[vab])
                S.op("pool", lambda e, va=va, det=det: e.dma_start(out=v_s[job][det * 128:(det + 1) * 128, :],
                                                                   in_=va.rearrange("p a b -> p (a b)")),
                     reads=[vab], dsem="va%d" % vak)
                yield

        if dilkv:
            gens.insert(2, dil_gen())

        def xgen():
            for _ in gproj_gen():
                yield
            for _ in interleave(gens, 3):
                yield
        def xbgen():
            for ti in order:
                for _ in tm[ti]["state"]:
                    yield
        return xgen(), ((job, tiles, hT, hTb, tm, gset) if full else None), (None if full else xbgen())

    def y_gen(job, tiles, hT, hTb, tm, gset):
        spr, spb = sprs[gset], spbs[gset]
        Fsnap, bFsnap = Fsnaps[gset], bFsnaps[gset]
        for j in range(2):
            EE = []
            for d in range(2):
                pb_, pbb, _ = PS.next()
                for ti in range(4):
                    S.op("pe", lambda e, ti=ti: e.matmul(pb_[:, ti * 128:(ti + 1) * 128], spr[ti][d][:, j * 128:(j + 1) * 128],
                                                         Mm[:, d, :], start=True, stop=True),
                         reads=[spb[ti][d], bconst], writes=[pbb])
                Ep, Epb, _ = Er.next()
                S.op("act", lambda e: e.activation(out=Ep, in_=pb_[:, :], func=AF.Exp), reads=[pbb], writes=[Epb])
                En, Enb, _ = Er.next()
                S.op("act", lambda e: e.activation(out=En, in_=pb_[:, :], func=AF.Exp, scale=-1.0), reads=[pbb], writes=[Enb])
                EE.append((Ep, Epb, En, Enb))
            yield
            pqa, pqab, _ = PS.next()
            mm_group(pqa[:, :], [(winb[:, kc, C_QA + j * 128:C_QA + (j + 1) * 128], hT[:, kc, :]) for kc in range(8)], pqab, [hTb, bwin])
            pka, pkab, _ = PS.next()
            mm_group(pka[:, :], [(winb[:, kc, C_KA + j * 128:C_KA + (j + 1) * 128], hT[:, kc, :]) for kc in range(8)], pkab, [hTb, bwin])
            for d in range(2):
                Ep, Epb, En, Enb = EE[d]
                S.op("dve", lambda e: e.scalar_tensor_tensor(out=qin[d][j], in0=pqa[:, :], scalar=0.125, in1=Ep,
                                                             op0=ALU.mult, op1=ALU.mult),
                     reads=[pqab, Epb], writes=[qinb[d][j]])
                S.op("dve", lambda e: e.tensor_tensor(out=kin[d][j], in0=pka[:, :], in1=En, op=ALU.mult),
                     reads=[pkab, Enb], writes=[kinb[d][j]])
            yield

        def y_tile(ti):
            et = tiles[ti]
            mt = et - (32 if job == 1 else 0)
            col = ti * 128
            vb, vbb = tm[ti]["vb"]
            atts = []
            for d in range(2):
                at, atb, _ = attr.next()
                atv = at.rearrange("p (j q t) -> p j q t", q=2, t=128)
                for par in range(2):
                    pa, pab, _ = PS.next()
                    rows = slice(par * 64, (par + 1) * 64)
                    for j in range(2):
                        S.op("pe", lambda e, j=j: e.matmul(pa[:, j * 128:(j + 1) * 128], kin[d][j][rows, col:col + 128],
                                                           qin[d][j][rows, col:col + 128], start=True, stop=True),
                             reads=[kinb[d][j], qinb[d][j]], writes=[pab])
                    S.op("dve", lambda e: e.tensor_tensor(out=atv[:, :, par, :], in0=pa[:, 0:256].rearrange("p (j t) -> p j t", t=128),
                                                          in1=Gm[:, d, 0:256].rearrange("p (j t) -> p j t", t=128), op=ALU.mult),
                         reads=[pab, bconst], writes=[atb])
                atts.append((at, atb))
            pr, prb, _ = PS.next()
            mm_group(pr[:, :], [(hT[:, kc, col:col + 128], winb[:, kc, C_RA:C_RA + 512]) for kc in range(8)], prb, [hTb, bwin])
            er, erb, _ = err.next()
            S.op("act", lambda e: e.activation(out=er, in_=pr[:, :], func=AF.Exp, scale=-1.0), reads=[prb], writes=[erb])
            S.op("pool", lambda e: e.tensor_scalar(out=er, in0=er, scalar1=1.0, scalar2=1.0, op0=ALU.add, op1=ALU.mult),
                 reads=[erb], writes=[erb])
            S.op("dve", lambda e: e.reciprocal(out=er, in_=er), reads=[erb], writes=[erb])
            S.op("dve", lambda e: e.tensor_tensor(out=er, in0=pr[:, :], in1=er, op=ALU.mult), reads=[prb, erb], writes=[erb])
            yield
            po, pob, _ = PS.next()
            for h in range(4):
                j, par = h // 2, h % 2
                rows = slice(par * 64, (par + 1) * 64)
                hc = slice(h * 128, (h + 1) * 128)
                pairs = [(atts[0][0][:, hc], vb[:, hc]), (atts[1][0][:, hc], vb[:, hc]),
                         (qin[0][j][rows, col:col + 128], Fsnap[rows, ti, j, :]),
                         (qin[1][j][rows, col:col + 128], Ssnap[rows, mt, j, :])]
                mm_group(po[:, hc], pairs, pob, [atts[0][1], atts[1][1], vbb, qinb[0][j], qinb[1][j], bFsnap, bSsnap])
            sq, sqb, _ = sqr.next()
            S.op("act", lambda e: e.activation(out=sq, in_=po[:, :], func=AF.Square), reads=[pob], writes=[sqb])
            ss, ssb = smallcol(4)
            S.op("dve", lambda e: e.tensor_reduce(out=ss, in_=sq.rearrange("p (a b) -> p a b", b=128), axis=AX.X, op=ALU.add),
                 reads=[sqb], writes=[ssb])
            ln, lnb = smallcol(4)
            S.op("act", lambda e: e.activation(out=ln, in_=ss, func=AF.Ln, scale=1.0 / 128, bias=epsc[:, 0:1]),
                 reads=[ssb, bconst], writes=[lnb])
            rs, rsb = smallcol(4)
            S.op("act", lambda e: e.activation(out=rs, in_=ln, func=AF.Exp, scale=-0.5), reads=[lnb], writes=[rsb])
            on, onb, _ = onr.next()
            for h in range(4):
                S.op("act" if h % 2 else "dve", (lambda e, h=h: e.activation(out=on[:, h * 128:(h + 1) * 128], in_=po[:, h * 128:(h + 1) * 128],
                                                                            func=AF.Copy, scale=rs[:, h:h + 1])) if h % 2 else
                     (lambda e, h=h: e.tensor_scalar(out=on[:, h * 128:(h + 1) * 128], in0=po[:, h * 128:(h + 1) * 128],
                                                     scalar1=rs[:, h:h + 1], scalar2=None, op0=ALU.mult)),
                     reads=[pob, rsb], writes=[onb])
            og, ogb, ogk = ogr.next()
            S.op("pool", lambda e: e.tensor_tensor(out=og, in0=on, in1=er, op=ALU.mult), reads=[onb, erb], writes=[ogb])
            S.op("pool", lambda e: e.dma_start(out=o_s[job][mt * 128:(mt + 1) * 128, 512:1024], in_=og),
                 reads=[ogb], dsem="og%d" % ogk)
            yield

        for _ in interleave([y_tile(ti) for ti in range(4)], 2):
            yield

    def zero_state(d):
        S.op("dve", lambda e: e.memset(Sst[:, d, :, :], 0.0), reads=bSq[d], writes=bSq[d])

    for job in range(2):
        S.op("sp", lambda e, job=job: e.dma_start(out=validt[:], in_=valid[job][:, :]), writes=[bvalid], dsem="vld")
        if job == 0:
            for hp in range(4):
                for c0 in (0, HALO + SEG):
                    S.op("pool", lambda e, hp=hp, c0=c0: e.dma_start(out=kT_s[0][hp][:, c0:c0 + HALO], in_=zt[:, 0:HALO]),
                         reads=[bzt], dsem="z0")
            for r0 in (0, HALO + SEG):
                for k in range(8):
                    S.op("pool", lambda e, r0=r0, k=k: e.dma_start(out=v_s[0][r0 + k * 128:r0 + (k + 1) * 128, :], in_=zt[:, 0:520]),
                         reads=[bzt], dsem="z1")
            moff = 0
        else:
            moff = 32
        descs = []
        if job == 1:
            for g in range(23, 15, -1):
                descs.append(([4 * g + i for i in range(4)], 1, False, g < 18))
        for g in range(7, -1, -1):
            descs.append(([moff + 4 * g + i for i in range(4)], 1, False, False))
        nback = len(descs)
        if job == 1:
            for g in range(0, 8):
                descs.append(([4 * g + i for i in range(4)], 0, False, g >= 6))
        for g in range(8):
            descs.append(([moff + 4 * g + i for i in range(4)], 0, True, True))
        zero_state(1)
        hctx, pg0 = prep_group(job, descs[0][0], descs[0][2])
        run_streams([pg0], 1)
        prev_y = None
        prev_xb = None
        nfull = 0
        for i, (tiles, scan, full, dilkv) in enumerate(descs):
            if i == nback:
                zero_state(0)
            gset = nfull % 2
            xg, yctx, xb = proc_group(job, tiles, scan, full, dilkv, gset, hctx)
            streams = [xg]
            if prev_xb is not None:
                streams.insert(0, prev_xb)
            prev_xb = xb
            if prev_y is not None:
                streams.append(y_gen(*prev_y))
            if i + 1 < len(descs):
                hctx, pgn = prep_group(job, descs[i + 1][0], descs[i + 1][2])
                streams.append(pgn)
            run_streams(streams, 4)
            prev_y = yctx
            if full:
                nfull += 1
        run_streams([y_gen(*prev_y)], 1)
    S.barrier()

    A = Arena()
    qT = A.take(SEG * 2, BF, [128, SEG]); q4 = A.take(SEG * 2, BF, [128, 4, SEG // 4]); q16 = A.take(SEG * 2, BF, [128, 16, SEG // 16])
    kT = A.take(EXT * 2, BF, [128, EXT]); k4 = A.take(EXT * 2, BF, [128, 4, EXT // 4]); k16 = A.take(EXT * 2, BF, [128, 16, EXT // 16])
    NT1, NT4, NT16 = 33, 9, 3
    vg1 = A.take(NT1 * 260, BF, [128, NT1, 130])
    vg4 = A.take(4 * NT4 * 260, BF, [128, 4 * NT4, 130])
    vg16 = A.take(16 * NT16 * 260, BF, [128, 16 * NT16, 130])
    accTs = [[A.take(SEG * 4, F32, [128, SEG]) for _ in range(2)] for _ in range(2)]
    Pr = A.ring(4, BF, [128, 256])
    Pmr = A.ring(10, BF, [128, 256])
    ostr = A.ring(2, BF, [128, 128])
    bq = Buf(); bk = Buf(); bqp = [None, Buf(), Buf()]; bkp = [None, Buf(), Buf()]; bvgs = [Buf(), Buf(), Buf()]; baccs = [[Buf(), Buf()], [Buf(), Buf()]]
    bqp[0] = bq; bkp[0] = bk
    PSs = Ring([psb[i][:, 0:256] for i in range(4)])
    PSu = [Ring([psb[4][:], psb[5][:]]), Ring([psb[6][:], psb[7][:]])]
    PSe = Ring([])
    PSe.items = [PSs.items[0], PSu[0].items[0], PSs.items[1], PSu[1].items[0], PSs.items[2], PSu[0].items[1], PSs.items[3], PSu[1].items[1]]

    for job in range(2):
        for hp in range(4):
            accT = accTs[hp % 2]; bacc = baccs[hp % 2]
            S.op("sp", lambda e, job=job, hp=hp: e.dma_start(out=qT[:], in_=qT_s[job][hp]), writes=[bq], dsem="bq")
            S.op("sp", lambda e, job=job, hp=hp: e.dma_start(out=kT[:], in_=kT_s[job][hp]), writes=[bk], dsem="bk")
            S.op("dve", lambda e: e.tensor_copy(q4[:], qT.rearrange("p (l r) -> p r l", r=4)), reads=[bq], writes=[bqp[1]])
            S.op("dve", lambda e: e.tensor_copy(k4[:], kT.rearrange("p (l r) -> p r l", r=4)), reads=[bk], writes=[bkp[1]])
            S.op("dve", lambda e: e.tensor_copy(q16[:], qT.rearrange("p (l r) -> p r l", r=16)), reads=[bq], writes=[bqp[2]])
            S.op("act", lambda e: e.activation(out=k16[:], in_=kT.rearrange("p (l r) -> p r l", r=16), func=AF.Copy), reads=[bk], writes=[bkp[2]])
            vcols = slice(hp * 130, (hp + 1) * 130)
            vsrc = v_s[job]
            for t0 in (0, 11, 22):
                S.op("sp", lambda e: e.dma_start(out=vg1[:, t0:t0 + 11, :], in_=vsrc[960 + t0 * 128:960 + (t0 + 11) * 128, vcols].rearrange("(t p) c -> p t c", p=128)),
                     writes=[bvgs[0]], dsem="bv0")
            for r in range(4):
                S.op("sp", lambda e, vsrc=vsrc, vcols=vcols, r=r: e.dma_start(
                    out=vg4[:, r * NT4:(r + 1) * NT4, :],
                    in_=vsrc[r + 768:r + 768 + 4 * (NT4 * 128 - 1) + 1:4, vcols].rearrange("(t p) c -> p t c", p=128)),
                    writes=[bvgs[1]], dsem="bv1")
            for r in range(16):
                S.op("sp", lambda e, vsrc=vsrc, vcols=vcols, r=r: e.dma_start(
                    out=vg16[:, r * NT16:(r + 1) * NT16, :],
                    in_=vsrc[r:r + 16 * (NT16 * 128 - 1) + 1:16, vcols].rearrange("(t p) c -> p t c", p=128)),
                    writes=[bvgs[2]], dsem="bv2")
            pats = [(1, 32, qT.rearrange("p (r l) -> p r l", r=1), kT.rearrange("p (r l) -> p r l", r=1), vg1, 1024),
                    (4, 8, q4, k4, vg4, 256), (16, 2, q16, k16, vg16, 64)]
            units = []
            for pi, pat in enumerate(pats):
                for r in range(pat[0]):
                    curs = [{}, {}]
                    for kt in range(pat[1] + 1):
                        for par in range(2):
                            units.append((pi, pat, r, par, kt, curs[par]))
            staged = {}

            def stage1(ui):
                pi, (dil, nqt, qp, kp, vg, Q0), r, par, kt, cur = units[ui]
                rows = slice(par * 64, (par + 1) * 64)
                qa = max(0, kt - 1); qb_ = min(nqt - 1, kt)
                n = (qb_ - qa + 1) * 128
                jlo = 128 * (qa - (kt - 1))
                k0 = Q0 + 128 * kt - 64
                ps_, psb_, _ = PSs.next()
                mode = MASK_MODES[ui % len(MASK_MODES)]
                P2, P2b, _ = Pmr.next()
                if mode == "pe":
                    S.op("pe", lambda e: e.matmul(ps_[:, 0:n], identb[:], mask01[:, jlo:jlo + n], start=True, stop=False),
                         reads=[bconst], writes=[psb_])
                    S.op("pe", lambda e: e.matmul(ps_[:, 0:n], kp[rows, r, k0:k0 + 128], qp[rows, r, 128 * qa:128 * qa + n], start=False, stop=True),
                         reads=[bqp[pi], bkp[pi]], writes=[psb_])
                    S.op("act", lambda e: e.activation(out=P2[:, 0:n], in_=ps_[:, 0:n], func=AF.Exp, scale=0.125),
                         reads=[psb_], writes=[P2b])
                else:
                    S.op("pe", lambda e: e.matmul(ps_[:, 0:n], kp[rows, r, k0:k0 + 128], qp[rows, r, 128 * qa:128 * qa + n], start=True, stop=True),
                         reads=[bqp[pi], bkp[pi]], writes=[psb_])
                    P, Pb, _ = Pr.next()
                    S.op("act", lambda e: e.activation(out=P[:, 0:n], in_=ps_[:, 0:n], func=AF.Exp, scale=0.125),
                         reads=[psb_], writes=[Pb])
                    S.op(mode, lambda e: e.tensor_tensor(out=P2[:, 0:n], in0=P[:, 0:n], in1=mask01m[:, jlo:jlo + n], op=ALU.mult),
                         reads=[Pb, bconst], writes=[P2b])
                staged[ui] = (P2, P2b, qa)

            def stage2(ui):
                pi, (dil, nqt, qp, kp, vg, Q0), r, par, kt, cur = units[ui]
                ntile = nqt + 1
                acc = accT[par]
                P2, P2b, qa = staged.pop(ui)
                vgt = vg[:, r * ntile + kt, par * 65:(par + 1) * 65]
                qts = [qt for qt in (kt - 1, kt) if 0 <= qt <= nqt - 1]
                if len(qts) == 2 and kt % 4 != 0:
                    groups = [qts]
                else:
                    groups = [[qt] for qt in qts]
                for grp in groups:
                    qt0 = grp[0]
                    bank = qt0 // 4
                    first = (bank not in cur)
                    if first:
                        cur[bank] = PSu[par].next()
                    pu, pub, _ = cur[bank]
                    last = (kt == grp[-1] + 1) and (grp[-1] % 4 == 3 or grp[-1] == nqt - 1)
                    c0 = (qt0 % 4) * 128
                    nn = 128 * len(grp)
                    p0 = (qt0 - qa) * 128
                    S.op("pe", lambda e: e.matmul(pu[0:65, c0:c0 + nn], vgt, P2[:, p0:p0 + nn], start=first, stop=last),
                         reads=[P2b, bvgs[pi]], writes=[pub])
                    if last:
                        qt = grp[-1]
                        q0 = (qt // 4) * 4
                        ncol = (qt - q0 + 1) * 128
                        s0 = r + dil * 128 * q0
                        dst = acc[0:65, s0:s0 + dil * (ncol - 1) + 1:dil]
                        if pi == 0:
                            S.op("dve", lambda e: e.tensor_copy(dst, pu[0:65, 0:ncol]),
                                 reads=[pub], writes=[bacc[par]])
                        else:
                            S.op("dve", lambda e: e.tensor_tensor(out=dst, in0=pu[0:65, 0:ncol], in1=dst, op=ALU.add),
                                 reads=[pub, bacc[par]], writes=[bacc[par]])

            LA = 3
            for i in range(len(units) + LA):
                if i < len(units):
                    stage1(i)
                if i - LA >= 0:
                    stage2(i - LA)
            for i in range(32):
                os_, osb, osk = ostr.next()
                for par in range(2):
                    pf, pfb, _ = PSe.next()
                    S.op("pe", lambda e, pf=pf, par=par, i=i: e.transpose(pf[:, 0:65], accT[par][0:65, i * 128:(i + 1) * 128], identf[0:65, 0:65]),
                         reads=[bacc[par], bconst], writes=[pfb])
                    rc, rcb = smallcol()
                    S.op("dve", lambda e, rc=rc, pf=pf: e.reciprocal(out=rc, in_=pf[:, 64:65]), reads=[pfb], writes=[rcb])
                    S.op("act", lambda e, os_=os_, pf=pf, rc=rc, par=par: e.activation(out=os_[:, par * 64:(par + 1) * 64], in_=pf[:, 0:64], func=AF.Copy, scale=rc),
                         reads=[pfb, rcb], writes=[osb])
                S.op("pool", lambda e, os_=os_, job=job, hp=hp, i=i: e.dma_start(out=o_s[job][i * 128:(i + 1) * 128, hp * 128:(hp + 1) * 128], in_=os_),
                     reads=[osb], dsem="os%d" % osk)
    S.barrier()

    A = Arena()
    woutb = A.take(8 * D * 2, BF, [128, 8, D])
    wdb = A.take(22 * D * 2, BF, [128, 22, D])
    wgr = A.ring(3, BF, [128, 8, 128])
    wur = A.ring(3, BF, [128, 8, 128])
    x1s = [A.take(4 * D * 4, F32, [128, 4, D]) for _ in range(2)]
    h2Ts = [A.take(8 * 512 * 2, BF, [128, 8, 512]) for _ in range(2)]
    aT = A.take(22 * 512 * 2, BF, [128, 22, 512])
    otr = A.ring(2, BF, [128, D])
    oTr = A.ring(2, BF, [128, 8, 128])
    xr2 = A.ring(2, F32, [128, D])
    h2r = A.ring(2, BF, [128, D])
    sgr = A.ring(2, F32, [128, 512])
    yr = A.ring(2, F32, [128, D])
    junk2 = A.take(D * 2, BF, [128, D])
    bwo = Buf(); bwd = Buf(); bx1s = [[Buf() for _ in range(4)] for _ in range(2)]; bh2Ts = [Buf(), Buf()]; baT = [Buf() for _ in range(22)]
    for kc in range(8):
        S.op("sp", lambda e, kc=kc: e.dma_start(out=woutb[:, kc, :], in_=wout_b[kc * 128:(kc + 1) * 128, :]), writes=[bwo], dsem="cw0")
    for c in range(22):
        S.op("sp", lambda e, c=c: e.dma_start(out=wdb[:, c, :], in_=wd_b[c * 128:(c + 1) * 128, :]), writes=[bwd], dsem="cw1")
    junk = junk2

    def c_prologue(job, blk, cset):
        x1, h2T, bx1, bh2T = x1s[cset], h2Ts[cset], bx1s[cset], bh2Ts[cset]
        xoff = 0 if job == 0 else SEG
        for i in range(4):
            row0 = blk * 512 + i * 128
            ot, otb, otk = otr.next()
            S.op("sp", lambda e: e.dma_start(out=ot, in_=o_s[job][row0:row0 + 128, :]), writes=[otb], dsem="co%d" % otk)
            xt, xb, xk = xr2.next()
            S.op("sp", lambda e: e.dma_start(out=xt, in_=xin[job][xoff + row0:xoff + row0 + 128, :]), writes=[xb], dsem="cx%d" % xk)
            pt, ptb, _ = PS.next()
            ptv = psbf(pt)
            for kc in range(8):
                S.op("pe", lambda e, kc=kc: e.transpose(ptv[:, kc * 128:(kc + 1) * 128], ot[:, kc * 128:(kc + 1) * 128], identb[:]),
                     reads=[otb, bconst], writes=[ptb])
            oT, oTb, _ = oTr.next()
            S.op("act", lambda e: e.activation(out=oT, in_=ptv.rearrange("p (a b) -> p a b", b=128), func=AF.Copy),
                 reads=[ptb], writes=[oTb])
            yield
            for half in range(2):
                py, pyb, _ = PS.next()
                mm_group(py[:, :], [(oT[:, kc, :], woutb[:, kc, half * 512:(half + 1) * 512]) for kc in range(8)], pyb, [oTb, bwo])
                S.op("dve", lambda e, py=py, half=half: e.tensor_tensor(out=x1[:, i, half * 512:(half + 1) * 512], in0=py[:, :],
                                                                       in1=xt[:, half * 512:(half + 1) * 512], op=ALU.add),
                     reads=[pyb, xb], writes=[bx1[i]])
            rs, rsb = rstd_of(x1[:, i, :], bx1[i], D)
            h2, h2b, _ = h2r.next()
            S.op("pool", lambda e: e.tensor_scalar(out=h2, in0=x1[:, i, :], scalar1=rs, scalar2=1.0, op0=ALU.mult, op1=ALU.mult),
                 reads=[bx1[i], rsb], writes=[h2b])
            yield
            pt2, pt2b, _ = PS.next()
            ptv2 = psbf(pt2)
            for kc in range(8):
                S.op("pe", lambda e, kc=kc: e.transpose(ptv2[:, kc * 128:(kc + 1) * 128], h2[:, kc * 128:(kc + 1) * 128], identb[:]),
                     reads=[h2b, bconst], writes=[pt2b])
            S.op("act", lambda e: e.activation(out=h2T[:, :, i * 128:(i + 1) * 128], in_=ptv2.rearrange("p (a b) -> p a b", b=128), func=AF.Copy),
                 reads=[pt2b], writes=[bh2T])
            yield

    def c_main(job, blk, cset):
        x1, h2T, bx1, bh2T = x1s[cset], h2Ts[cset], bx1s[cset], bh2Ts[cset]
        for c in range(22):
            wgc, wgb_, wgk = wgr.next()
            S.op("sp", lambda e: e.dma_start(out=wgc, in_=wg_b[:, c * 128:(c + 1) * 128].rearrange("(kc p) n -> p kc n", p=128)),
                 writes=[wgb_], dsem="cg%d" % wgk)
            wuc, wub_, wuk = wur.next()
            S.op("sp", lambda e: e.dma_start(out=wuc, in_=wu_b[:, c * 128:(c + 1) * 128].rearrange("(kc p) n -> p kc n", p=128)),
                 writes=[wub_], dsem="cu%d" % wuk)
            pg, pgb, _ = PS.next()
            mm_group(pg[:, :], [(wgc[:, kc, :], h2T[:, kc, :]) for kc in range(8)], pgb, [wgb_, bh2T])
            pu, pub, _ = PS.next()
            mm_group(pu[:, :], [(wuc[:, kc, :], h2T[:, kc, :]) for kc in range(8)], pub, [wub_, bh2T])
            sg, sgb, _ = sgr.next()
            S.op("act", lambda e: e.activation(out=sg, in_=pg[:, :], func=AF.Silu), reads=[pgb], writes=[sgb])
            S.op("dve", lambda e: e.tensor_tensor(out=aT[:, c, :], in0=pu[:, :], in1=sg, op=ALU.mult),
                 reads=[pub, sgb], writes=[baT[c]])
            yield
        for i in range(4):
            row0 = blk * 512 + i * 128
            yt, ytb, yk = yr.next()
            for half in range(2):
                pd, pdb, _ = PS.next()
                mm_group(pd[:, :], [(aT[:, c, i * 128:(i + 1) * 128], wdb[:, c, half * 512:(half + 1) * 512]) for c in range(22)],
                         pdb, baT + [bwd])
                S.op("dve", lambda e, pd=pd, half=half: e.tensor_tensor(out=yt[:, half * 512:(half + 1) * 512], in0=pd[:, :],
                                                                       in1=x1[:, i, half * 512:(half + 1) * 512], op=ALU.add),
                     reads=[pdb, bx1[i]], writes=[ytb])
                yield
            rs, rsb = rstd_of(yt, ytb, D)
            S.op("dve", lambda e: e.scalar_tensor_tensor(out=yt, in0=yt, scalar=rs, in1=gfin[:], op0=ALU.mult, op1=ALU.mult),
                 reads=[ytb, rsb, bconst], writes=[ytb])
            S.op("pool", lambda e: e.dma_start(out=yout[job][row0:row0 + 128, :], in_=yt), reads=[ytb], dsem="y%d" % yk)
            yield

    cblocks = [(job, blk) for job in range(2) for blk in range(8)]
    run_streams([c_prologue(cblocks[0][0], cblocks[0][1], 0)], 1)
    for bi, (job, blk) in enumerate(cblocks):
        streams = [c_main(job, blk, bi % 2)]
        if bi + 1 < len(cblocks):
            streams.append(c_prologue(cblocks[bi + 1][0], cblocks[bi + 1][1], (bi + 1) % 2))
        run_streams(streams, 2)
    S.emit(st)
    st.close()
    return nc, None


def _consts():
    identb = np.eye(128, dtype=np.float32).astype(ml_dtypes.bfloat16)
    identf = np.eye(128, dtype=np.float32)
    pm = np.zeros((128, 128), np.float32)
    for par in range(2):
        b = par * 64
        for i in range(8):
            pm[b + i + 8, b + i] = -1.0
            pm[b + i, b + 8 + i] = 1.0
    p = np.arange(128)[:, None]
    j = np.arange(256)[None, :]
    mask = np.where((j - p >= 0) & (j - p <= 128), 0.0, -30000.0).astype(np.float32)
    maskm = ((j - p >= 0) & (j - p <= 128)).astype(np.float32)
    s = np.arange(128)[:, None]
    t = np.arange(128)[None, :]
    c = -1.0 / 16.0
    M = np.stack([(s <= t) * c, (s >= t) * c, (s > t) * c, (s < t) * c]).astype(np.float32)
    G = np.stack([np.tile((s <= t).astype(np.float32), (1, 4)), np.tile((s > t).astype(np.float32), (1, 4))])
    return dict(c_identb=identb, c_identf=identf, c_pm=pm.astype(ml_dtypes.bfloat16), c_mask=mask.astype(ml_dtypes.bfloat16), c_maskm=maskm.astype(ml_dtypes.bfloat16),
                c_M=M, c_G=G.astype(ml_dtypes.bfloat16))


def _rot_table(pos0):
    inv_freq = np.power(np.float32(500000.0), -np.arange(0, 16, 2, dtype=np.float32) / np.float32(16))
    pos = (np.arange(EXT, dtype=np.float32) + np.float32(pos0 - HALO)).astype(np.float32)
    ang = pos[None, :] * inv_freq[:, None].astype(np.float32)
    cos = np.cos(ang).astype(np.float32)
    sin = np.sin(ang).astype(np.float32)
    C = np.ones((128, EXT), np.float32)
    Sn = np.zeros((128, EXT), np.float32)
    for par in range(2):
        b = par * 64
        C[b:b + 8] = cos
        C[b + 8:b + 16] = cos
        Sn[b:b + 8] = sin
        Sn[b + 8:b + 16] = sin
    return np.stack([C, Sn]).astype(np.float32)


_CACHE = {}


def make_in_maps(x_prompt, x_sample, g_mix, w_in, w_gate_f, b_gate_f, w_gate_b, b_gate_b, g_gla, w_out,
                 g_ffn, w_ffn_gate, w_ffn_up, w_ffn_down, g_final):
    f = lambda a: np.ascontiguousarray(np.asarray(a, dtype=np.float32))
    x_prompt = f(x_prompt); x_sample = f(x_sample)
    consts = _consts()
    shared = dict(g_mix=f(g_mix)[0], g_ffn=f(g_ffn)[0], g_gla=f(g_gla)[0], g_final=f(g_final),
                  w_in=f(w_in)[0], w_out=f(w_out)[0], w_ffn_gate=f(w_ffn_gate)[0], w_ffn_up=f(w_ffn_up)[0],
                  w_ffn_down=f(w_ffn_down)[0], w_gate_f=f(w_gate_f)[0], w_gate_b=f(w_gate_b)[0],
                  b_gate_f=f(b_gate_f)[0], b_gate_b=f(b_gate_b)[0])
    shared.update(consts)
    rotP = _rot_table(0)
    validP = np.zeros((EXT,), np.float32); validP[HALO:HALO + SEG] = 1.0
    in_maps = []
    for c in range(8):
        sq, half = c // 2, c % 2
        xs = np.zeros((3 * SEG, D), np.float32)
        xs[SEG:2 * SEG] = x_sample[sq, half * SEG:(half + 1) * SEG]
        vs = np.zeros((EXT,), np.float32); vs[HALO:HALO + SEG] = 1.0
        if half == 0:
            xs[2 * SEG:] = x_sample[sq, SEG:]
            vs[HALO + SEG:] = 1.0
        else:
            xs[:SEG] = x_sample[sq, :SEG]
            vs[:HALO] = 1.0
        m = dict(shared)
        m.update(xP=x_prompt[c], xS=xs, rotP=rotP, rotS=_rot_table(half * SEG),
                 validP=np.ascontiguousarray(validP.reshape(EXT // 128, 128).T),
                 validS=np.ascontiguousarray(vs.reshape(EXT // 128, 128).T))
        in_maps.append(m)
    return in_maps


def kernel(**inputs):
    if "nc" not in _CACHE:
        _CACHE["nc"] = build_program()
    nc, _st = _CACHE["nc"]
    in_maps = make_in_maps(**inputs)
    res = run_bass_kernel_spmd(nc, in_maps, core_ids=list(range(8)))
    yp = np.stack([np.asarray(res.results[c]["yP"], dtype=np.float32) for c in range(8)])
    ys = np.stack([np.concatenate([np.asarray(res.results[2 * s]["yS"], dtype=np.float32),
                                   np.asarray(res.results[2 * s + 1]["yS"], dtype=np.float32)], axis=0) for s in range(4)])
    return yp, ys
```

```python
import contextlib
import numpy as np
import ml_dtypes
import concourse.bass as bass
import concourse.mybir as mybir
from concourse.bass_utils import run_bass_kernel_spmd

F32 = mybir.dt.float32
BF = mybir.dt.bfloat16
AF = mybir.ActivationFunctionType
ALU = mybir.AluOpType
AX = mybir.AxisListType

D = 1024
DFF = 2816
PW = 3104
SEG = 4096
HALO = 1024
EXT = SEG + 2 * HALO
EPS = 1e-6
DEBUG = False
MASK_MODES = ("dve", "pool", "dve")
SAME_SYNC = ("act", "dve", "pool")
ENG_NAMES = ("pe", "act", "dve", "pool", "sp")
C_QD, C_KD, C_VD, C_QA, C_KA, C_VA, C_RA, C_G = 0, 512, 1024, 1536, 1792, 2048, 2560, 3072


class _Rec:
    def __getattr__(self, name):
        def f(*a, **k):
            self.call = (name, a, k)
            return None
        return f


class Op:
    __slots__ = ("eng", "fn", "deps", "sig", "cnt", "dsem", "dval")

    def __init__(self, eng, fn, deps, dsem=None, dval=0):
        self.eng = eng
        rec = _Rec()
        fn(rec)
        self.fn = rec.call
        self.deps = deps
        self.sig = False
        self.cnt = 0
        self.dsem = dsem
        self.dval = dval


class Buf:
    __slots__ = ("w", "r")

    def __init__(self):
        self.w = None
        self.r = []


class Sched:
    def __init__(self, nc):
        self.nc = nc
        self.ops = {e: [] for e in ENG_NAMES}
        self.dma_cnt = {}
        self.last_dma = {}
        self.bar = []

    def op(self, eng, fn, reads=(), writes=(), dsem=None):
        deps = list(self.bar)
        for b in reads:
            if b.w is not None:
                deps.append(b.w)
        for b in writes:
            if b.w is not None:
                deps.append(b.w)
            deps.extend(b.r)
        dval = 0
        if dsem is not None:
            self.dma_cnt[dsem] = self.dma_cnt.get(dsem, 0) + 16
            dval = self.dma_cnt[dsem]
        o = Op(eng, fn, deps, dsem, dval)
        if dsem is not None:
            self.last_dma[dsem] = o
        self.ops[eng].append(o)
        for b in reads:
            b.r.append(o)
        for b in writes:
            b.w = o
            b.r = []
        return o

    def barrier(self):
        bar = []
        for e in ENG_NAMES:
            for o in reversed(self.ops[e]):
                if o.dsem is None:
                    bar.append(o)
                    break
        bar.extend(self.last_dma.values())
        self.bar = bar

    def emit(self, stack):
        nc = self.nc
        for e in ENG_NAMES:
            for o in self.ops[e]:
                for d in o.deps:
                    if d.dsem is None and (d.eng != o.eng or o.eng in SAME_SYNC):
                        d.sig = True
        esem = {}
        for e in ENG_NAMES:
            c = 0
            for o in self.ops[e]:
                if o.sig:
                    c += 1
                    o.cnt = c
            esem[e] = stack.enter_context(nc.semaphore("s_" + e))
        dsems = {k: stack.enter_context(nc.semaphore("d_%s" % k)) for k in self.dma_cnt}
        block = stack.enter_context(nc.Block())
        engobj = {"pe": block.tensor, "act": block.scalar, "dve": block.vector,
                  "pool": block.gpsimd, "sp": block.sync}
        finals = list(self.last_dma.values())

        def run(ename):
            def body(eng):
                waited = {}
                for o in self.ops[ename]:
                    need = {}
                    for d in o.deps:
                        if d.dsem is not None:
                            key, sem, val = ("d", d.dsem), dsems[d.dsem], d.dval
                        else:
                            if d.eng == ename and ename not in SAME_SYNC:
                                continue
                            key, sem, val = ("e", d.eng), esem[d.eng], d.cnt
                        if waited.get(key, 0) >= val:
                            continue
                        if key not in need or need[key][1] < val:
                            need[key] = (sem, val)
                    for key, (sem, val) in need.items():
                        eng.wait_ge(sem, val)
                        waited[key] = val
                    nm_, a_, k_ = o.fn
                    inst = getattr(eng, nm_)(*a_, **k_)
                    if o.dsem is not None:
                        inst.then_inc(dsems[o.dsem], 16)
                    elif o.sig:
                        inst.then_inc(esem[ename], 1)
                if ename == "sp":
                    for d in finals:
                        if waited.get(("d", d.dsem), 0) < d.dval:
                            eng.wait_ge(dsems[d.dsem], d.dval)
            return body

        for e in ENG_NAMES:
            engobj[e](run(e))


class Ring:
    def __init__(self, items):
        self.items = [(ap, Buf()) for ap in items]
        self.i = 0
        self.base = id(self) % 100000

    def next(self):
        k = self.i % len(self.items)
        self.i += 1
        ap, b = self.items[k]
        return ap, b, k


def build_program():
    nc = bass.Bass("TRN2", target_bir_lowering=False)
    st = contextlib.ExitStack()
    S = Sched(nc)

    def din(name, shape, dt=F32):
        return nc.dram_tensor(name, shape, dt, kind="ExternalInput").ap()

    def dscr(name, shape, dt=BF):
        if DEBUG:
            return nc.dram_tensor(name, shape, dt, kind="ExternalOutput").ap()
        return nc.dram_tensor(name, shape, dt).ap()

    xin = [din("xP", [SEG, D]), din("xS", [3 * SEG, D])]
    rot = [din("rotP", [2, 128, EXT]), din("rotS", [2, 128, EXT])]
    valid = [din("validP", [128, EXT // 128]), din("validS", [128, EXT // 128])]
    g_mix = din("g_mix", [D]); g_ffn = din("g_ffn", [D]); g_gla = din("g_gla", [128]); g_final = din("g_final", [D])
    w_in = din("w_in", [D, PW]); w_out = din("w_out", [D, D])
    w_g = din("w_ffn_gate", [D, DFF]); w_u = din("w_ffn_up", [D, DFF]); w_d = din("w_ffn_down", [DFF, D])
    wgate = [din("w_gate_f", [16, 256]), din("w_gate_b", [16, 256])]
    bgate = [din("b_gate_f", [256]), din("b_gate_b", [256])]
    c_identb = din("c_identb", [128, 128], BF); c_identf = din("c_identf", [128, 128])
    c_pm = din("c_pm", [128, 128], BF); c_mask = din("c_mask", [128, 256], BF); c_maskm = din("c_maskm", [128, 256], BF)
    c_M = din("c_M", [4, 128, 128]); c_G = din("c_G", [2, 128, 512], BF)
    yout = [nc.dram_tensor("yP", [SEG, D], F32, kind="ExternalOutput").ap(),
            nc.dram_tensor("yS", [SEG, D], F32, kind="ExternalOutput").ap()]
    win_b = dscr("win_b", [D, PW]); wout_b = dscr("wout_b", [D, D])
    wg_b = dscr("wg_b", [D, DFF]); wu_b = dscr("wu_b", [D, DFF]); wd_b = dscr("wd_b", [DFF, D])
    qT_s = [dscr("qT_s%d" % j, [4, 128, SEG]) for j in range(2)]
    kT_s = [dscr("kT_s%d" % j, [4, 128, EXT]) for j in range(2)]
    v_s = [dscr("v_s%d" % j, [EXT, 520]) for j in range(2)]
    o_s = [dscr("o_s%d" % j, [SEG, D]) for j in range(2)]
    B_scr = {}

    def dbuf(key):
        if key not in B_scr:
            B_scr[key] = Buf()
        return B_scr[key]

    def sb(name, shape, dt):
        return st.enter_context(nc.sbuf_tensor(name, shape, dt))

    identb = sb("identb", [128, 128], BF); identf = sb("identf", [128, 128], F32)
    pm = sb("pm", [128, 128], BF); mask01 = sb("mask01", [128, 256], BF); mask01m = sb("mask01m", [128, 256], BF)
    Mm = sb("Mm", [128, 4, 128], F32)
    Gm = sb("Gm", [128, 2, 512], BF)
    negcol = sb("negcol", [128, 2], F32); epsc = sb("epsc", [128, 1], F32); onec = sb("onec", [128, 1], F32)
    gt = sb("gt", [128, 18], F32)
    gfin = sb("gfin", [128, D], F32)
    wgs = sb("wgs", [32, 2, 256], F32); wga = sb("wga", [32, 2, 256], BF)
    gaug = sb("gaug", [32, 2, 512], BF)
    Sst = sb("Sst", [128, 2, 2, 128], F32)
    Fsnaps = [sb("Fsnap%d" % i, [128, 4, 2, 128], BF) for i in range(2)]
    small = sb("small", [128, 64], F32)
    ARENA = 47700
    arena = sb("arena", [128, ARENA], F32)
    psb = [st.enter_context(nc.psum_tensor("pb%d" % i, [128, 512], F32)) for i in range(8)]
    bconst = Buf()
    cnt = [0]

    def dkey(tag):
        return tag

    cl = [(identb[:], c_identb[:, :]), (identf[:], c_identf[:, :]), (pm[:], c_pm[:, :]), (mask01[:], c_mask[:, :]), (mask01m[:], c_maskm[:, :]),
          (gfin[:], g_final.partition_broadcast(128))]
    for k in range(4):
        cl.append((Mm[:, k, :], c_M[k]))
    for k in range(2):
        cl.append((Gm[:, k, :], c_G[k]))
        cl.append((wgs[0:16, k, :], wgate[k][:, :]))
        cl.append((wgs[16:17, k, :], bgate[k].rearrange("(a n) -> a n", a=1)))
    cl.append((gt[:, 0:8], g_mix.rearrange("(c p) -> p c", p=128)))
    cl.append((gt[:, 8:16], g_ffn.rearrange("(c p) -> p c", p=128)))
    cl.append((gt[:, 16:17], g_gla.rearrange("(p a) -> p a", a=1)))
    S.op("pool", lambda e: e.memset(wgs[:], 0.0), writes=[bconst])
    for i, (dst, src) in enumerate(cl):
        S.op("sp", lambda e, dst=dst, src=src: e.dma_start(out=dst, in_=src, allow_slow_non_contiguous=True),
             writes=[bconst], dsem="c%d" % (i % 4))
    S.op("dve", lambda e: e.memset(negcol[:], -1.0 / 16.0), writes=[bconst])
    S.op("dve", lambda e: e.memset(epsc[:], EPS), writes=[bconst])
    S.op("dve", lambda e: e.memset(onec[:], 1.0), writes=[bconst])
    S.op("dve", lambda e: e.memset(gt[:, 17:18], 1.0), writes=[bconst])
    S.op("dve", lambda e: e.memset(gaug[:], 1.0), writes=[bconst])
    S.barrier()
    S.op("dve", lambda e: e.tensor_copy(wga[0:17], wgs[0:17]), reads=[bconst], writes=[bconst])

    class Arena:
        def __init__(self):
            self.off = 0

        def take(self, nbytes_per_part, dt, shape):
            words = (nbytes_per_part + 3) // 4
            words = (words + 7) // 8 * 8
            a = arena[:, self.off:self.off + words]
            self.off += words
            assert self.off <= ARENA, ("arena overflow", self.off)
            if dt == BF:
                a = a.bitcast(BF)
                n = int(np.prod(shape[1:]))
                a = a[:, 0:n]
            else:
                n = int(np.prod(shape[1:]))
                a = a[:, 0:n]
            if len(shape) == 3:
                a = a.rearrange("p (a b) -> p a b", b=shape[2])
            elif len(shape) == 4:
                a = a.rearrange("p (a b c) -> p a b c", b=shape[2], c=shape[3])
            return a

        def ring(self, n, dt, shape):
            esz = 2 if dt == BF else 4
            nb = int(np.prod(shape[1:])) * esz
            return Ring([self.take(nb, dt, shape) for _ in range(n)])

    PS = Ring([p[:] for p in psb])

    def psbf(ap):
        return ap.bitcast(BF)

    smallr = Ring([small[:, k * 4:k * 4 + 4] for k in range(16)])

    def smallcol(n=1):
        ap, b, _ = smallr.next()
        return ap[:, 0:n], b

    wi = [0]

    def wconv_gen(wlist, stg, stgb, tag, colchunk):
        pieces = []
        for (W, Wb, K, N, g0) in wlist:
            for kc in range(K // 128):
                gcol = 17
                if g0 is not None:
                    gcol = g0 + kc
                elif W is w_out and kc >= 4:
                    gcol = 16
                for c0 in range(0, N, colchunk):
                    pieces.append((W, Wb, kc, gcol, c0, min(colchunk, N - c0)))
        st_ = {}
        n_ = len(pieces)
        for t in range(n_ + 2):
            if t < n_:
                W, Wb, kc, gcol, c0, n = pieces[t]
                a, ab, ak = stg.next()
                S.op("sp", lambda e: e.dma_start(out=a[:, 0:n], in_=W[kc * 128:(kc + 1) * 128, c0:c0 + n]),
                     writes=[ab], dsem="%sl%d" % (tag, ak))
                st_[t] = [a, ab]
            if 0 <= t - 1 < n_:
                W, Wb, kc, gcol, c0, n = pieces[t - 1]
                a, ab = st_[t - 1]
                o, ob, ok = stgb.next()
                if wi[0] % 2 == 0:
                    S.op("act", lambda e: e.activation(out=o[:, 0:n], in_=a[:, 0:n], func=AF.Copy, scale=gt[:, gcol:gcol + 1]),
                         reads=[ab, bconst], writes=[ob])
                else:
                    S.op("dve", lambda e: e.tensor_scalar(out=o[:, 0:n], in0=a[:, 0:n], scalar1=gt[:, gcol:gcol + 1], scalar2=None,
                                                          op0=ALU.mult),
                         reads=[ab, bconst], writes=[ob])
                wi[0] += 1
                st_[t - 1] = [o, ob, ok]
            if 0 <= t - 2 < n_:
                W, Wb, kc, gcol, c0, n = pieces[t - 2]
                o, ob, ok = st_.pop(t - 2)
                S.op("pool", lambda e: e.dma_start(out=Wb[kc * 128:(kc + 1) * 128, c0:c0 + n], in_=o[:, 0:n]),
                     reads=[ob], dsem="%ss%d" % (tag, ok))
            yield

    A = Arena()
    stg = A.ring(4, F32, [128, PW])
    stgb = A.ring(4, BF, [128, PW])
    for _ in wconv_gen([(w_in, win_b, D, PW, 0)], stg, stgb, "w", PW):
        pass
    S.barrier()

    A = Arena()
    winb = A.take(8 * PW * 2, BF, [128, 8, PW])
    Ssnap = A.take(32 * 2 * 128 * 2, BF, [128, 32, 2, 128])
    Xr = A.ring(2, F32, [128, D])
    Hr = A.ring(2, BF, [128, D])
    hTr = A.ring(3, BF, [128, 8, 512])
    junk = A.take(D * 2, BF, [128, D])
    sprs = [[[A.take(1024, F32, [128, 256]) for _ in range(2)] for _ in range(4)] for _ in range(2)]
    spbs = [[[Buf() for _ in range(2)] for _ in range(4)] for _ in range(2)]
    etmp = A.ring(1, F32, [128, 256])
    eDr = A.ring(1, F32, [128, 256])
    kdr = A.ring(8, BF, [128, 256])
    vbr = A.ring(8, BF, [128, 512])
    decr = A.ring(8, F32, [128, 4])
    Er = A.ring(4, F32, [128, 512])
    qin = [[A.take(1024, BF, [128, 512]) for _ in range(2)] for _ in range(2)]
    kin = [[A.take(1024, BF, [128, 512]) for _ in range(2)] for _ in range(2)]
    qinb = [[Buf() for _ in range(2)] for _ in range(2)]
    kinb = [[Buf() for _ in range(2)] for _ in range(2)]
    attr = A.ring(6, BF, [128, 512])
    sqr = A.ring(2, F32, [128, 512])
    onr = A.ring(2, F32, [128, 512])
    err = A.ring(2, F32, [128, 512])
    ogr = A.ring(2, BF, [128, 512])
    rotr = A.ring(1, F32, [128, 2, 512])
    qrawr = A.ring(2, BF, [128, 512])
    t1r = A.ring(1, F32, [128, 512])
    t2r = A.ring(1, F32, [128, 512])
    qfr = A.ring(2, BF, [128, 512])
    vaugr = A.ring(2, BF, [128, 8, 65])
    validt = A.take(48 * 4, F32, [128, 48])
    zt = Ssnap.rearrange("p a b c -> p (a b c)")[:, 0:1024]
    bwin = Buf(); bvalid = Buf()
    bS = [Buf(), Buf()]; bSq = [[Buf() for _ in range(4)] for _ in range(2)]; bSsnap = Buf(); bzt = bSsnap; bFsnaps = [Buf(), Buf()]; bgaug = [Buf(), Buf()]

    for kc in range(8):
        S.op("sp", lambda e, kc=kc: e.dma_start(out=winb[:, kc, :], in_=win_b[kc * 128:(kc + 1) * 128, :]),
             writes=[bwin], dsem="wi%d" % (kc % 2))
    S.op("pool", lambda e: e.memset(zt[:], 0.0), writes=[bzt])

    def mm_group(out, pairs, ps_b, reads):
        n = len(pairs)
        for i, (l, r) in enumerate(pairs):
            S.op("pe", lambda e, l=l, r=r, i=i: e.matmul(out, l, r, start=(i == 0), stop=(i == n - 1)),
                 reads=reads, writes=[ps_b])

    def rstd_of(src, src_b, width):
        ss, ssb = smallcol()
        S.op("act", lambda e: e.activation(out=junk[:, 0:width], in_=src, func=AF.Square, accum_out=ss),
             reads=[src_b], writes=[ssb, bjunk])
        ln, lnb = smallcol()
        S.op("act", lambda e: e.activation(out=ln, in_=ss, func=AF.Ln, scale=1.0 / width, bias=epsc[:, 0:1]),
             reads=[ssb, bconst], writes=[lnb])
        rs, rsb = smallcol()
        S.op("act", lambda e: e.activation(out=rs, in_=ln, func=AF.Exp, scale=-0.5), reads=[lnb], writes=[rsb])
        return rs, rsb

    bjunk = Buf()

    dbg = {}
    if DEBUG:
        dbg["x"] = nc.dram_tensor("dbgx", [32, 128, D], F32, kind="ExternalOutput").ap()
        dbg["h"] = nc.dram_tensor("dbgh", [32, 128, D], BF, kind="ExternalOutput").ap()
        dbg["r"] = nc.dram_tensor("dbgr", [32, 128, 4], F32, kind="ExternalOutput").ap()
        dbg["hT"] = nc.dram_tensor("dbghT", [8, 128, 8, 512], BF, kind="ExternalOutput").ap()

    def prep_tile(xsrc_rows, hT, hTb, col, dbi=None):
        xt, xb, xk = Xr.next()
        S.op("sp", lambda e: e.dma_start(out=xt, in_=xsrc_rows), writes=[xb], dsem="x%d" % xk)
        rs, rsb = rstd_of(xt, xb, D)
        hb, hbb, _ = Hr.next()
        S.op("dve", lambda e: e.tensor_scalar(out=hb, in0=xt, scalar1=rs, scalar2=None, op0=ALU.mult),
             reads=[xb, rsb], writes=[hbb])
        pt, ptb, _ = PS.next()
        ptv = psbf(pt)
        for kc in range(8):
            S.op("pe", lambda e, kc=kc: e.transpose(ptv[:, kc * 128:(kc + 1) * 128], hb[:, kc * 128:(kc + 1) * 128], identb[:]),
                 reads=[hbb, bconst], writes=[ptb])
        S.op("act", lambda e: e.activation(out=hT[:, :, col:col + 128], in_=ptv.rearrange("p (a b) -> p a b", b=128),
                                           func=AF.Copy),
             reads=[ptb], writes=[hTb])
        if DEBUG and dbi is not None:
            S.op("sp", lambda e: e.dma_start(out=dbg["x"][dbi], in_=xt), reads=[xb], dsem="dbg0")
            S.op("sp", lambda e: e.dma_start(out=dbg["h"][dbi], in_=hb), reads=[hbb], dsem="dbg1")
            S.op("sp", lambda e: e.dma_start(out=dbg["r"][dbi][:, 0:1], in_=rs, allow_slow_non_contiguous=True), reads=[rsb], dsem="dbg2")
            if col == 384:
                S.op("sp", lambda e: e.dma_start(out=dbg["hT"][dbi // 4], in_=hT), reads=[hTb], dsem="dbg3")

    def interleave(gens, width):
        pending = list(gens)
        active = []
        while pending or active:
            while pending and len(active) < width:
                active.append(pending.pop(0))
            for g in list(active):
                try:
                    next(g)
                except StopIteration:
                    active.remove(g)
                yield

    def run_streams(gens, width):
        for _ in interleave(gens, width):
            pass

    def gla_tokmajor(hT, hTb, ti, col, dirs, scan, snap_dst, res, gset, split_state=False):
        spr, spb = sprs[gset], spbs[gset]
        bFsnap = bFsnaps[gset]
        lh = [hT[:, kc, col:col + 128] for kc in range(8)]
        pv, pvb, _ = PS.next()
        mm_group(pv[:, :], [(lh[kc], winb[:, kc, C_VA:C_VA + 512]) for kc in range(8)], pvb, [hTb, bwin])
        vb, vbb, _ = vbr.next()
        S.op("act", lambda e: e.activation(out=vb, in_=pv[:, :], func=AF.Copy), reads=[pvb], writes=[vbb])
        res["vb"] = (vb, vbb)
        for d in dirs:
            pz, pzb, _ = PS.next()
            S.op("pe", lambda e, d=d: e.matmul(pz[:, 0:256], gaug[0:17, d, col:col + 128], wga[0:17, d, :], start=True, stop=True),
                 reads=[bgaug[d], bconst], writes=[pzb])
            et_, etb, _ = etmp.next()
            S.op("act", lambda e, et_=et_: e.activation(out=et_, in_=pz[:, 0:256], func=AF.Exp, scale=-1.0),
                 reads=[pzb], writes=[etb])
            sp_, spb_ = spr[ti][d], spb[ti][d]
            S.op("act", lambda e, et_=et_, sp_=sp_: e.activation(out=sp_, in_=et_, func=AF.Ln, bias=onec[:, 0:1]),
                 reads=[etb, bconst], writes=[spb_])
        yield
        d = scan
        sp_, spb_ = spr[ti][d], spb[ti][d]
        pd, pdb, _ = PS.next()
        S.op("pe", lambda e: e.matmul(pd[:, 0:256], Mm[:, 2 + d, :], sp_, start=True, stop=True),
             reads=[spb_, bconst], writes=[pdb])
        for j in range(2):
            S.op("pe", lambda e, j=j: e.matmul(pd[:, 256 + 2 * j:258 + 2 * j], sp_[:, j * 128:(j + 1) * 128], negcol[:, 0:2],
                                               start=True, stop=True),
                 reads=[spb_, bconst], writes=[pdb])
        eD, eDb, _ = eDr.next()
        S.op("act", lambda e: e.activation(out=eD, in_=pd[:, 0:256], func=AF.Exp), reads=[pdb], writes=[eDb])
        dec, decb, _ = decr.next()
        S.op("act", lambda e: e.activation(out=dec, in_=pd[:, 256:260], func=AF.Exp), reads=[pdb], writes=[decb])
        pk, pkb, _ = PS.next()
        mm_group(pk[:, 0:256], [(lh[kc], winb[:, kc, C_KA:C_KA + 256]) for kc in range(8)], pkb, [hTb, bwin])
        kd, kdb, _ = kdr.next()
        S.op("dve", lambda e: e.tensor_tensor(out=kd, in0=pk[:, 0:256], in1=eD, op=ALU.mult), reads=[pkb, eDb], writes=[kdb])
        yield

        def state_gen():
            if snap_dst is not None:
                S.op("act", lambda e: e.activation(out=snap_dst, in_=Sst[:, d, :, :], func=AF.Copy), reads=bSq[d],
                     writes=[bSsnap if d == 1 else bFsnap])
            pkv, pkvb, _ = PS.next()
            for j in range(2):
                S.op("pe", lambda e, j=j: e.matmul(pkv[:, j * 256:(j + 1) * 256], kd[:, j * 128:(j + 1) * 128],
                                                   vb[:, j * 256:(j + 1) * 256], start=True, stop=True),
                     reads=[kdb, vbb], writes=[pkvb])
            for j in range(2):
                for par in range(2):
                    rows = slice(par * 64, (par + 1) * 64)
                    sbuf_ = bSq[d][2 * j + par]
                    S.op("dve", lambda e, j=j, par=par, rows=rows: e.scalar_tensor_tensor(
                        out=Sst[rows, d, j, :], in0=Sst[rows, d, j, :], scalar=dec[rows, 2 * j:2 * j + 1],
                        in1=pkv[rows, j * 256 + par * 128:j * 256 + (par + 1) * 128], op0=ALU.mult, op1=ALU.add),
                        reads=[pkvb, decb, sbuf_], writes=[sbuf_])
            yield
        if split_state:
            res["state"] = state_gen()
        else:
            for _ in state_gen():
                yield

    def prep_group(job, tiles, full):
        hT, hTb, _ = hTr.next()

        def gen():
            xsrc = xin[job]
            for ti in range(4):
                et = tiles[ti]
                prep_tile(xsrc[et * 128:(et + 1) * 128, :], hT, hTb, ti * 128, dbi=(et if (full and job == 0) else None))
                yield
        return (hT, hTb), gen()

    def proc_group(job, tiles, scan, full, dilkv, gset, hTctx):
        Fsnap = Fsnaps[gset]
        base = tiles[0]
        hT, hTb = hTctx
        order = list(range(4)) if scan == 0 else [3, 2, 1, 0]
        dirs = [0, 1] if full else [scan]

        def gproj_gen():
            for d in dirs:
                pg, pgb, _ = PS.next()
                mm_group(pg[0:16, :], [(winb[:, kc, C_G + 16 * d:C_G + 16 * d + 16], hT[:, kc, :]) for kc in range(8)], pgb, [hTb, bwin])
                S.op("act", lambda e, d=d, pg=pg: e.activation(out=gaug[0:16, d, :], in_=pg[0:16, :], func=AF.Copy),
                     reads=[pgb], writes=[bgaug[d]])
            yield
        tm = {}
        gens = []
        for ti in order:
            et = tiles[ti]
            is_main = (job == 0) or (32 <= et < 64)
            mt = et - (32 if job == 1 else 0)
            if not is_main:
                snap_dst = None
            elif scan == 1:
                snap_dst = Ssnap[:, mt, :, :]
            else:
                snap_dst = Fsnap[:, ti, :, :]
            tm[ti] = {}
            gens.append(gla_tokmajor(hT, hTb, ti, ti * 128, dirs, scan, snap_dst, tm[ti], gset, split_state=(not full)))

        def dil_gen():
            det0 = (base + 8) if job == 0 else (base - 24)
            ecol = det0 * 128
            rt, rtb, rk = rotr.next()
            S.op("sp", lambda e: e.dma_start(out=rt, in_=rot[job][:, :, ecol:ecol + 512].rearrange("a p n -> p a n")),
                 writes=[rtb], dsem="rot%d" % rk)
            which = [("k", C_KD)] + ([("q", C_QD)] if full else [])

            def finish(nm, hp, qr, qrb):
                pp, ppb, _ = PS.next()
                S.op("pe", lambda e: e.matmul(pp[:, :], pm[:], qr, start=True, stop=True),
                     reads=[qrb, bconst], writes=[ppb])
                t1, t1b, _ = t1r.next()
                S.op("dve", lambda e: e.tensor_tensor(out=t1, in0=pp[:, :], in1=rt[:, 1, :], op=ALU.mult),
                     reads=[ppb, rtb], writes=[t1b])
                t2, t2b, _ = t2r.next()
                S.op("pool", lambda e: e.tensor_tensor(out=t2, in0=qr, in1=rt[:, 0, :], op=ALU.mult),
                     reads=[qrb, rtb], writes=[t2b])
                qf, qfb, qfk = qfr.next()
                S.op("dve", lambda e: e.tensor_tensor(out=qf, in0=t1, in1=t2, op=ALU.add),
                     reads=[t1b, t2b], writes=[qfb])
                if nm == "q":
                    mcol = (base - (32 if job == 1 else 0)) * 128
                    dst = qT_s[job][hp][:, mcol:mcol + 512]
                else:
                    dst = kT_s[job][hp][:, ecol:ecol + 512]
                S.op("pool", lambda e: e.dma_start(out=dst, in_=qf), reads=[qfb], dsem="qf%d" % qfk)

            prev = None
            for hp in range(4):
                for nm, c0 in which:
                    pq, pqb, _ = PS.next()
                    mm_group(pq[:, :], [(winb[:, kc, c0 + hp * 128:c0 + (hp + 1) * 128], hT[:, kc, :]) for kc in range(8)],
                             pqb, [hTb, bwin])
                    qr, qrb, _ = qrawr.next()
                    S.op("act", lambda e, qr=qr, pq=pq: e.activation(out=qr, in_=pq[:, :], func=AF.Copy), reads=[pqb], writes=[qrb])
                    if prev is not None:
                        finish(*prev)
                    prev = (nm, hp, qr, qrb)
                    yield
            for ti in range(4):
                col = ti * 128
                pv, pvb, _ = PS.next()
                mm_group(pv[:, :], [(hT[:, kc, col:col + 128], winb[:, kc, C_VD:C_VD + 512]) for kc in range(8)], pvb, [hTb, bwin])
                if prev is not None:
                    finish(*prev)
                    prev = None
                va, vab, vak = vaugr.next()
                S.op("act", lambda e, va=va, pv=pv: e.activation(out=va[:, :, 0:64], in_=pv[:, :].rearrange("p (a b) -> p a b", b=64),
                                                                   func=AF.Copy), reads=[pvb], writes=[vab])
                det = det0 + ti
                S.op("dve", lambda e, va=va, det=det: e.tensor_copy(va[:, :, 64:65],
                                                                    validt[:, det:det + 1].broadcast_to([128, 8]).rearrange("p (a b) -> p a b", b=1)),
                     reads=[bvalid], writes=[vab])
                S.op("pool", lambda e, va=va, det=det: e.dma_start(out=v_s[job][det * 128:(det + 1) * 128, :],
                                                                   in_=va.rearrange("p a b -> p (a b)")),
                     reads=[vab], dsem="va%d" % vak)
                yield

        if dilkv:
            gens.insert(2, dil_gen())

        def xgen():
            for _ in gproj_gen():
                yield
            for _ in interleave(gens, 3):
                yield
        def xbgen():
            for ti in order:
                for _ in tm[ti]["state"]:
                    yield
        return xgen(), ((job, tiles, hT, hTb, tm, gset) if full else None), (None if full else xbgen())

    def y_gen(job, tiles, hT, hTb, tm, gset):
        spr, spb = sprs[gset], spbs[gset]
        Fsnap, bFsnap = Fsnaps[gset], bFsnaps[gset]
        for j in range(2):
            EE = []
            for d in range(2):
                pb_, pbb, _ = PS.next()
                for ti in range(4):
                    S.op("pe", lambda e, ti=ti: e.matmul(pb_[:, ti * 128:(ti + 1) * 128], spr[ti][d][:, j * 128:(j + 1) * 128],
                                                         Mm[:, d, :], start=True, stop=True),
                         reads=[spb[ti][d], bconst], writes=[pbb])
                Ep, Epb, _ = Er.next()
                S.op("act", lambda e: e.activation(out=Ep, in_=pb_[:, :], func=AF.Exp), reads=[pbb], writes=[Epb])
                En, Enb, _ = Er.next()
                S.op("act", lambda e: e.activation(out=En, in_=pb_[:, :], func=AF.Exp, scale=-1.0), reads=[pbb], writes=[Enb])
                EE.append((Ep, Epb, En, Enb))
            yield
            pqa, pqab, _ = PS.next()
            mm_group(pqa[:, :], [(winb[:, kc, C_QA + j * 128:C_QA + (j + 1) * 128], hT[:, kc, :]) for kc in range(8)], pqab, [hTb, bwin])
            pka, pkab, _ = PS.next()
            mm_group(pka[:, :], [(winb[:, kc, C_KA + j * 128:C_KA + (j + 1) * 128], hT[:, kc, :]) for kc in range(8)], pkab, [hTb, bwin])
            for d in range(2):
                Ep, Epb, En, Enb = EE[d]
                S.op("dve", lambda e: e.scalar_tensor_tensor(out=qin[d][j], in0=pqa[:, :], scalar=0.125, in1=Ep,
                                                             op0=ALU.mult, op1=ALU.mult),
                     reads=[pqab, Epb], writes=[qinb[d][j]])
                S.op("dve", lambda e: e.tensor_tensor(out=kin[d][j], in0=pka[:, :], in1=En, op=ALU.mult),
                     reads=[pkab, Enb], writes=[kinb[d][j]])
            yield

        def y_tile(ti):
            et = tiles[ti]
            mt = et - (32 if job == 1 else 0)
            col = ti * 128
            vb, vbb = tm[ti]["vb"]
            atts = []
            for d in range(2):
                at, atb, _ = attr.next()
                atv = at.rearrange("p (j q t) -> p j q t", q=2, t=128)
                for par in range(2):
                    pa, pab, _ = PS.next()
                    rows = slice(par * 64, (par + 1) * 64)
                    for j in range(2):
                        S.op("pe", lambda e, j=j: e.matmul(pa[:, j * 128:(j + 1) * 128], kin[d][j][rows, col:col + 128],
                                                           qin[d][j][rows, col:col + 128], start=True, stop=True),
                             reads=[kinb[d][j], qinb[d][j]], writes=[pab])
                    S.op("dve", lambda e: e.tensor_tensor(out=atv[:, :, par, :], in0=pa[:, 0:256].rearrange("p (j t) -> p j t", t=128),
                                                          in1=Gm[:, d, 0:256].rearrange("p (j t) -> p j t", t=128), op=ALU.mult),
                         reads=[pab, bconst], writes=[atb])
                atts.append((at, atb))
            pr, prb, _ = PS.next()
            mm_group(pr[:, :], [(hT[:, kc, col:col + 128], winb[:, kc, C_RA:C_RA + 512]) for kc in range(8)], prb, [hTb, bwin])
            er, erb, _ = err.next()
            S.op("act", lambda e: e.activation(out=er, in_=pr[:, :], func=AF.Exp, scale=-1.0), reads=[prb], writes=[erb])
            S.op("pool", lambda e: e.tensor_scalar(out=er, in0=er, scalar1=1.0, scalar2=1.0, op0=ALU.add, op1=ALU.mult),
                 reads=[erb], writes=[erb])
            S.op("dve", lambda e: e.reciprocal(out=er, in_=er), reads=[erb], writes=[erb])
            S.op("dve", lambda e: e.tensor_tensor(out=er, in0=pr[:, :], in1=er, op=ALU.mult), reads=[prb, erb], writes=[erb])
            yield
            po, pob, _ = PS.next()
            for h in range(4):
                j, par = h // 2, h % 2
                rows = slice(par * 64, (par + 1) * 64)
                hc = slice(h * 128, (h + 1) * 128)
                pairs = [(atts[0][0][:, hc], vb[:, hc]), (atts[1][0][:, hc], vb[:, hc]),
                         (qin[0][j][rows, col:col + 128], Fsnap[rows, ti, j, :]),
                         (qin[1][j][rows, col:col + 128], Ssnap[rows, mt, j, :])]
                mm_group(po[:, hc], pairs, pob, [atts[0][1], atts[1][1], vbb, qinb[0][j], qinb[1][j], bFsnap, bSsnap])
            sq, sqb, _ = sqr.next()
            S.op("act", lambda e: e.activation(out=sq, in_=po[:, :], func=AF.Square), reads=[pob], writes=[sqb])
            ss, ssb = smallcol(4)
            S.op("dve", lambda e: e.tensor_reduce(out=ss, in_=sq.rearrange("p (a b) -> p a b", b=128), axis=AX.X, op=ALU.add),
                 reads=[sqb], writes=[ssb])
            ln, lnb = smallcol(4)
            S.op("act", lambda e: e.activation(out=ln, in_=ss, func=AF.Ln, scale=1.0 / 128, bias=epsc[:, 0:1]),
                 reads=[ssb, bconst], writes=[lnb])
            rs, rsb = smallcol(4)
            S.op("act", lambda e: e.activation(out=rs, in_=ln, func=AF.Exp, scale=-0.5), reads=[lnb], writes=[rsb])
            on, onb, _ = onr.next()
            for h in range(4):
                S.op("act" if h % 2 else "dve", (lambda e, h=h: e.activation(out=on[:, h * 128:(h + 1) * 128], in_=po[:, h * 128:(h + 1) * 128],
                                                                            func=AF.Copy, scale=rs[:, h:h + 1])) if h % 2 else
                     (lambda e, h=h: e.tensor_scalar(out=on[:, h * 128:(h + 1) * 128], in0=po[:, h * 128:(h + 1) * 128],
                                                     scalar1=rs[:, h:h + 1], scalar2=None, op0=ALU.mult)),
                     reads=[pob, rsb], writes=[onb])
            og, ogb, ogk = ogr.next()
            S.op("pool", lambda e: e.tensor_tensor(out=og, in0=on, in1=er, op=ALU.mult), reads=[onb, erb], writes=[ogb])
            S.op("pool", lambda e: e.dma_start(out=o_s[job][mt * 128:(mt + 1) * 128, 512:1024], in_=og),
                 reads=[ogb], dsem="og%d" % ogk)
            yield

        for _ in interleave([y_tile(ti) for ti in range(4)], 2):
            yield

    def zero_state(d):
        S.op("dve", lambda e: e.memset(Sst[:, d, :, :], 0.0), reads=bSq[d], writes=bSq[d])

    for job in range(2):
        S.op("sp", lambda e, job=job: e.dma_start(out=validt[:], in_=valid[job][:, :]), writes=[bvalid], dsem="vld")
        if job == 0:
            for hp in range(4):
                for c0 in (0, HALO + SEG):
                    S.op("pool", lambda e, hp=hp, c0=c0: e.dma_start(out=kT_s[0][hp][:, c0:c0 + HALO], in_=zt[:, 0:HALO]),
                         reads=[bzt], dsem="z0")
            for r0 in (0, HALO + SEG):
                for k in range(8):
                    S.op("pool", lambda e, r0=r0, k=k: e.dma_start(out=v_s[0][r0 + k * 128:r0 + (k + 1) * 128, :], in_=zt[:, 0:520]),
                         reads=[bzt], dsem="z1")
            moff = 0
        else:
            moff = 32
        descs = []
        if job == 1:
            for g in range(23, 15, -1):
                descs.append(([4 * g + i for i in range(4)], 1, False, g < 18))
        for g in range(7, -1, -1):
            descs.append(([moff + 4 * g + i for i in range(4)], 1, False, False))
        nback = len(descs)
        if job == 1:
            for g in range(0, 8):
                descs.append(([4 * g + i for i in range(4)], 0, False, g >= 6))
        for g in range(8):
            descs.append(([moff + 4 * g + i for i in range(4)], 0, True, True))
        zero_state(1)
        hctx, pg0 = prep_group(job, descs[0][0], descs[0][2])
        run_streams([pg0], 1)
        prev_y = None
        prev_xb = None
        nfull = 0
        for i, (tiles, scan, full, dilkv) in enumerate(descs):
            if i == nback:
                zero_state(0)
            gset = nfull % 2
            xg, yctx, xb = proc_group(job, tiles, scan, full, dilkv, gset, hctx)
            streams = [xg]
            if prev_xb is not None:
                streams.insert(0, prev_xb)
            prev_xb = xb
            if prev_y is not None:
                streams.append(y_gen(*prev_y))
            if i + 1 < len(descs):
                hctx, pgn = prep_group(job, descs[i + 1][0], descs[i + 1][2])
                streams.append(pgn)
            run_streams(streams, 4)
            prev_y = yctx
            if full:
                nfull += 1
        run_streams([y_gen(*prev_y)], 1)
    S.barrier()

    A = Arena()
    qT = A.take(SEG * 2, BF, [128, SEG]); q4 = A.take(SEG * 2, BF, [128, 4, SEG // 4]); q16 = A.take(SEG * 2, BF, [128, 16, SEG // 16])
    kT = A.take(EXT * 2, BF, [128, EXT]); k4 = A.take(EXT * 2, BF, [128, 4, EXT // 4]); k16 = A.take(EXT * 2, BF, [128, 16, EXT // 16])
    NT1, NT4, NT16 = 33, 9, 3
    vg1 = A.take(NT1 * 260, BF, [128, NT1, 130])
    vg4 = A.take(4 * NT4 * 260, BF, [128, 4 * NT4, 130])
    vg16 = A.take(16 * NT16 * 260, BF, [128, 16 * NT16, 130])
    accTs = [[A.take(SEG * 4, F32, [128, SEG]) for _ in range(2)] for _ in range(2)]
    Pr = A.ring(4, BF, [128, 256])
    Pmr = A.ring(10, BF, [128, 256])
    ostr = A.ring(2, BF, [128, 128])
    stg2 = A.ring(3, F32, [128, 1408])
    stgb2 = A.ring(3, BF, [128, 1408])
    wgen2 = wconv_gen([(w_out, wout_b, D, D, None), (w_g, wg_b, D, DFF, 8), (w_u, wu_b, D, DFF, 8), (w_d, wd_b, DFF, D, None)],
                      stg2, stgb2, "v", 1408)
    bq = Buf(); bk = Buf(); bqp = [None, Buf(), Buf()]; bkp = [None, Buf(), Buf()]; bvgs = [Buf(), Buf(), Buf()]; baccs = [[Buf(), Buf()], [Buf(), Buf()]]
    bqp[0] = bq; bkp[0] = bk
    PSs = Ring([psb[i][:, 0:256] for i in range(4)])
    PSu = [Ring([psb[4][:], psb[5][:]]), Ring([psb[6][:], psb[7][:]])]

    for job in range(2):
        for hp in range(4):
            accT = accTs[hp % 2]; bacc = baccs[hp % 2]
            S.op("sp", lambda e, job=job, hp=hp: e.dma_start(out=qT[:], in_=qT_s[job][hp]), writes=[bq], dsem="bq")
            S.op("sp", lambda e, job=job, hp=hp: e.dma_start(out=kT[:], in_=kT_s[job][hp]), writes=[bk], dsem="bk")
            S.op("dve", lambda e: e.tensor_copy(q4[:], qT.rearrange("p (l r) -> p r l", r=4)), reads=[bq], writes=[bqp[1]])
            S.op("dve", lambda e: e.tensor_copy(k4[:], kT.rearrange("p (l r) -> p r l", r=4)), reads=[bk], writes=[bkp[1]])
            S.op("dve", lambda e: e.tensor_copy(q16[:], qT.rearrange("p (l r) -> p r l", r=16)), reads=[bq], writes=[bqp[2]])
            S.op("act", lambda e: e.activation(out=k16[:], in_=kT.rearrange("p (l r) -> p r l", r=16), func=AF.Copy), reads=[bk], writes=[bkp[2]])
            vcols = slice(hp * 130, (hp + 1) * 130)
            vsrc = v_s[job]
            for t0 in (0, 11, 22):
                S.op("sp", lambda e: e.dma_start(out=vg1[:, t0:t0 + 11, :], in_=vsrc[960 + t0 * 128:960 + (t0 + 11) * 128, vcols].rearrange("(t p) c -> p t c", p=128)),
                     writes=[bvgs[0]], dsem="bv0")
            for r in range(4):
                S.op("sp", lambda e, vsrc=vsrc, vcols=vcols, r=r: e.dma_start(
                    out=vg4[:, r * NT4:(r + 1) * NT4, :],
                    in_=vsrc[r + 768:r + 768 + 4 * (NT4 * 128 - 1) + 1:4, vcols].rearrange("(t p) c -> p t c", p=128)),
                    writes=[bvgs[1]], dsem="bv1")
            for r in range(16):
                S.op("sp", lambda e, vsrc=vsrc, vcols=vcols, r=r: e.dma_start(
                    out=vg16[:, r * NT16:(r + 1) * NT16, :],
                    in_=vsrc[r:r + 16 * (NT16 * 128 - 1) + 1:16, vcols].rearrange("(t p) c -> p t c", p=128)),
                    writes=[bvgs[2]], dsem="bv2")
            pats = [(1, 32, qT.rearrange("p (r l) -> p r l", r=1), kT.rearrange("p (r l) -> p r l", r=1), vg1, 1024),
                    (4, 8, q4, k4, vg4, 256), (16, 2, q16, k16, vg16, 64)]
            units = []
            for pi, pat in enumerate(pats):
                for r in range(pat[0]):
                    curs = [{}, {}]
                    for kt in range(pat[1] + 1):
                        for par in range(2):
                            units.append((pi, pat, r, par, kt, curs[par]))
            staged = {}

            def stage1(ui):
                pi, (dil, nqt, qp, kp, vg, Q0), r, par, kt, cur = units[ui]
                rows = slice(par * 64, (par + 1) * 64)
                qa = max(0, kt - 1); qb_ = min(nqt - 1, kt)
                n = (qb_ - qa + 1) * 128
                jlo = 128 * (qa - (kt - 1))
                k0 = Q0 + 128 * kt - 64
                ps_, psb_, _ = PSs.next()
                mode = MASK_MODES[ui % len(MASK_MODES)]
                P2, P2b, _ = Pmr.next()
                if mode == "pe":
                    S.op("pe", lambda e: e.matmul(ps_[:, 0:n], identb[:], mask01[:, jlo:jlo + n], start=True, stop=False),
                         reads=[bconst], writes=[psb_])
                    S.op("pe", lambda e: e.matmul(ps_[:, 0:n], kp[rows, r, k0:k0 + 128], qp[rows, r, 128 * qa:128 * qa + n], start=False, stop=True),
                         reads=[bqp[pi], bkp[pi]], writes=[psb_])
                    S.op("act", lambda e: e.activation(out=P2[:, 0:n], in_=ps_[:, 0:n], func=AF.Exp, scale=0.125),
                         reads=[psb_], writes=[P2b])
                else:
                    S.op("pe", lambda e: e.matmul(ps_[:, 0:n], kp[rows, r, k0:k0 + 128], qp[rows, r, 128 * qa:128 * qa + n], start=True, stop=True),
                         reads=[bqp[pi], bkp[pi]], writes=[psb_])
                    P, Pb, _ = Pr.next()
                    S.op("act", lambda e: e.activation(out=P[:, 0:n], in_=ps_[:, 0:n], func=AF.Exp, scale=0.125),
                         reads=[psb_], writes=[Pb])
                    S.op(mode, lambda e: e.tensor_tensor(out=P2[:, 0:n], in0=P[:, 0:n], in1=mask01m[:, jlo:jlo + n], op=ALU.mult),
                         reads=[Pb, bconst], writes=[P2b])
                staged[ui] = (P2, P2b, qa)

            def stage2(ui):
                pi, (dil, nqt, qp, kp, vg, Q0), r, par, kt, cur = units[ui]
                ntile = nqt + 1
                acc = accT[par]
                P2, P2b, qa = staged.pop(ui)
                vgt = vg[:, r * ntile + kt, par * 65:(par + 1) * 65]
                qts = [qt for qt in (kt - 1, kt) if 0 <= qt <= nqt - 1]
                if len(qts) == 2 and kt % 4 != 0:
                    groups = [qts]
                else:
                    groups = [[qt] for qt in qts]
                for grp in groups:
                    qt0 = grp[0]
                    bank = qt0 // 4
                    first = (bank not in cur)
                    if first:
                        cur[bank] = PSu[par].next()
                    pu, pub, _ = cur[bank]
                    last = (kt == grp[-1] + 1) and (grp[-1] % 4 == 3 or grp[-1] == nqt - 1)
                    c0 = (qt0 % 4) * 128
                    nn = 128 * len(grp)
                    p0 = (qt0 - qa) * 128
                    S.op("pe", lambda e: e.matmul(pu[0:65, c0:c0 + nn], vgt, P2[:, p0:p0 + nn], start=first, stop=last),
                         reads=[P2b, bvgs[pi]], writes=[pub])
                    if last:
                        qt = grp[-1]
                        q0 = (qt // 4) * 4
                        ncol = (qt - q0 + 1) * 128
                        s0 = r + dil * 128 * q0
                        dst = acc[0:65, s0:s0 + dil * (ncol - 1) + 1:dil]
                        if pi == 0:
                            S.op("dve", lambda e: e.tensor_copy(dst, pu[0:65, 0:ncol]),
                                 reads=[pub], writes=[bacc[par]])
                        else:
                            S.op("dve", lambda e: e.tensor_tensor(out=dst, in0=pu[0:65, 0:ncol], in1=dst, op=ALU.add),
                                 reads=[pub, bacc[par]], writes=[bacc[par]])

            LA = 3
            for i in range(len(units) + LA):
                if i < len(units):
                    stage1(i)
                if i - LA >= 0:
                    stage2(i - LA)
                if i % 40 == 20:
                    next(wgen2, None)
            for i in range(32):
                os_, osb, osk = ostr.next()
                for par in range(2):
                    pf, pfb, _ = PSs.next()
                    S.op("pe", lambda e, pf=pf, par=par, i=i: e.transpose(pf[:, 0:65], accT[par][0:65, i * 128:(i + 1) * 128], identf[0:65, 0:65]),
                         reads=[bacc[par], bconst], writes=[pfb])
                    rc, rcb = smallcol()
                    S.op("dve", lambda e, rc=rc, pf=pf: e.reciprocal(out=rc, in_=pf[:, 64:65]), reads=[pfb], writes=[rcb])
                    S.op("act", lambda e, os_=os_, pf=pf, rc=rc, par=par: e.activation(out=os_[:, par * 64:(par + 1) * 64], in_=pf[:, 0:64], func=AF.Copy, scale=rc),
                         reads=[pfb, rcb], writes=[osb])
                S.op("pool", lambda e, os_=os_, job=job, hp=hp, i=i: e.dma_start(out=o_s[job][i * 128:(i + 1) * 128, hp * 128:(hp + 1) * 128], in_=os_),
                     reads=[osb], dsem="os%d" % osk)
    for _ in wgen2:
        pass
    S.barrier()

    A = Arena()
    woutb = A.take(8 * D * 2, BF, [128, 8, D])
    wdb = A.take(22 * D * 2, BF, [128, 22, D])
    wgr = A.ring(3, BF, [128, 8, 128])
    wur = A.ring(3, BF, [128, 8, 128])
    x1s = [A.take(4 * D * 4, F32, [128, 4, D]) for _ in range(2)]
    h2Ts = [A.take(8 * 512 * 2, BF, [128, 8, 512]) for _ in range(2)]
    aT = A.take(22 * 512 * 2, BF, [128, 22, 512])
    otr = A.ring(2, BF, [128, D])
    oTr = A.ring(2, BF, [128, 8, 128])
    xr2 = A.ring(2, F32, [128, D])
    h2r = A.ring(2, BF, [128, D])
    sgr = A.ring(2, F32, [128, 512])
    yr = A.ring(2, F32, [128, D])
    junk2 = A.take(D * 2, BF, [128, D])
    bwo = Buf(); bwd = Buf(); bx1s = [[Buf() for _ in range(4)] for _ in range(2)]; bh2Ts = [Buf(), Buf()]; baT = [Buf() for _ in range(22)]
    for kc in range(8):
        S.op("sp", lambda e, kc=kc: e.dma_start(out=woutb[:, kc, :], in_=wout_b[kc * 128:(kc + 1) * 128, :]), writes=[bwo], dsem="cw0")
    for c in range(22):
        S.op("sp", lambda e, c=c: e.dma_start(out=wdb[:, c, :], in_=wd_b[c * 128:(c + 1) * 128, :]), writes=[bwd], dsem="cw1")
    junk = junk2

    def c_prologue(job, blk, cset):
        x1, h2T, bx1, bh2T = x1s[cset], h2Ts[cset], bx1s[cset], bh2Ts[cset]
        xoff = 0 if job == 0 else SEG
        for i in range(4):
            row0 = blk * 512 + i * 128
            ot, otb, otk = otr.next()
            S.op("sp", lambda e: e.dma_start(out=ot, in_=o_s[job][row0:row0 + 128, :]), writes=[otb], dsem="co%d" % otk)
            xt, xb, xk = xr2.next()
            S.op("sp", lambda e: e.dma_start(out=xt, in_=xin[job][xoff + row0:xoff + row0 + 128, :]), writes=[xb], dsem="cx%d" % xk)
            pt, ptb, _ = PS.next()
            ptv = psbf(pt)
            for kc in range(8):
                S.op("pe", lambda e, kc=kc: e.transpose(ptv[:, kc * 128:(kc + 1) * 128], ot[:, kc * 128:(kc + 1) * 128], identb[:]),
                     reads=[otb, bconst], writes=[ptb])
            oT, oTb, _ = oTr.next()
            S.op("act", lambda e: e.activation(out=oT, in_=ptv.rearrange("p (a b) -> p a b", b=128), func=AF.Copy),
                 reads=[ptb], writes=[oTb])
            yield
            for half in range(2):
                py, pyb, _ = PS.next()
                mm_group(py[:, :], [(oT[:, kc, :], woutb[:, kc, half * 512:(half + 1) * 512]) for kc in range(8)], pyb, [oTb, bwo])
                S.op("dve", lambda e, py=py, half=half: e.tensor_tensor(out=x1[:, i, half * 512:(half + 1) * 512], in0=py[:, :],
                                                                       in1=xt[:, half * 512:(half + 1) * 512], op=ALU.add),
                     reads=[pyb, xb], writes=[bx1[i]])
            rs, rsb = rstd_of(x1[:, i, :], bx1[i], D)
            h2, h2b, _ = h2r.next()
            S.op("pool", lambda e: e.tensor_scalar(out=h2, in0=x1[:, i, :], scalar1=rs, scalar2=1.0, op0=ALU.mult, op1=ALU.mult),
                 reads=[bx1[i], rsb], writes=[h2b])
            yield
            pt2, pt2b, _ = PS.next()
            ptv2 = psbf(pt2)
            for kc in range(8):
                S.op("pe", lambda e, kc=kc: e.transpose(ptv2[:, kc * 128:(kc + 1) * 128], h2[:, kc * 128:(kc + 1) * 128], identb[:]),
                     reads=[h2b, bconst], writes=[pt2b])
            S.op("act", lambda e: e.activation(out=h2T[:, :, i * 128:(i + 1) * 128], in_=ptv2.rearrange("p (a b) -> p a b", b=128), func=AF.Copy),
                 reads=[pt2b], writes=[bh2T])
            yield

    def c_main(job, blk, cset):
        x1, h2T, bx1, bh2T = x1s[cset], h2Ts[cset], bx1s[cset], bh2Ts[cset]
        for c in range(22):
            wgc, wgb_, wgk = wgr.next()
            S.op("sp", lambda e: e.dma_start(out=wgc, in_=wg_b[:, c * 128:(c + 1) * 128].rearrange("(kc p) n -> p kc n", p=128)),
                 writes=[wgb_], dsem="cg%d" % wgk)
            wuc, wub_, wuk = wur.next()
            S.op("sp", lambda e: e.dma_start(out=wuc, in_=wu_b[:, c * 128:(c + 1) * 128].rearrange("(kc p) n -> p kc n", p=128)),
                 writes=[wub_], dsem="cu%d" % wuk)
            pg, pgb, _ = PS.next()
            mm_group(pg[:, :], [(wgc[:, kc, :], h2T[:, kc, :]) for kc in range(8)], pgb, [wgb_, bh2T])
            pu, pub, _ = PS.next()
            mm_group(pu[:, :], [(wuc[:, kc, :], h2T[:, kc, :]) for kc in range(8)], pub, [wub_, bh2T])
            sg, sgb, _ = sgr.next()
            S.op("act", lambda e: e.activation(out=sg, in_=pg[:, :], func=AF.Silu), reads=[pgb], writes=[sgb])
            S.op("dve", lambda e: e.tensor_tensor(out=aT[:, c, :], in0=pu[:, :], in1=sg, op=ALU.mult),
                 reads=[pub, sgb], writes=[baT[c]])
            yield
        for i in range(4):
            row0 = blk * 512 + i * 128
            yt, ytb, yk = yr.next()
            for half in range(2):
                pd, pdb, _ = PS.next()
                mm_group(pd[:, :], [(aT[:, c, i * 128:(i + 1) * 128], wdb[:, c, half * 512:(half + 1) * 512]) for c in range(22)],
                         pdb, baT + [bwd])
                S.op("dve", lambda e, pd=pd, half=half: e.tensor_tensor(out=yt[:, half * 512:(half + 1) * 512], in0=pd[:, :],
                                                                       in1=x1[:, i, half * 512:(half + 1) * 512], op=ALU.add),
                     reads=[pdb, bx1[i]], writes=[ytb])
                yield
            rs, rsb = rstd_of(yt, ytb, D)
            S.op("dve", lambda e: e.scalar_tensor_tensor(out=yt, in0=yt, scalar=rs, in1=gfin[:], op0=ALU.mult, op1=ALU.mult),
                 reads=[ytb, rsb, bconst], writes=[ytb])
            S.op("pool", lambda e: e.dma_start(out=yout[job][row0:row0 + 128, :], in_=yt), reads=[ytb], dsem="y%d" % yk)
            yield

    cblocks = [(job, blk) for job in range(2) for blk in range(8)]
    run_streams([c_prologue(cblocks[0][0], cblocks[0][1], 0)], 1)
    for bi, (job, blk) in enumerate(cblocks):
        streams = [c_main(job, blk, bi % 2)]
        if bi + 1 < len(cblocks):
            streams.append(c_prologue(cblocks[bi + 1][0], cblocks[bi + 1][1], (bi + 1) % 2))
        run_streams(streams, 2)
    S.emit(st)
    st.close()
    return nc, None


def _consts():
    identb = np.eye(128, dtype=np.float32).astype(ml_dtypes.bfloat16)
    identf = np.eye(128, dtype=np.float32)
    pm = np.zeros((128, 128), np.float32)
    for par in range(2):
        b = par * 64
        for i in range(8):
            pm[b + i + 8, b + i] = -1.0
            pm[b + i, b + 8 + i] = 1.0
    p = np.arange(128)[:, None]
    j = np.arange(256)[None, :]
    mask = np.where((j - p >= 0) & (j - p <= 128), 0.0, -30000.0).astype(np.float32)
    maskm = ((j - p >= 0) & (j - p <= 128)).astype(np.float32)
    s = np.arange(128)[:, None]
    t = np.arange(128)[None, :]
    c = -1.0 / 16.0
    M = np.stack([(s <= t) * c, (s >= t) * c, (s > t) * c, (s < t) * c]).astype(np.float32)
    G = np.stack([np.tile((s <= t).astype(np.float32), (1, 4)), np.tile((s > t).astype(np.float32), (1, 4))])
    return dict(c_identb=identb, c_identf=identf, c_pm=pm.astype(ml_dtypes.bfloat16), c_mask=mask.astype(ml_dtypes.bfloat16), c_maskm=maskm.astype(ml_dtypes.bfloat16),
                c_M=M, c_G=G.astype(ml_dtypes.bfloat16))


def _rot_table(pos0):
    inv_freq = np.power(np.float32(500000.0), -np.arange(0, 16, 2, dtype=np.float32) / np.float32(16))
    pos = (np.arange(EXT, dtype=np.float32) + np.float32(pos0 - HALO)).astype(np.float32)
    ang = pos[None, :] * inv_freq[:, None].astype(np.float32)
    cos = np.cos(ang).astype(np.float32)
    sin = np.sin(ang).astype(np.float32)
    C = np.ones((128, EXT), np.float32)
    Sn = np.zeros((128, EXT), np.float32)
    for par in range(2):
        b = par * 64
        C[b:b + 8] = cos
        C[b + 8:b + 16] = cos
        Sn[b:b + 8] = sin
        Sn[b + 8:b + 16] = sin
    return np.stack([C, Sn]).astype(np.float32)


_CACHE = {}


def make_in_maps(x_prompt, x_sample, g_mix, w_in, w_gate_f, b_gate_f, w_gate_b, b_gate_b, g_gla, w_out,
                 g_ffn, w_ffn_gate, w_ffn_up, w_ffn_down, g_final):
    f = lambda a: np.ascontiguousarray(np.asarray(a, dtype=np.float32))
    x_prompt = f(x_prompt); x_sample = f(x_sample)
    consts = _consts()
    shared = dict(g_mix=f(g_mix)[0], g_ffn=f(g_ffn)[0], g_gla=f(g_gla)[0], g_final=f(g_final),
                  w_in=f(w_in)[0], w_out=f(w_out)[0], w_ffn_gate=f(w_ffn_gate)[0], w_ffn_up=f(w_ffn_up)[0],
                  w_ffn_down=f(w_ffn_down)[0], w_gate_f=f(w_gate_f)[0], w_gate_b=f(w_gate_b)[0],
                  b_gate_f=f(b_gate_f)[0], b_gate_b=f(b_gate_b)[0])
    shared.update(consts)
    rotP = _rot_table(0)
    validP = np.zeros((EXT,), np.float32); validP[HALO:HALO + SEG] = 1.0
    in_maps = []
    for c in range(8):
        sq, half = c // 2, c % 2
        xs = np.zeros((3 * SEG, D), np.float32)
        xs[SEG:2 * SEG] = x_sample[sq, half * SEG:(half + 1) * SEG]
        vs = np.zeros((EXT,), np.float32); vs[HALO:HALO + SEG] = 1.0
        if half == 0:
            xs[2 * SEG:] = x_sample[sq, SEG:]
            vs[HALO + SEG:] = 1.0
        else:
            xs[:SEG] = x_sample[sq, :SEG]
            vs[:HALO] = 1.0
        m = dict(shared)
        m.update(xP=x_prompt[c], xS=xs, rotP=rotP, rotS=_rot_table(half * SEG),
                 validP=np.ascontiguousarray(validP.reshape(EXT // 128, 128).T),
                 validS=np.ascontiguousarray(vs.reshape(EXT // 128, 128).T))
        in_maps.append(m)
    return in_maps


def kernel(**inputs):
    if "nc" not in _CACHE:
        _CACHE["nc"] = build_program()
    nc, _st = _CACHE["nc"]
    in_maps = make_in_maps(**inputs)
    res = run_bass_kernel_spmd(nc, in_maps, core_ids=list(range(8)))
    yp = np.stack([np.asarray(res.results[c]["yP"], dtype=np.float32) for c in range(8)])
    ys = np.stack([np.concatenate([np.asarray(res.results[2 * s]["yS"], dtype=np.float32),
                                   np.asarray(res.results[2 * s + 1]["yS"], dtype=np.float32)], axis=0) for s in range(4)])
    return yp, ys
```

```python
import contextlib
import numpy as np
import ml_dtypes
import concourse.bass as bass
import concourse.mybir as mybir
from concourse.bass_utils import run_bass_kernel_spmd

F32 = mybir.dt.float32
BF = mybir.dt.bfloat16
AF = mybir.ActivationFunctionType
ALU = mybir.AluOpType
AX = mybir.AxisListType

D = 1024
DFF = 2816
PW = 3104
SEG = 4096
HALO = 1024
EXT = SEG + 2 * HALO
EPS = 1e-6
DEBUG = False
MASK_MODES = ("dve", "pool", "dve")
SAME_SYNC = ("act", "dve", "pool")
ENG_NAMES = ("pe", "act", "dve", "pool", "sp")
C_QD, C_KD, C_VD, C_QA, C_KA, C_VA, C_RA, C_G = 0, 512, 1024, 1536, 1792, 2048, 2560, 3072


class _Rec:
    def __getattr__(self, name):
        def f(*a, **k):
            self.call = (name, a, k)
            return None
        return f


class Op:
    __slots__ = ("eng", "fn", "deps", "sig", "cnt", "dsem", "dval")

    def __init__(self, eng, fn, deps, dsem=None, dval=0):
        self.eng = eng
        rec = _Rec()
        fn(rec)
        self.fn = rec.call
        self.deps = deps
        self.sig = False
        self.cnt = 0
        self.dsem = dsem
        self.dval = dval


class Buf:
    __slots__ = ("w", "r")

    def __init__(self):
        self.w = None
        self.r = []


class Sched:
    def __init__(self, nc):
        self.nc = nc
        self.ops = {e: [] for e in ENG_NAMES}
        self.dma_cnt = {}
        self.last_dma = {}
        self.bar = []

    def op(self, eng, fn, reads=(), writes=(), dsem=None):
        deps = list(self.bar)
        for b in reads:
            if b.w is not None:
                deps.append(b.w)
        for b in writes:
            if b.w is not None:
                deps.append(b.w)
            deps.extend(b.r)
        dval = 0
        if dsem is not None:
            self.dma_cnt[dsem] = self.dma_cnt.get(dsem, 0) + 16
            dval = self.dma_cnt[dsem]
        o = Op(eng, fn, deps, dsem, dval)
        if dsem is not None:
            self.last_dma[dsem] = o
        self.ops[eng].append(o)
        for b in reads:
            b.r.append(o)
        for b in writes:
            b.w = o
            b.r = []
        return o

    def barrier(self):
        bar = []
        for e in ENG_NAMES:
            for o in reversed(self.ops[e]):
                if o.dsem is None:
                    bar.append(o)
                    break
        bar.extend(self.last_dma.values())
        self.bar = bar

    def emit(self, stack):
        nc = self.nc
        for e in ENG_NAMES:
            for o in self.ops[e]:
                for d in o.deps:
                    if d.dsem is None and (d.eng != o.eng or o.eng in SAME_SYNC):
                        d.sig = True
        esem = {}
        for e in ENG_NAMES:
            c = 0
            for o in self.ops[e]:
                if o.sig:
                    c += 1
                    o.cnt = c
            esem[e] = stack.enter_context(nc.semaphore("s_" + e))
        dsems = {k: stack.enter_context(nc.semaphore("d_%s" % k)) for k in self.dma_cnt}
        block = stack.enter_context(nc.Block())
        engobj = {"pe": block.tensor, "act": block.scalar, "dve": block.vector,
                  "pool": block.gpsimd, "sp": block.sync}
        finals = list(self.last_dma.values())

        def run(ename):
            def body(eng):
                waited = {}
                for o in self.ops[ename]:
                    need = {}
                    for d in o.deps:
                        if d.dsem is not None:
                            key, sem, val = ("d", d.dsem), dsems[d.dsem], d.dval
                        else:
                            if d.eng == ename and ename not in SAME_SYNC:
                                continue
                            key, sem, val = ("e", d.eng), esem[d.eng], d.cnt
                        if waited.get(key, 0) >= val:
                            continue
                        if key not in need or need[key][1] < val:
                            need[key] = (sem, val)
                    for key, (sem, val) in need.items():
                        eng.wait_ge(sem, val)
                        waited[key] = val
                    nm_, a_, k_ = o.fn
                    inst = getattr(eng, nm_)(*a_, **k_)
                    if o.dsem is not None:
                        inst.then_inc(dsems[o.dsem], 16)
                    elif o.sig:
                        inst.then_inc(esem[ename], 1)
                if ename == "sp":
                    for d in finals:
                        if waited.get(("d", d.dsem), 0) < d.dval:
                            eng.wait_ge(dsems[d.dsem], d.dval)
            return body

        for e in ENG_NAMES:
            engobj[e](run(e))


class Ring:
    def __init__(self, items):
        self.items = [(ap, Buf()) for ap in items]
        self.i = 0
        self.base = id(self) % 100000

    def next(self):
        k = self.i % len(self.items)
        self.i += 1
        ap, b = self.items[k]
        return ap, b, k


def build_program():
    nc = bass.Bass("TRN2", target_bir_lowering=False)
    st = contextlib.ExitStack()
    S = Sched(nc)

    def din(name, shape, dt=F32):
        return nc.dram_tensor(name, shape, dt, kind="ExternalInput").ap()

    def dscr(name, shape, dt=BF):
        if DEBUG:
            return nc.dram_tensor(name, shape, dt, kind="ExternalOutput").ap()
        return nc.dram_tensor(name, shape, dt).ap()

    xin = [din("xP", [SEG, D]), din("xS", [3 * SEG, D])]
    rot = [din("rotP", [2, 128, EXT]), din("rotS", [2, 128, EXT])]
    valid = [din("validP", [128, EXT // 128]), din("validS", [128, EXT // 128])]
    g_mix = din("g_mix", [D]); g_ffn = din("g_ffn", [D]); g_gla = din("g_gla", [128]); g_final = din("g_final", [D])
    w_in = din("w_in", [D, PW]); w_out = din("w_out", [D, D])
    w_g = din("w_ffn_gate", [D, DFF]); w_u = din("w_ffn_up", [D, DFF]); w_d = din("w_ffn_down", [DFF, D])
    wgate = [din("w_gate_f", [16, 256]), din("w_gate_b", [16, 256])]
    bgate = [din("b_gate_f", [256]), din("b_gate_b", [256])]
    c_identb = din("c_identb", [128, 128], BF); c_identf = din("c_identf", [128, 128])
    c_pm = din("c_pm", [128, 128], BF); c_mask = din("c_mask", [128, 256], BF); c_maskm = din("c_maskm", [128, 256], BF)
    c_M = din("c_M", [4, 128, 128]); c_G = din("c_G", [2, 128, 512], BF)
    yout = [nc.dram_tensor("yP", [SEG, D], F32, kind="ExternalOutput").ap(),
            nc.dram_tensor("yS", [SEG, D], F32, kind="ExternalOutput").ap()]
    win_b = dscr("win_b", [D, PW]); wout_b = dscr("wout_b", [D, D])
    wg_b = dscr("wg_b", [D, DFF]); wu_b = dscr("wu_b", [D, DFF]); wd_b = dscr("wd_b", [DFF, D])
    qT_s = [dscr("qT_s%d" % j, [4, 128, SEG]) for j in range(2)]
    kT_s = [dscr("kT_s%d" % j, [4, 128, EXT]) for j in range(2)]
    v_s = [dscr("v_s%d" % j, [EXT, 520]) for j in range(2)]
    o_s = [dscr("o_s%d" % j, [SEG, D]) for j in range(2)]
    B_scr = {}

    def dbuf(key):
        if key not in B_scr:
            B_scr[key] = Buf()
        return B_scr[key]

    def sb(name, shape, dt):
        return st.enter_context(nc.sbuf_tensor(name, shape, dt))

    identb = sb("identb", [128, 128], BF); identf = sb("identf", [128, 128], F32)
    pm = sb("pm", [128, 128], BF); mask01 = sb("mask01", [128, 256], BF); mask01m = sb("mask01m", [128, 256], BF)
    Mm = sb("Mm", [128, 4, 128], F32)
    Gm = sb("Gm", [128, 2, 512], BF)
    negcol = sb("negcol", [128, 2], F32); epsc = sb("epsc", [128, 1], F32); onec = sb("onec", [128, 1], F32)
    gt = sb("gt", [128, 18], F32)
    gfin = sb("gfin", [128, D], F32)
    wgs = sb("wgs", [32, 2, 256], F32); wga = sb("wga", [32, 2, 256], BF)
    gaug = sb("gaug", [32, 2, 512], BF)
    Sst = sb("Sst", [128, 2, 2, 128], F32)
    Fsnaps = [sb("Fsnap%d" % i, [128, 4, 2, 128], BF) for i in range(2)]
    small = sb("small", [128, 64], F32)
    ARENA = 47700
    arena = sb("arena", [128, ARENA], F32)
    psb = [st.enter_context(nc.psum_tensor("pb%d" % i, [128, 512], F32)) for i in range(8)]
    bconst = Buf()
    cnt = [0]

    def dkey(tag):
        return tag

    cl = [(identb[:], c_identb[:, :]), (identf[:], c_identf[:, :]), (pm[:], c_pm[:, :]), (mask01[:], c_mask[:, :]), (mask01m[:], c_maskm[:, :]),
          (gfin[:], g_final.partition_broadcast(128))]
    for k in range(4):
        cl.append((Mm[:, k, :], c_M[k]))
    for k in range(2):
        cl.append((Gm[:, k, :], c_G[k]))
        cl.append((wgs[0:16, k, :], wgate[k][:, :]))
        cl.append((wgs[16:17, k, :], bgate[k].rearrange("(a n) -> a n", a=1)))
    cl.append((gt[:, 0:8], g_mix.rearrange("(c p) -> p c", p=128)))
    cl.append((gt[:, 8:16], g_ffn.rearrange("(c p) -> p c", p=128)))
    cl.append((gt[:, 16:17], g_gla.rearrange("(p a) -> p a", a=1)))
    S.op("pool", lambda e: e.memset(wgs[:], 0.0), writes=[bconst])
    for i, (dst, src) in enumerate(cl):
        S.op("sp", lambda e, dst=dst, src=src: e.dma_start(out=dst, in_=src, allow_slow_non_contiguous=True),
             writes=[bconst], dsem="c%d" % (i % 4))
    S.op("dve", lambda e: e.memset(negcol[:], -1.0 / 16.0), writes=[bconst])
    S.op("dve", lambda e: e.memset(epsc[:], EPS), writes=[bconst])
    S.op("dve", lambda e: e.memset(onec[:], 1.0), writes=[bconst])
    S.op("dve", lambda e: e.memset(gt[:, 17:18], 1.0), writes=[bconst])
    S.op("dve", lambda e: e.memset(gaug[:], 1.0), writes=[bconst])
    S.barrier()
    S.op("dve", lambda e: e.tensor_copy(wga[0:17], wgs[0:17]), reads=[bconst], writes=[bconst])

    class Arena:
        def __init__(self):
            self.off = 0

        def take(self, nbytes_per_part, dt, shape):
            words = (nbytes_per_part + 3) // 4
            words = (words + 7) // 8 * 8
            a = arena[:, self.off:self.off + words]
            self.off += words
            assert self.off <= ARENA, ("arena overflow", self.off)
            if dt == BF:
                a = a.bitcast(BF)
                n = int(np.prod(shape[1:]))
                a = a[:, 0:n]
            else:
                n = int(np.prod(shape[1:]))
                a = a[:, 0:n]
            if len(shape) == 3:
                a = a.rearrange("p (a b) -> p a b", b=shape[2])
            elif len(shape) == 4:
                a = a.rearrange("p (a b c) -> p a b c", b=shape[2], c=shape[3])
            return a

        def ring(self, n, dt, shape):
            esz = 2 if dt == BF else 4
            nb = int(np.prod(shape[1:])) * esz
            return Ring([self.take(nb, dt, shape) for _ in range(n)])

    PS = Ring([p[:] for p in psb])

    def psbf(ap):
        return ap.bitcast(BF)

    smallr = Ring([small[:, k * 4:k * 4 + 4] for k in range(16)])

    def smallcol(n=1):
        ap, b, _ = smallr.next()
        return ap[:, 0:n], b

    wi = [0]

    def wconv_gen(wlist, stg, stgb, tag, colchunk):
        pieces = []
        for (W, Wb, K, N, g0) in wlist:
            for kc in range(K // 128):
                gcol = 17
                if g0 is not None:
                    gcol = g0 + kc
                elif W is w_out and kc >= 4:
                    gcol = 16
                for c0 in range(0, N, colchunk):
                    pieces.append((W, Wb, kc, gcol, c0, min(colchunk, N - c0)))
        st_ = {}
        n_ = len(pieces)
        for t in range(n_ + 2):
            if t < n_:
                W, Wb, kc, gcol, c0, n = pieces[t]
                a, ab, ak = stg.next()
                S.op("sp", lambda e: e.dma_start(out=a[:, 0:n], in_=W[kc * 128:(kc + 1) * 128, c0:c0 + n]),
                     writes=[ab], dsem="%sl%d" % (tag, ak))
                st_[t] = [a, ab]
            if 0 <= t - 1 < n_:
                W, Wb, kc, gcol, c0, n = pieces[t - 1]
                a, ab = st_[t - 1]
                o, ob, ok = stgb.next()
                if wi[0] % 2 == 0:
                    S.op("act", lambda e: e.activation(out=o[:, 0:n], in_=a[:, 0:n], func=AF.Copy, scale=gt[:, gcol:gcol + 1]),
                         reads=[ab, bconst], writes=[ob])
                else:
                    S.op("dve", lambda e: e.tensor_scalar(out=o[:, 0:n], in0=a[:, 0:n], scalar1=gt[:, gcol:gcol + 1], scalar2=None,
                                                          op0=ALU.mult),
                         reads=[ab, bconst], writes=[ob])
                wi[0] += 1
                st_[t - 1] = [o, ob, ok]
            if 0 <= t - 2 < n_:
                W, Wb, kc, gcol, c0, n = pieces[t - 2]
                o, ob, ok = st_.pop(t - 2)
                S.op("pool", lambda e: e.dma_start(out=Wb[kc * 128:(kc + 1) * 128, c0:c0 + n], in_=o[:, 0:n]),
                     reads=[ob], dsem="%ss%d" % (tag, ok))
            yield

    A = Arena()
    stg = A.ring(4, F32, [128, PW])
    stgb = A.ring(4, BF, [128, PW])
    for _ in wconv_gen([(w_in, win_b, D, PW, 0)], stg, stgb, "w", PW):
        pass
    S.barrier()

    A = Arena()
    winb = A.take(8 * PW * 2, BF, [128, 8, PW])
    Ssnap = A.take(32 * 2 * 128 * 2, BF, [128, 32, 2, 128])
    Xr = A.ring(2, F32, [128, D])
    Hr = A.ring(2, BF, [128, D])
    hTr = A.ring(3, BF, [128, 8, 512])
    junk = A.take(D * 2, BF, [128, D])
    sprs = [[[A.take(1024, F32, [128, 256]) for _ in range(2)] for _ in range(4)] for _ in range(2)]
    spbs = [[[Buf() for _ in range(2)] for _ in range(4)] for _ in range(2)]
    etmp = A.ring(1, F32, [128, 256])
    eDr = A.ring(1, F32, [128, 256])
    kdr = A.ring(8, BF, [128, 256])
    vbr = A.ring(8, BF, [128, 512])
    decr = A.ring(8, F32, [128, 4])
    Er = A.ring(4, F32, [128, 512])
    qin = [[A.take(1024, BF, [128, 512]) for _ in range(2)] for _ in range(2)]
    kin = [[A.take(1024, BF, [128, 512]) for _ in range(2)] for _ in range(2)]
    qinb = [[Buf() for _ in range(2)] for _ in range(2)]
    kinb = [[Buf() for _ in range(2)] for _ in range(2)]
    attr = A.ring(6, BF, [128, 512])
    sqr = A.ring(2, F32, [128, 512])
    onr = A.ring(2, F32, [128, 512])
    err = A.ring(2, F32, [128, 512])
    ogr = A.ring(2, BF, [128, 512])
    rotr = A.ring(1, F32, [128, 2, 512])
    qrawr = A.ring(2, BF, [128, 512])
    t1r = A.ring(1, F32, [128, 512])
    t2r = A.ring(1, F32, [128, 512])
    qfr = A.ring(2, BF, [128, 512])
    vaugr = A.ring(2, BF, [128, 8, 65])
    validt = A.take(48 * 4, F32, [128, 48])
    zt = Ssnap.rearrange("p a b c -> p (a b c)")[:, 0:1024]
    bwin = Buf(); bvalid = Buf()
    bS = [Buf(), Buf()]; bSq = [[Buf() for _ in range(4)] for _ in range(2)]; bSsnap = Buf(); bzt = bSsnap; bFsnaps = [Buf(), Buf()]; bgaug = [Buf(), Buf()]

    for kc in range(8):
        S.op("sp", lambda e, kc=kc: e.dma_start(out=winb[:, kc, :], in_=win_b[kc * 128:(kc + 1) * 128, :]),
             writes=[bwin], dsem="wi%d" % (kc % 2))
    S.op("pool", lambda e: e.memset(zt[:], 0.0), writes=[bzt])

    def mm_group(out, pairs, ps_b, reads):
        n = len(pairs)
        for i, (l, r) in enumerate(pairs):
            S.op("pe", lambda e, l=l, r=r, i=i: e.matmul(out, l, r, start=(i == 0), stop=(i == n - 1)),
                 reads=reads, writes=[ps_b])

    def rstd_of(src, src_b, width):
        ss, ssb = smallcol()
        S.op("act", lambda e: e.activation(out=junk[:, 0:width], in_=src, func=AF.Square, accum_out=ss),
             reads=[src_b], writes=[ssb, bjunk])
        ln, lnb = smallcol()
        S.op("act", lambda e: e.activation(out=ln, in_=ss, func=AF.Ln, scale=1.0 / width, bias=epsc[:, 0:1]),
             reads=[ssb, bconst], writes=[lnb])
        rs, rsb = smallcol()
        S.op("act", lambda e: e.activation(out=rs, in_=ln, func=AF.Exp, scale=-0.5), reads=[lnb], writes=[rsb])
        return rs, rsb

    bjunk = Buf()

    dbg = {}
    if DEBUG:
        dbg["x"] = nc.dram_tensor("dbgx", [32, 128, D], F32, kind="ExternalOutput").ap()
        dbg["h"] = nc.dram_tensor("dbgh", [32, 128, D], BF, kind="ExternalOutput").ap()
        dbg["r"] = nc.dram_tensor("dbgr", [32, 128, 4], F32, kind="ExternalOutput").ap()
        dbg["hT"] = nc.dram_tensor("dbghT", [8, 128, 8, 512], BF, kind="ExternalOutput").ap()

    def prep_tile(xsrc_rows, hT, hTb, col, dbi=None):
        xt, xb, xk = Xr.next()
        S.op("sp", lambda e: e.dma_start(out=xt, in_=xsrc_rows), writes=[xb], dsem="x%d" % xk)
        rs, rsb = rstd_of(xt, xb, D)
        hb, hbb, _ = Hr.next()
        S.op("dve", lambda e: e.tensor_scalar(out=hb, in0=xt, scalar1=rs, scalar2=None, op0=ALU.mult),
             reads=[xb, rsb], writes=[hbb])
        pt, ptb, _ = PS.next()
        ptv = psbf(pt)
        for kc in range(8):
            S.op("pe", lambda e, kc=kc: e.transpose(ptv[:, kc * 128:(kc + 1) * 128], hb[:, kc * 128:(kc + 1) * 128], identb[:]),
                 reads=[hbb, bconst], writes=[ptb])
        S.op("act", lambda e: e.activation(out=hT[:, :, col:col + 128], in_=ptv.rearrange("p (a b) -> p a b", b=128),
                                           func=AF.Copy),
             reads=[ptb], writes=[hTb])
        if DEBUG and dbi is not None:
            S.op("sp", lambda e: e.dma_start(out=dbg["x"][dbi], in_=xt), reads=[xb], dsem="dbg0")
            S.op("sp", lambda e: e.dma_start(out=dbg["h"][dbi], in_=hb), reads=[hbb], dsem="dbg1")
            S.op("sp", lambda e: e.dma_start(out=dbg["r"][dbi][:, 0:1], in_=rs, allow_slow_non_contiguous=True), reads=[rsb], dsem="dbg2")
            if col == 384:
                S.op("sp", lambda e: e.dma_start(out=dbg["hT"][dbi // 4], in_=hT), reads=[hTb], dsem="dbg3")

    def interleave(gens, width):
        pending = list(gens)
        active = []
        while pending or active:
            while pending and len(active) < width:
                active.append(pending.pop(0))
            for g in list(active):
                try:
                    next(g)
                except StopIteration:
                    active.remove(g)
                yield

    def run_streams(gens, width):
        for _ in interleave(gens, width):
            pass

    def gla_tokmajor(hT, hTb, ti, col, dirs, scan, snap_dst, res, gset, split_state=False):
        spr, spb = sprs[gset], spbs[gset]
        bFsnap = bFsnaps[gset]
        lh = [hT[:, kc, col:col + 128] for kc in range(8)]
        pv, pvb, _ = PS.next()
        mm_group(pv[:, :], [(lh[kc], winb[:, kc, C_VA:C_VA + 512]) for kc in range(8)], pvb, [hTb, bwin])
        vb, vbb, _ = vbr.next()
        S.op("act", lambda e: e.activation(out=vb, in_=pv[:, :], func=AF.Copy), reads=[pvb], writes=[vbb])
        res["vb"] = (vb, vbb)
        for d in dirs:
            pz, pzb, _ = PS.next()
            S.op("pe", lambda e, d=d: e.matmul(pz[:, 0:256], gaug[0:17, d, col:col + 128], wga[0:17, d, :], start=True, stop=True),
                 reads=[bgaug[d], bconst], writes=[pzb])
            et_, etb, _ = etmp.next()
            S.op("act", lambda e, et_=et_: e.activation(out=et_, in_=pz[:, 0:256], func=AF.Exp, scale=-1.0),
                 reads=[pzb], writes=[etb])
            sp_, spb_ = spr[ti][d], spb[ti][d]
            S.op("act", lambda e, et_=et_, sp_=sp_: e.activation(out=sp_, in_=et_, func=AF.Ln, bias=onec[:, 0:1]),
                 reads=[etb, bconst], writes=[spb_])
        yield
        d = scan
        sp_, spb_ = spr[ti][d], spb[ti][d]
        pd, pdb, _ = PS.next()
        S.op("pe", lambda e: e.matmul(pd[:, 0:256], Mm[:, 2 + d, :], sp_, start=True, stop=True),
             reads=[spb_, bconst], writes=[pdb])
        for j in range(2):
            S.op("pe", lambda e, j=j: e.matmul(pd[:, 256 + 2 * j:258 + 2 * j], sp_[:, j * 128:(j + 1) * 128], negcol[:, 0:2],
                                               start=True, stop=True),
                 reads=[spb_, bconst], writes=[pdb])
        eD, eDb, _ = eDr.next()
        S.op("act", lambda e: e.activation(out=eD, in_=pd[:, 0:256], func=AF.Exp), reads=[pdb], writes=[eDb])
        dec, decb, _ = decr.next()
        S.op("act", lambda e: e.activation(out=dec, in_=pd[:, 256:260], func=AF.Exp), reads=[pdb], writes=[decb])
        pk, pkb, _ = PS.next()
        mm_group(pk[:, 0:256], [(lh[kc], winb[:, kc, C_KA:C_KA + 256]) for kc in range(8)], pkb, [hTb, bwin])
        kd, kdb, _ = kdr.next()
        S.op("dve", lambda e: e.tensor_tensor(out=kd, in0=pk[:, 0:256], in1=eD, op=ALU.mult), reads=[pkb, eDb], writes=[kdb])
        yield

        def state_gen():
            if snap_dst is not None:
                S.op("act", lambda e: e.activation(out=snap_dst, in_=Sst[:, d, :, :], func=AF.Copy), reads=bSq[d],
                     writes=[bSsnap if d == 1 else bFsnap])
            pkv, pkvb, _ = PS.next()
            for j in range(2):
                S.op("pe", lambda e, j=j: e.matmul(pkv[:, j * 256:(j + 1) * 256], kd[:, j * 128:(j + 1) * 128],
                                                   vb[:, j * 256:(j + 1) * 256], start=True, stop=True),
                     reads=[kdb, vbb], writes=[pkvb])
            for j in range(2):
                for par in range(2):
                    rows = slice(par * 64, (par + 1) * 64)
                    sbuf_ = bSq[d][2 * j + par]
                    S.op("dve", lambda e, j=j, par=par, rows=rows: e.scalar_tensor_tensor(
                        out=Sst[rows, d, j, :], in0=Sst[rows, d, j, :], scalar=dec[rows, 2 * j:2 * j + 1],
                        in1=pkv[rows, j * 256 + par * 128:j * 256 + (par + 1) * 128], op0=ALU.mult, op1=ALU.add),
                        reads=[pkvb, decb, sbuf_], writes=[sbuf_])
            yield
        if split_state:
            res["state"] = state_gen()
        else:
            for _ in state_gen():
                yield

    def prep_group(job, tiles, full):
        hT, hTb, _ = hTr.next()

        def gen():
            xsrc = xin[job]
            for ti in range(4):
                et = tiles[ti]
                prep_tile(xsrc[et * 128:(et + 1) * 128, :], hT, hTb, ti * 128, dbi=(et if (full and job == 0) else None))
                yield
        return (hT, hTb), gen()

    def proc_group(job, tiles, scan, full, dilkv, gset, hTctx):
        Fsnap = Fsnaps[gset]
        base = tiles[0]
        hT, hTb = hTctx
        order = list(range(4)) if scan == 0 else [3, 2, 1, 0]
        dirs = [0, 1] if full else [scan]

        def gproj_gen():
            for d in dirs:
                pg, pgb, _ = PS.next()
                mm_group(pg[0:16, :], [(winb[:, kc, C_G + 16 * d:C_G + 16 * d + 16], hT[:, kc, :]) for kc in range(8)], pgb, [hTb, bwin])
                S.op("act", lambda e, d=d, pg=pg: e.activation(out=gaug[0:16, d, :], in_=pg[0:16, :], func=AF.Copy),
                     reads=[pgb], writes=[bgaug[d]])
            yield
        tm = {}
        gens = []
        for ti in order:
            et = tiles[ti]
            is_main = (job == 0) or (32 <= et < 64)
            mt = et - (32 if job == 1 else 0)
            if not is_main:
                snap_dst = None
            elif scan == 1:
                snap_dst = Ssnap[:, mt, :, :]
            else:
                snap_dst = Fsnap[:, ti, :, :]
            tm[ti] = {}
            gens.append(gla_tokmajor(hT, hTb, ti, ti * 128, dirs, scan, snap_dst, tm[ti], gset, split_state=(not full)))

        def dil_gen():
            det0 = (base + 8) if job == 0 else (base - 24)
            ecol = det0 * 128
            rt, rtb, rk = rotr.next()
            S.op("sp", lambda e: e.dma_start(out=rt, in_=rot[job][:, :, ecol:ecol + 512].rearrange("a p n -> p a n")),
                 writes=[rtb], dsem="rot%d" % rk)
            which = [("k", C_KD)] + ([("q", C_QD)] if full else [])

            def finish(nm, hp, qr, qrb):
                pp, ppb, _ = PS.next()
                S.op("pe", lambda e: e.matmul(pp[:, :], pm[:], qr, start=True, stop=True),
                     reads=[qrb, bconst], writes=[ppb])
                t1, t1b, _ = t1r.next()
                S.op("dve", lambda e: e.tensor_tensor(out=t1, in0=pp[:, :], in1=rt[:, 1, :], op=ALU.mult),
                     reads=[ppb, rtb], writes=[t1b])
                t2, t2b, _ = t2r.next()
                S.op("pool", lambda e: e.tensor_tensor(out=t2, in0=qr, in1=rt[:, 0, :], op=ALU.mult),
                     reads=[qrb, rtb], writes=[t2b])
                qf, qfb, qfk = qfr.next()
                S.op("dve", lambda e: e.tensor_tensor(out=qf, in0=t1, in1=t2, op=ALU.add),
                     reads=[t1b, t2b], writes=[qfb])
                if nm == "q":
                    mcol = (base - (32 if job == 1 else 0)) * 128
                    dst = qT_s[job][hp][:, mcol:mcol + 512]
                else:
                    dst = kT_s[job][hp][:, ecol:ecol + 512]
                S.op("pool", lambda e: e.dma_start(out=dst, in_=qf), reads=[qfb], dsem="qf%d" % qfk)

            prev = None
            for hp in range(4):
                for nm, c0 in which:
                    pq, pqb, _ = PS.next()
                    mm_group(pq[:, :], [(winb[:, kc, c0 + hp * 128:c0 + (hp + 1) * 128], hT[:, kc, :]) for kc in range(8)],
                             pqb, [hTb, bwin])
                    qr, qrb, _ = qrawr.next()
                    S.op("act", lambda e, qr=qr, pq=pq: e.activation(out=qr, in_=pq[:, :], func=AF.Copy), reads=[pqb], writes=[qrb])
                    if prev is not None:
                        finish(*prev)
                    prev = (nm, hp, qr, qrb)
                    yield
            for ti in range(4):
                col = ti * 128
                pv, pvb, _ = PS.next()
                mm_group(pv[:, :], [(hT[:, kc, col:col + 128], winb[:, kc, C_VD:C_VD + 512]) for kc in range(8)], pvb, [hTb, bwin])
                if prev is not None:
                    finish(*prev)
                    prev = None
                va, vab, vak = vaugr.next()
                S.op("act", lambda e, va=va, pv=pv: e.activation(out=va[:, :, 0:64], in_=pv[:, :].rearrange("p (a b) -> p a b", b=64),
                                                                   func=AF.Copy), reads=[pvb], writes=[vab])
                det = det0 + ti
                S.op("dve", lambda e, va=va, det=det: e.tensor_copy(va[:, :, 64:65],
                                                                    validt[:, det:det + 1].broadcast_to([128, 8]).rearrange("p (a b) -> p a b", b=1)),
                     reads=[bvalid], writes=[vab])
                S.op("pool", lambda e, va=va, det=det: e.dma_start(out=v_s[job][det * 128:(det + 1) * 128, :],
                                                                   in_=va.rearrange("p a b -> p (a b)")),
                     reads=[vab], dsem="va%d" % vak)
                yield

        if dilkv:
            gens.insert(2, dil_gen())

        def xgen():
            for _ in gproj_gen():
                yield
            for _ in interleave(gens, 3):
                yield
        def xbgen():
            for ti in order:
                for _ in tm[ti]["state"]:
                    yield
        return xgen(), ((job, tiles, hT, hTb, tm, gset) if full else None), (None if full else xbgen())

    def y_gen(job, tiles, hT, hTb, tm, gset):
        spr, spb = sprs[gset], spbs[gset]
        Fsnap, bFsnap = Fsnaps[gset], bFsnaps[gset]
        for j in range(2):
            EE = []
            for d in range(2):
                pb_, pbb, _ = PS.next()
                for ti in range(4):
                    S.op("pe", lambda e, ti=ti: e.matmul(pb_[:, ti * 128:(ti + 1) * 128], spr[ti][d][:, j * 128:(j + 1) * 128],
                                                         Mm[:, d, :], start=True, stop=True),
                         reads=[spb[ti][d], bconst], writes=[pbb])
                Ep, Epb, _ = Er.next()
                S.op("act", lambda e: e.activation(out=Ep, in_=pb_[:, :], func=AF.Exp), reads=[pbb], writes=[Epb])
                En, Enb, _ = Er.next()
                S.op("act", lambda e: e.activation(out=En, in_=pb_[:, :], func=AF.Exp, scale=-1.0), reads=[pbb], writes=[Enb])
                EE.append((Ep, Epb, En, Enb))
            yield
            pqa, pqab, _ = PS.next()
            mm_group(pqa[:, :], [(winb[:, kc, C_QA + j * 128:C_QA + (j + 1) * 128], hT[:, kc, :]) for kc in range(8)], pqab, [hTb, bwin])
            pka, pkab, _ = PS.next()
            mm_group(pka[:, :], [(winb[:, kc, C_KA + j * 128:C_KA + (j + 1) * 128], hT[:, kc, :]) for kc in range(8)], pkab, [hTb, bwin])
            for d in range(2):
                Ep, Epb, En, Enb = EE[d]
                S.op("dve", lambda e: e.scalar_tensor_tensor(out=qin[d][j], in0=pqa[:, :], scalar=0.125, in1=Ep,
                                                             op0=ALU.mult, op1=ALU.mult),
                     reads=[pqab, Epb], writes=[qinb[d][j]])
                S.op("dve", lambda e: e.tensor_tensor(out=kin[d][j], in0=pka[:, :], in1=En, op=ALU.mult),
                     reads=[pkab, Enb], writes=[kinb[d][j]])
            yield

        def y_tile(ti):
            et = tiles[ti]
            mt = et - (32 if job == 1 else 0)
            col = ti * 128
            vb, vbb = tm[ti]["vb"]
            atts = []
            for d in range(2):
                at, atb, _ = attr.next()
                atv = at.rearrange("p (j q t) -> p j q t", q=2, t=128)
                for par in range(2):
                    pa, pab, _ = PS.next()
                    rows = slice(par * 64, (par + 1) * 64)
                    for j in range(2):
                        S.op("pe", lambda e, j=j: e.matmul(pa[:, j * 128:(j + 1) * 128], kin[d][j][rows, col:col + 128],
                                                           qin[d][j][rows, col:col + 128], start=True, stop=True),
                             reads=[kinb[d][j], qinb[d][j]], writes=[pab])
                    S.op("dve", lambda e: e.tensor_tensor(out=atv[:, :, par, :], in0=pa[:, 0:256].rearrange("p (j t) -> p j t", t=128),
                                                          in1=Gm[:, d, 0:256].rearrange("p (j t) -> p j t", t=128), op=ALU.mult),
                         reads=[pab, bconst], writes=[atb])
                atts.append((at, atb))
            pr, prb, _ = PS.next()
            mm_group(pr[:, :], [(hT[:, kc, col:col + 128], winb[:, kc, C_RA:C_RA + 512]) for kc in range(8)], prb, [hTb, bwin])
            er, erb, _ = err.next()
            S.op("act", lambda e: e.activation(out=er, in_=pr[:, :], func=AF.Exp, scale=-1.0), reads=[prb], writes=[erb])
            S.op("pool", lambda e: e.tensor_scalar(out=er, in0=er, scalar1=1.0, scalar2=1.0, op0=ALU.add, op1=ALU.mult),
                 reads=[erb], writes=[erb])
            S.op("dve", lambda e: e.reciprocal(out=er, in_=er), reads=[erb], writes=[erb])
            S.op("dve", lambda e: e.tensor_tensor(out=er, in0=pr[:, :], in1=er, op=ALU.mult), reads=[prb, erb], writes=[erb])
            yield
            po, pob, _ = PS.next()
            for h in range(4):
                j, par = h // 2, h % 2
                rows = slice(par * 64, (par + 1) * 64)
                hc = slice(h * 128, (h + 1) * 128)
                pairs = [(atts[0][0][:, hc], vb[:, hc]), (atts[1][0][:, hc], vb[:, hc]),
                         (qin[0][j][rows, col:col + 128], Fsnap[rows, ti, j, :]),
                         (qin[1][j][rows, col:col + 128], Ssnap[rows, mt, j, :])]
                mm_group(po[:, hc], pairs, pob, [atts[0][1], atts[1][1], vbb, qinb[0][j], qinb[1][j], bFsnap, bSsnap])
            sq, sqb, _ = sqr.next()
            S.op("act", lambda e: e.activation(out=sq, in_=po[:, :], func=AF.Square), reads=[pob], writes=[sqb])
            ss, ssb = smallcol(4)
            S.op("dve", lambda e: e.tensor_reduce(out=ss, in_=sq.rearrange("p (a b) -> p a b", b=128), axis=AX.X, op=ALU.add),
                 reads=[sqb], writes=[ssb])
            ln, lnb = smallcol(4)
            S.op("act", lambda e: e.activation(out=ln, in_=ss, func=AF.Ln, scale=1.0 / 128, bias=epsc[:, 0:1]),
                 reads=[ssb, bconst], writes=[lnb])
            rs, rsb = smallcol(4)
            S.op("act", lambda e: e.activation(out=rs, in_=ln, func=AF.Exp, scale=-0.5), reads=[lnb], writes=[rsb])
            on, onb, _ = onr.next()
            for h in range(4):
                S.op("act" if h % 2 else "dve", (lambda e, h=h: e.activation(out=on[:, h * 128:(h + 1) * 128], in_=po[:, h * 128:(h + 1) * 128],
                                                                            func=AF.Copy, scale=rs[:, h:h + 1])) if h % 2 else
                     (lambda e, h=h: e.tensor_scalar(out=on[:, h * 128:(h + 1) * 128], in0=po[:, h * 128:(h + 1) * 128],
                                                     scalar1=rs[:, h:h + 1], scalar2=None, op0=ALU.mult)),
                     reads=[pob, rsb], writes=[onb])
            og, ogb, ogk = ogr.next()
            S.op("pool", lambda e: e.tensor_tensor(out=og, in0=on, in1=er, op=ALU.mult), reads=[onb, erb], writes=[ogb])
            S.op("pool", lambda e: e.dma_start(out=o_s[job][mt * 128:(mt + 1) * 128, 512:1024], in_=og),
                 reads=[ogb], dsem="og%d" % ogk)
            yield

        for _ in interleave([y_tile(ti) for ti in range(4)], 2):
            yield

    def zero_state(d):
        S.op("dve", lambda e: e.memset(Sst[:, d, :, :], 0.0), reads=bSq[d], writes=bSq[d])

    for job in range(2):
        S.op("sp", lambda e, job=job: e.dma_start(out=validt[:], in_=valid[job][:, :]), writes=[bvalid], dsem="vld")
        if job == 0:
            for hp in range(4):
                for c0 in (0, HALO + SEG):
                    S.op("pool", lambda e, hp=hp, c0=c0: e.dma_start(out=kT_s[0][hp][:, c0:c0 + HALO], in_=zt[:, 0:HALO]),
                         reads=[bzt], dsem="z0")
            for r0 in (0, HALO + SEG):
                for k in range(8):
                    S.op("pool", lambda e, r0=r0, k=k: e.dma_start(out=v_s[0][r0 + k * 128:r0 + (k + 1) * 128, :], in_=zt[:, 0:520]),
                         reads=[bzt], dsem="z1")
            moff = 0
        else:
            moff = 32
        descs = []
        if job == 1:
            for g in range(23, 15, -1):
                descs.append(([4 * g + i for i in range(4)], 1, False, g < 18))
        for g in range(7, -1, -1):
            descs.append(([moff + 4 * g + i for i in range(4)], 1, False, False))
        nback = len(descs)
        if job == 1:
            for g in range(0, 8):
                descs.append(([4 * g + i for i in range(4)], 0, False, g >= 6))
        for g in range(8):
            descs.append(([moff + 4 * g + i for i in range(4)], 0, True, True))
        zero_state(1)
        hctx, pg0 = prep_group(job, descs[0][0], descs[0][2])
        run_streams([pg0], 1)
        prev_y = None
        prev_xb = None
        nfull = 0
        for i, (tiles, scan, full, dilkv) in enumerate(descs):
            if i == nback:
                zero_state(0)
            gset = nfull % 2
            xg, yctx, xb = proc_group(job, tiles, scan, full, dilkv, gset, hctx)
            streams = [xg]
            if prev_xb is not None:
                streams.insert(0, prev_xb)
            prev_xb = xb
            if prev_y is not None:
                streams.append(y_gen(*prev_y))
            if i + 1 < len(descs):
                hctx, pgn = prep_group(job, descs[i + 1][0], descs[i + 1][2])
                streams.append(pgn)
            run_streams(streams, 4)
            prev_y = yctx
            if full:
                nfull += 1
        run_streams([y_gen(*prev_y)], 1)
    S.barrier()

    A = Arena()
    qT = A.take(SEG * 2, BF, [128, SEG]); q4 = A.take(SEG * 2, BF, [128, 4, SEG // 4]); q16 = A.take(SEG * 2, BF, [128, 16, SEG // 16])
    kT = A.take(EXT * 2, BF, [128, EXT]); k4 = A.take(EXT * 2, BF, [128, 4, EXT // 4]); k16 = A.take(EXT * 2, BF, [128, 16, EXT // 16])
    NT1, NT4, NT16 = 33, 9, 3
    vg1 = A.take(NT1 * 260, BF, [128, NT1, 130])
    vg4 = A.take(4 * NT4 * 260, BF, [128, 4 * NT4, 130])
    vg16 = A.take(16 * NT16 * 260, BF, [128, 16 * NT16, 130])
    accTs = [[A.take(SEG * 4, F32, [128, SEG]) for _ in range(2)] for _ in range(2)]
    Pr = A.ring(4, BF, [128, 256])
    Pmr = A.ring(10, BF, [128, 256])
    ostr = A.ring(2, BF, [128, 128])
    stg2 = A.ring(3, F32, [128, 1408])
    stgb2 = A.ring(3, BF, [128, 1408])
    wgen2 = wconv_gen([(w_out, wout_b, D, D, None), (w_g, wg_b, D, DFF, 8), (w_u, wu_b, D, DFF, 8), (w_d, wd_b, DFF, D, None)],
                      stg2, stgb2, "v", 1408)
    bq = Buf(); bk = Buf(); bqp = [None, Buf(), Buf()]; bkp = [None, Buf(), Buf()]; bvgs = [Buf(), Buf(), Buf()]; baccs = [[Buf(), Buf()], [Buf(), Buf()]]
    bqp[0] = bq; bkp[0] = bk
    PSs = Ring([psb[i][:, 0:256] for i in range(4)])
    PSu = [Ring([psb[4][:], psb[5][:]]), Ring([psb[6][:], psb[7][:]])]

    for job in range(2):
        for hp in range(4):
            accT = accTs[hp % 2]; bacc = baccs[hp % 2]
            S.op("sp", lambda e, job=job, hp=hp: e.dma_start(out=qT[:], in_=qT_s[job][hp]), writes=[bq], dsem="bq")
            S.op("sp", lambda e, job=job, hp=hp: e.dma_start(out=kT[:], in_=kT_s[job][hp]), writes=[bk], dsem="bk")
            S.op("dve", lambda e: e.tensor_copy(q4[:], qT.rearrange("p (l r) -> p r l", r=4)), reads=[bq], writes=[bqp[1]])
            S.op("dve", lambda e: e.tensor_copy(k4[:], kT.rearrange("p (l r) -> p r l", r=4)), reads=[bk], writes=[bkp[1]])
            S.op("dve", lambda e: e.tensor_copy(q16[:], qT.rearrange("p (l r) -> p r l", r=16)), reads=[bq], writes=[bqp[2]])
            S.op("act", lambda e: e.activation(out=k16[:], in_=kT.rearrange("p (l r) -> p r l", r=16), func=AF.Copy), reads=[bk], writes=[bkp[2]])
            vcols = slice(hp * 130, (hp + 1) * 130)
            vsrc = v_s[job]
            for t0 in (0, 11, 22):
                S.op("sp", lambda e: e.dma_start(out=vg1[:, t0:t0 + 11, :], in_=vsrc[960 + t0 * 128:960 + (t0 + 11) * 128, vcols].rearrange("(t p) c -> p t c", p=128)),
                     writes=[bvgs[0]], dsem="bv0")
            for r in range(4):
                S.op("sp", lambda e, vsrc=vsrc, vcols=vcols, r=r: e.dma_start(
                    out=vg4[:, r * NT4:(r + 1) * NT4, :],
                    in_=vsrc[r + 768:r + 768 + 4 * (NT4 * 128 - 1) + 1:4, vcols].rearrange("(t p) c -> p t c", p=128)),
                    writes=[bvgs[1]], dsem="bv1")
            for r in range(16):
                S.op("sp", lambda e, vsrc=vsrc, vcols=vcols, r=r: e.dma_start(
                    out=vg16[:, r * NT16:(r + 1) * NT16, :],
                    in_=vsrc[r:r + 16 * (NT16 * 128 - 1) + 1:16, vcols].rearrange("(t p) c -> p t c", p=128)),
                    writes=[bvgs[2]], dsem="bv2")
            pats = [(1, 32, qT.rearrange("p (r l) -> p r l", r=1), kT.rearrange("p (r l) -> p r l", r=1), vg1, 1024),
                    (4, 8, q4, k4, vg4, 256), (16, 2, q16, k16, vg16, 64)]
            units = []
            for pi, pat in enumerate(pats):
                for r in range(pat[0]):
                    curs = [{}, {}]
                    for kt in range(pat[1] + 1):
                        for par in range(2):
                            units.append((pi, pat, r, par, kt, curs[par]))
            staged = {}

            def stage1(ui):
                pi, (dil, nqt, qp, kp, vg, Q0), r, par, kt, cur = units[ui]
                rows = slice(par * 64, (par + 1) * 64)
                qa = max(0, kt - 1); qb_ = min(nqt - 1, kt)
                n = (qb_ - qa + 1) * 128
                jlo = 128 * (qa - (kt - 1))
                k0 = Q0 + 128 * kt - 64
                ps_, psb_, _ = PSs.next()
                mode = MASK_MODES[ui % len(MASK_MODES)]
                P2, P2b, _ = Pmr.next()
                if mode == "pe":
                    S.op("pe", lambda e: e.matmul(ps_[:, 0:n], identb[:], mask01[:, jlo:jlo + n], start=True, stop=False),
                         reads=[bconst], writes=[psb_])
                    S.op("pe", lambda e: e.matmul(ps_[:, 0:n], kp[rows, r, k0:k0 + 128], qp[rows, r, 128 * qa:128 * qa + n], start=False, stop=True),
                         reads=[bqp[pi], bkp[pi]], writes=[psb_])
                    S.op("act", lambda e: e.activation(out=P2[:, 0:n], in_=ps_[:, 0:n], func=AF.Exp, scale=0.125),
                         reads=[psb_], writes=[P2b])
                else:
                    S.op("pe", lambda e: e.matmul(ps_[:, 0:n], kp[rows, r, k0:k0 + 128], qp[rows, r, 128 * qa:128 * qa + n], start=True, stop=True),
                         reads=[bqp[pi], bkp[pi]], writes=[psb_])
                    P, Pb, _ = Pr.next()
                    S.op("act", lambda e: e.activation(out=P[:, 0:n], in_=ps_[:, 0:n], func=AF.Exp, scale=0.125),
                         reads=[psb_], writes=[Pb])
                    S.op(mode, lambda e: e.tensor_tensor(out=P2[:, 0:n], in0=P[:, 0:n], in1=mask01m[:, jlo:jlo + n], op=ALU.mult),
                         reads=[Pb, bconst], writes=[P2b])
                staged[ui] = (P2, P2b, qa)

            def stage2(ui):
                pi, (dil, nqt, qp, kp, vg, Q0), r, par, kt, cur = units[ui]
                ntile = nqt + 1
                acc = accT[par]
                P2, P2b, qa = staged.pop(ui)
                vgt = vg[:, r * ntile + kt, par * 65:(par + 1) * 65]
                qts = [qt for qt in (kt - 1, kt) if 0 <= qt <= nqt - 1]
                if len(qts) == 2 and kt % 4 != 0:
                    groups = [qts]
                else:
                    groups = [[qt] for qt in qts]
                for grp in groups:
                    qt0 = grp[0]
                    bank = qt0 // 4
                    first = (bank not in cur)
                    if first:
                        cur[bank] = PSu[par].next()
                    pu, pub, _ = cur[bank]
                    last = (kt == grp[-1] + 1) and (grp[-1] % 4 == 3 or grp[-1] == nqt - 1)
                    c0 = (qt0 % 4) * 128
                    nn = 128 * len(grp)
                    p0 = (qt0 - qa) * 128
                    S.op("pe", lambda e: e.matmul(pu[0:65, c0:c0 + nn], vgt, P2[:, p0:p0 + nn], start=first, stop=last),
                         reads=[P2b, bvgs[pi]], writes=[pub])
                    if last:
                        qt = grp[-1]
                        q0 = (qt // 4) * 4
                        ncol = (qt - q0 + 1) * 128
                        s0 = r + dil * 128 * q0
                        dst = acc[0:65, s0:s0 + dil * (ncol - 1) + 1:dil]
                        if pi == 0:
                            S.op("dve", lambda e: e.tensor_copy(dst, pu[0:65, 0:ncol]),
                                 reads=[pub], writes=[bacc[par]])
                        else:
                            S.op("dve", lambda e: e.tensor_tensor(out=dst, in0=pu[0:65, 0:ncol], in1=dst, op=ALU.add),
                                 reads=[pub, bacc[par]], writes=[bacc[par]])

            LA = 3
            for i in range(len(units) + LA):
                if i < len(units):
                    stage1(i)
                if i - LA >= 0:
                    stage2(i - LA)
            for i in range(32):
                os_, osb, osk = ostr.next()
                for par in range(2):
                    pf, pfb, _ = PSs.next()
                    S.op("pe", lambda e, pf=pf, par=par, i=i: e.transpose(pf[:, 0:65], accT[par][0:65, i * 128:(i + 1) * 128], identf[0:65, 0:65]),
                         reads=[bacc[par], bconst], writes=[pfb])
                    rc, rcb = smallcol()
                    S.op("dve", lambda e, rc=rc, pf=pf: e.reciprocal(out=rc, in_=pf[:, 64:65]), reads=[pfb], writes=[rcb])
                    S.op("act", lambda e, os_=os_, pf=pf, rc=rc, par=par: e.activation(out=os_[:, par * 64:(par + 1) * 64], in_=pf[:, 0:64], func=AF.Copy, scale=rc),
                         reads=[pfb, rcb], writes=[osb])
                S.op("pool", lambda e, os_=os_, job=job, hp=hp, i=i: e.dma_start(out=o_s[job][i * 128:(i + 1) * 128, hp * 128:(hp + 1) * 128], in_=os_),
                     reads=[osb], dsem="os%d" % osk)
                if i % 4 == 1:
                    next(wgen2, None)
    for _ in wgen2:
        pass
    S.barrier()

    A = Arena()
    woutb = A.take(8 * D * 2, BF, [128, 8, D])
    wdb = A.take(22 * D * 2, BF, [128, 22, D])
    wgr = A.ring(3, BF, [128, 8, 128])
    wur = A.ring(3, BF, [128, 8, 128])
    x1s = [A.take(4 * D * 4, F32, [128, 4, D]) for _ in range(2)]
    h2Ts = [A.take(8 * 512 * 2, BF, [128, 8, 512]) for _ in range(2)]
    aT = A.take(22 * 512 * 2, BF, [128, 22, 512])
    otr = A.ring(2, BF, [128, D])
    oTr = A.ring(2, BF, [128, 8, 128])
    xr2 = A.ring(2, F32, [128, D])
    h2r = A.ring(2, BF, [128, D])
    sgr = A.ring(2, F32, [128, 512])
    yr = A.ring(2, F32, [128, D])
    junk2 = A.take(D * 2, BF, [128, D])
    bwo = Buf(); bwd = Buf(); bx1s = [[Buf() for _ in range(4)] for _ in range(2)]; bh2Ts = [Buf(), Buf()]; baT = [Buf() for _ in range(22)]
    for kc in range(8):
        S.op("sp", lambda e, kc=kc: e.dma_start(out=woutb[:, kc, :], in_=wout_b[kc * 128:(kc + 1) * 128, :]), writes=[bwo], dsem="cw0")
    for c in range(22):
        S.op("sp", lambda e, c=c: e.dma_start(out=wdb[:, c, :], in_=wd_b[c * 128:(c + 1) * 128, :]), writes=[bwd], dsem="cw1")
    junk = junk2

    def c_prologue(job, blk, cset):
        x1, h2T, bx1, bh2T = x1s[cset], h2Ts[cset], bx1s[cset], bh2Ts[cset]
        xoff = 0 if job == 0 else SEG
        for i in range(4):
            row0 = blk * 512 + i * 128
            ot, otb, otk = otr.next()
            S.op("sp", lambda e: e.dma_start(out=ot, in_=o_s[job][row0:row0 + 128, :]), writes=[otb], dsem="co%d" % otk)
            xt, xb, xk = xr2.next()
            S.op("sp", lambda e: e.dma_start(out=xt, in_=xin[job][xoff + row0:xoff + row0 + 128, :]), writes=[xb], dsem="cx%d" % xk)
            pt, ptb, _ = PS.next()
            ptv = psbf(pt)
            for kc in range(8):
                S.op("pe", lambda e, kc=kc: e.transpose(ptv[:, kc * 128:(kc + 1) * 128], ot[:, kc * 128:(kc + 1) * 128], identb[:]),
                     reads=[otb, bconst], writes=[ptb])
            oT, oTb, _ = oTr.next()
            S.op("act", lambda e: e.activation(out=oT, in_=ptv.rearrange("p (a b) -> p a b", b=128), func=AF.Copy),
                 reads=[ptb], writes=[oTb])
            yield
            for half in range(2):
                py, pyb, _ = PS.next()
                mm_group(py[:, :], [(oT[:, kc, :], woutb[:, kc, half * 512:(half + 1) * 512]) for kc in range(8)], pyb, [oTb, bwo])
                S.op("dve", lambda e, py=py, half=half: e.tensor_tensor(out=x1[:, i, half * 512:(half + 1) * 512], in0=py[:, :],
                                                                       in1=xt[:, half * 512:(half + 1) * 512], op=ALU.add),
                     reads=[pyb, xb], writes=[bx1[i]])
            rs, rsb = rstd_of(x1[:, i, :], bx1[i], D)
            h2, h2b, _ = h2r.next()
            S.op("pool", lambda e: e.tensor_scalar(out=h2, in0=x1[:, i, :], scalar1=rs, scalar2=1.0, op0=ALU.mult, op1=ALU.mult),
                 reads=[bx1[i], rsb], writes=[h2b])
            yield
            pt2, pt2b, _ = PS.next()
            ptv2 = psbf(pt2)
            for kc in range(8):
                S.op("pe", lambda e, kc=kc: e.transpose(ptv2[:, kc * 128:(kc + 1) * 128], h2[:, kc * 128:(kc + 1) * 128], identb[:]),
                     reads=[h2b, bconst], writes=[pt2b])
            S.op("act", lambda e: e.activation(out=h2T[:, :, i * 128:(i + 1) * 128], in_=ptv2.rearrange("p (a b) -> p a b", b=128), func=AF.Copy),
                 reads=[pt2b], writes=[bh2T])
            yield

    def c_main(job, blk, cset):
        x1, h2T, bx1, bh2T = x1s[cset], h2Ts[cset], bx1s[cset], bh2Ts[cset]
        for c in range(22):
            wgc, wgb_, wgk = wgr.next()
            S.op("sp", lambda e: e.dma_start(out=wgc, in_=wg_b[:, c * 128:(c + 1) * 128].rearrange("(kc p) n -> p kc n", p=128)),
                 writes=[wgb_], dsem="cg%d" % wgk)
            wuc, wub_, wuk = wur.next()
            S.op("sp", lambda e: e.dma_start(out=wuc, in_=wu_b[:, c * 128:(c + 1) * 128].rearrange("(kc p) n -> p kc n", p=128)),
                 writes=[wub_], dsem="cu%d" % wuk)
            pg, pgb, _ = PS.next()
            mm_group(pg[:, :], [(wgc[:, kc, :], h2T[:, kc, :]) for kc in range(8)], pgb, [wgb_, bh2T])
            pu, pub, _ = PS.next()
            mm_group(pu[:, :], [(wuc[:, kc, :], h2T[:, kc, :]) for kc in range(8)], pub, [wub_, bh2T])
            sg, sgb, _ = sgr.next()
            S.op("act", lambda e: e.activation(out=sg, in_=pg[:, :], func=AF.Silu), reads=[pgb], writes=[sgb])
            S.op("dve", lambda e: e.tensor_tensor(out=aT[:, c, :], in0=pu[:, :], in1=sg, op=ALU.mult),
                 reads=[pub, sgb], writes=[baT[c]])
            yield
        for i in range(4):
            row0 = blk * 512 + i * 128
            yt, ytb, yk = yr.next()
            for half in range(2):
                pd, pdb, _ = PS.next()
                mm_group(pd[:, :], [(aT[:, c, i * 128:(i + 1) * 128], wdb[:, c, half * 512:(half + 1) * 512]) for c in range(22)],
                         pdb, baT + [bwd])
                S.op("dve", lambda e, pd=pd, half=half: e.tensor_tensor(out=yt[:, half * 512:(half + 1) * 512], in0=pd[:, :],
                                                                       in1=x1[:, i, half * 512:(half + 1) * 512], op=ALU.add),
                     reads=[pdb, bx1[i]], writes=[ytb])
                yield
            rs, rsb = rstd_of(yt, ytb, D)
            S.op("dve", lambda e: e.scalar_tensor_tensor(out=yt, in0=yt, scalar=rs, in1=gfin[:], op0=ALU.mult, op1=ALU.mult),
                 reads=[ytb, rsb, bconst], writes=[ytb])
            S.op("pool", lambda e: e.dma_start(out=yout[job][row0:row0 + 128, :], in_=yt), reads=[ytb], dsem="y%d" % yk)
            yield

    cblocks = [(job, blk) for job in range(2) for blk in range(8)]
    run_streams([c_prologue(cblocks[0][0], cblocks[0][1], 0)], 1)
    for bi, (job, blk) in enumerate(cblocks):
        streams = [c_main(job, blk, bi % 2)]
        if bi + 1 < len(cblocks):
            streams.append(c_prologue(cblocks[bi + 1][0], cblocks[bi + 1][1], (bi + 1) % 2))
        run_streams(streams, 2)
    S.emit(st)
    st.close()
    return nc, None


def _consts():
    identb = np.eye(128, dtype=np.float32).astype(ml_dtypes.bfloat16)
    identf = np.eye(128, dtype=np.float32)
    pm = np.zeros((128, 128), np.float32)
    for par in range(2):
        b = par * 64
        for i in range(8):
            pm[b + i + 8, b + i] = -1.0
            pm[b + i, b + 8 + i] = 1.0
    p = np.arange(128)[:, None]
    j = np.arange(256)[None, :]
    mask = np.where((j - p >= 0) & (j - p <= 128), 0.0, -30000.0).astype(np.float32)
    maskm = ((j - p >= 0) & (j - p <= 128)).astype(np.float32)
    s = np.arange(128)[:, None]
    t = np.arange(128)[None, :]
    c = -1.0 / 16.0
    M = np.stack([(s <= t) * c, (s >= t) * c, (s > t) * c, (s < t) * c]).astype(np.float32)
    G = np.stack([np.tile((s <= t).astype(np.float32), (1, 4)), np.tile((s > t).astype(np.float32), (1, 4))])
    return dict(c_identb=identb, c_identf=identf, c_pm=pm.astype(ml_dtypes.bfloat16), c_mask=mask.astype(ml_dtypes.bfloat16), c_maskm=maskm.astype(ml_dtypes.bfloat16),
                c_M=M, c_G=G.astype(ml_dtypes.bfloat16))


def _rot_table(pos0):
    inv_freq = np.power(np.float32(500000.0), -np.arange(0, 16, 2, dtype=np.float32) / np.float32(16))
    pos = (np.arange(EXT, dtype=np.float32) + np.float32(pos0 - HALO)).astype(np.float32)
    ang = pos[None, :] * inv_freq[:, None].astype(np.float32)
    cos = np.cos(ang).astype(np.float32)
    sin = np.sin(ang).astype(np.float32)
    C = np.ones((128, EXT), np.float32)
    Sn = np.zeros((128, EXT), np.float32)
    for par in range(2):
        b = par * 64
        C[b:b + 8] = cos
        C[b + 8:b + 16] = cos
        Sn[b:b + 8] = sin
        Sn[b + 8:b + 16] = sin
    return np.stack([C, Sn]).astype(np.float32)


_CACHE = {}


def make_in_maps(x_prompt, x_sample, g_mix, w_in, w_gate_f, b_gate_f, w_gate_b, b_gate_b, g_gla, w_out,
                 g_ffn, w_ffn_gate, w_ffn_up, w_ffn_down, g_final):
    f = lambda a: np.ascontiguousarray(np.asarray(a, dtype=np.float32))
    x_prompt = f(x_prompt); x_sample = f(x_sample)
    consts = _consts()
    shared = dict(g_mix=f(g_mix)[0], g_ffn=f(g_ffn)[0], g_gla=f(g_gla)[0], g_final=f(g_final),
                  w_in=f(w_in)[0], w_out=f(w_out)[0], w_ffn_gate=f(w_ffn_gate)[0], w_ffn_up=f(w_ffn_up)[0],
                  w_ffn_down=f(w_ffn_down)[0], w_gate_f=f(w_gate_f)[0], w_gate_b=f(w_gate_b)[0],
                  b_gate_f=f(b_gate_f)[0], b_gate_b=f(b_gate_b)[0])
    shared.update(consts)
    rotP = _rot_table(0)
    validP = np.zeros((EXT,), np.float32); validP[HALO:HALO + SEG] = 1.0
    in_maps = []
    for c in range(8):
        sq, half = c // 2, c % 2
        xs = np.zeros((3 * SEG, D), np.float32)
        xs[SEG:2 * SEG] = x_sample[sq, half * SEG:(half + 1) * SEG]
        vs = np.zeros((EXT,), np.float32); vs[HALO:HALO + SEG] = 1.0
        if half == 0:
            xs[2 * SEG:] = x_sample[sq, SEG:]
            vs[HALO + SEG:] = 1.0
        else:
            xs[:SEG] = x_sample[sq, :SEG]
            vs[:HALO] = 1.0
        m = dict(shared)
        m.update(xP=x_prompt[c], xS=xs, rotP=rotP, rotS=_rot_table(half * SEG),
                 validP=np.ascontiguousarray(validP.reshape(EXT // 128, 128).T),
                 validS=np.ascontiguousarray(vs.reshape(EXT // 128, 128).T))
        in_maps.append(m)
    return in_maps


def kernel(**inputs):
    if "nc" not in _CACHE:
        _CACHE["nc"] = build_program()
    nc, _st = _CACHE["nc"]
    in_maps = make_in_maps(**inputs)
    res = run_bass_kernel_spmd(nc, in_maps, core_ids=list(range(8)))
    yp = np.stack([np.asarray(res.results[c]["yP"], dtype=np.float32) for c in range(8)])
    ys = np.stack([np.concatenate([np.asarray(res.results[2 * s]["yS"], dtype=np.float32),
                                   np.asarray(res.results[2 * s + 1]["yS"], dtype=np.float32)], axis=0) for s in range(4)])
    return yp, ys
```

```python
import contextlib
import numpy as np
import ml_dtypes
import concourse.bass as bass
import concourse.mybir as mybir
from concourse.bass_utils import run_bass_kernel_spmd

F32 = mybir.dt.float32
BF = mybir.dt.bfloat16
AF = mybir.ActivationFunctionType
ALU = mybir.AluOpType
AX = mybir.AxisListType

D = 1024
DFF = 2816
PW = 3104
SEG = 4096
HALO = 1024
EXT = SEG + 2 * HALO
EPS = 1e-6
DEBUG = False
MASK_MODES = ("dve", "pool", "dve")
SAME_SYNC = ("act", "dve", "pool")
ENG_NAMES = ("pe", "act", "dve", "pool", "sp")
C_QD, C_KD, C_VD, C_QA, C_KA, C_VA, C_RA, C_G = 0, 512, 1024, 1536, 1792, 2048, 2560, 3072


class _Rec:
    def __getattr__(self, name):
        def f(*a, **k):
            self.call = (name, a, k)
            return None
        return f


class Op:
    __slots__ = ("eng", "fn", "deps", "sig", "cnt", "dsem", "dval")

    def __init__(self, eng, fn, deps, dsem=None, dval=0):
        self.eng = eng
        rec = _Rec()
        fn(rec)
        self.fn = rec.call
        self.deps = deps
        self.sig = False
        self.cnt = 0
        self.dsem = dsem
        self.dval = dval


class Buf:
    __slots__ = ("w", "r")

    def __init__(self):
        self.w = None
        self.r = []


class Sched:
    def __init__(self, nc):
        self.nc = nc
        self.ops = {e: [] for e in ENG_NAMES}
        self.dma_cnt = {}
        self.last_dma = {}
        self.bar = []

    def op(self, eng, fn, reads=(), writes=(), dsem=None):
        deps = list(self.bar)
        for b in reads:
            if b.w is not None:
                deps.append(b.w)
        for b in writes:
            if b.w is not None:
                deps.append(b.w)
            deps.extend(b.r)
        dval = 0
        if dsem is not None:
            self.dma_cnt[dsem] = self.dma_cnt.get(dsem, 0) + 16
            dval = self.dma_cnt[dsem]
        o = Op(eng, fn, deps, dsem, dval)
        if dsem is not None:
            self.last_dma[dsem] = o
        self.ops[eng].append(o)
        for b in reads:
            b.r.append(o)
        for b in writes:
            b.w = o
            b.r = []
        return o

    def barrier(self):
        bar = []
        for e in ENG_NAMES:
            for o in reversed(self.ops[e]):
                if o.dsem is None:
                    bar.append(o)
                    break
        bar.extend(self.last_dma.values())
        self.bar = bar

    def emit(self, stack):
        nc = self.nc
        for e in ENG_NAMES:
            for o in self.ops[e]:
                for d in o.deps:
                    if d.dsem is None and (d.eng != o.eng or o.eng in SAME_SYNC):
                        d.sig = True
        esem = {}
        for e in ENG_NAMES:
            c = 0
            for o in self.ops[e]:
                if o.sig:
                    c += 1
                    o.cnt = c
            esem[e] = stack.enter_context(nc.semaphore("s_" + e))
        dsems = {k: stack.enter_context(nc.semaphore("d_%s" % k)) for k in self.dma_cnt}
        block = stack.enter_context(nc.Block())
        engobj = {"pe": block.tensor, "act": block.scalar, "dve": block.vector,
                  "pool": block.gpsimd, "sp": block.sync}
        finals = list(self.last_dma.values())

        def run(ename):
            def body(eng):
                waited = {}
                for o in self.ops[ename]:
                    need = {}
                    for d in o.deps:
                        if d.dsem is not None:
                            key, sem, val = ("d", d.dsem), dsems[d.dsem], d.dval
                        else:
                            if d.eng == ename and ename not in SAME_SYNC:
                                continue
                            key, sem, val = ("e", d.eng), esem[d.eng], d.cnt
                        if waited.get(key, 0) >= val:
                            continue
                        if key not in need or need[key][1] < val:
                            need[key] = (sem, val)
                    for key, (sem, val) in need.items():
                        eng.wait_ge(sem, val)
                        waited[key] = val
                    nm_, a_, k_ = o.fn
                    inst = getattr(eng, nm_)(*a_, **k_)
                    if o.dsem is not None:
                        inst.then_inc(dsems[o.dsem], 16)
                    elif o.sig:
                        inst.then_inc(esem[ename], 1)
                if ename == "sp":
                    for d in finals:
                        if waited.get(("d", d.dsem), 0) < d.dval:
                            eng.wait_ge(dsems[d.dsem], d.dval)
            return body

        for e in ENG_NAMES:
            engobj[e](run(e))


class Ring:
    def __init__(self, items):
        self.items = [(ap, Buf()) for ap in items]
        self.i = 0
        self.base = id(self) % 100000

    def next(self):
        k = self.i % len(self.items)
        self.i += 1
        ap, b = self.items[k]
        return ap, b, k


def build_program():
    nc = bass.Bass("TRN2", target_bir_lowering=False)
    st = contextlib.ExitStack()
    S = Sched(nc)

    def din(name, shape, dt=F32):
        return nc.dram_tensor(name, shape, dt, kind="ExternalInput").ap()

    def dscr(name, shape, dt=BF):
        if DEBUG:
            return nc.dram_tensor(name, shape, dt, kind="ExternalOutput").ap()
        return nc.dram_tensor(name, shape, dt).ap()

    xin = [din("xP", [SEG, D]), din("xS", [3 * SEG, D])]
    rot = [din("rotP", [2, 128, EXT]), din("rotS", [2, 128, EXT])]
    valid = [din("validP", [128, EXT // 128]), din("validS", [128, EXT // 128])]
    g_mix = din("g_mix", [D]); g_ffn = din("g_ffn", [D]); g_gla = din("g_gla", [128]); g_final = din("g_final", [D])
    w_in = din("w_in", [D, PW]); w_out = din("w_out", [D, D])
    w_g = din("w_ffn_gate", [D, DFF]); w_u = din("w_ffn_up", [D, DFF]); w_d = din("w_ffn_down", [DFF, D])
    wgate = [din("w_gate_f", [16, 256]), din("w_gate_b", [16, 256])]
    bgate = [din("b_gate_f", [256]), din("b_gate_b", [256])]
    c_identb = din("c_identb", [128, 128], BF); c_identf = din("c_identf", [128, 128])
    c_pm = din("c_pm", [128, 128], BF); c_mask = din("c_mask", [128, 256], BF); c_maskm = din("c_maskm", [128, 256], BF)
    c_M = din("c_M", [4, 128, 128]); c_G = din("c_G", [2, 128, 512], BF)
    yout = [nc.dram_tensor("yP", [SEG, D], F32, kind="ExternalOutput").ap(),
            nc.dram_tensor("yS", [SEG, D], F32, kind="ExternalOutput").ap()]
    win_b = dscr("win_b", [D, PW]); wout_b = dscr("wout_b", [D, D])
    wg_b = dscr("wg_b", [D, DFF]); wu_b = dscr("wu_b", [D, DFF]); wd_b = dscr("wd_b", [DFF, D])
    qT_s = [dscr("qT_s%d" % j, [4, 128, SEG]) for j in range(2)]
    kT_s = [dscr("kT_s%d" % j, [4, 128, EXT]) for j in range(2)]
    v_s = [dscr("v_s%d" % j, [EXT, 520]) for j in range(2)]
    o_s = [dscr("o_s%d" % j, [SEG, D]) for j in range(2)]
    B_scr = {}

    def dbuf(key):
        if key not in B_scr:
            B_scr[key] = Buf()
        return B_scr[key]

    def sb(name, shape, dt):
        return st.enter_context(nc.sbuf_tensor(name, shape, dt))

    identb = sb("identb", [128, 128], BF); identf = sb("identf", [128, 128], F32)
    pm = sb("pm", [128, 128], BF); mask01 = sb("mask01", [128, 256], BF); mask01m = sb("mask01m", [128, 256], BF)
    Mm = sb("Mm", [128, 4, 128], F32)
    Gm = sb("Gm", [128, 2, 512], BF)
    negcol = sb("negcol", [128, 2], F32); epsc = sb("epsc", [128, 1], F32); onec = sb("onec", [128, 1], F32)
    gt = sb("gt", [128, 18], F32)
    gfin = sb("gfin", [128, D], F32)
    wgs = sb("wgs", [32, 2, 256], F32); wga = sb("wga", [32, 2, 256], BF)
    gaug = sb("gaug", [32, 2, 512], BF)
    Sst = sb("Sst", [128, 2, 2, 128], F32)
    Fsnaps = [sb("Fsnap%d" % i, [128, 4, 2, 128], BF) for i in range(2)]
    small = sb("small", [128, 64], F32)
    ARENA = 47700
    arena = sb("arena", [128, ARENA], F32)
    psb = [st.enter_context(nc.psum_tensor("pb%d" % i, [128, 512], F32)) for i in range(8)]
    bconst = Buf()
    cnt = [0]

    def dkey(tag):
        return tag

    cl = [(identb[:], c_identb[:, :]), (identf[:], c_identf[:, :]), (pm[:], c_pm[:, :]), (mask01[:], c_mask[:, :]), (mask01m[:], c_maskm[:, :]),
          (gfin[:], g_final.partition_broadcast(128))]
    for k in range(4):
        cl.append((Mm[:, k, :], c_M[k]))
    for k in range(2):
        cl.append((Gm[:, k, :], c_G[k]))
        cl.append((wgs[0:16, k, :], wgate[k][:, :]))
        cl.append((wgs[16:17, k, :], bgate[k].rearrange("(a n) -> a n", a=1)))
    cl.append((gt[:, 0:8], g_mix.rearrange("(c p) -> p c", p=128)))
    cl.append((gt[:, 8:16], g_ffn.rearrange("(c p) -> p c", p=128)))
    cl.append((gt[:, 16:17], g_gla.rearrange("(p a) -> p a", a=1)))
    S.op("pool", lambda e: e.memset(wgs[:], 0.0), writes=[bconst])
    for i, (dst, src) in enumerate(cl):
        S.op("sp", lambda e, dst=dst, src=src: e.dma_start(out=dst, in_=src, allow_slow_non_contiguous=True),
             writes=[bconst], dsem="c%d" % (i % 4))
    S.op("dve", lambda e: e.memset(negcol[:], -1.0 / 16.0), writes=[bconst])
    S.op("dve", lambda e: e.memset(epsc[:], EPS), writes=[bconst])
    S.op("dve", lambda e: e.memset(onec[:], 1.0), writes=[bconst])
    S.op("dve", lambda e: e.memset(gt[:, 17:18], 1.0), writes=[bconst])
    S.op("dve", lambda e: e.memset(gaug[:], 1.0), writes=[bconst])
    S.barrier()
    S.op("dve", lambda e: e.tensor_copy(wga[0:17], wgs[0:17]), reads=[bconst], writes=[bconst])

    class Arena:
        def __init__(self):
            self.off = 0

        def take(self, nbytes_per_part, dt, shape):
            words = (nbytes_per_part + 3) // 4
            words = (words + 7) // 8 * 8
            a = arena[:, self.off:self.off + words]
            self.off += words
            assert self.off <= ARENA, ("arena overflow", self.off)
            if dt == BF:
                a = a.bitcast(BF)
                n = int(np.prod(shape[1:]))
                a = a[:, 0:n]
            else:
                n = int(np.prod(shape[1:]))
                a = a[:, 0:n]
            if len(shape) == 3:
                a = a.rearrange("p (a b) -> p a b", b=shape[2])
            elif len(shape) == 4:
                a = a.rearrange("p (a b c) -> p a b c", b=shape[2], c=shape[3])
            return a

        def ring(self, n, dt, shape):
            esz = 2 if dt == BF else 4
            nb = int(np.prod(shape[1:])) * esz
            return Ring([self.take(nb, dt, shape) for _ in range(n)])

    PS = Ring([p[:] for p in psb])

    def psbf(ap):
        return ap.bitcast(BF)

    smallr = Ring([small[:, k * 4:k * 4 + 4] for k in range(16)])

    def smallcol(n=1):
        ap, b, _ = smallr.next()
        return ap[:, 0:n], b

    A = Arena()
    stg = A.ring(4, F32, [128, PW])
    stgb = A.ring(4, BF, [128, PW])
    wi = [0]
    for (W, Wb, K, N, g0) in [(w_in, win_b, D, PW, 0), (w_out, wout_b, D, D, None), (w_g, wg_b, D, DFF, 8),
                              (w_u, wu_b, D, DFF, 8), (w_d, wd_b, DFF, D, None)]:
        for kc in range(K // 128):
            a, ab, ak = stg.next()
            o, ob, ok = stgb.next()
            gcol = 17
            if g0 is not None:
                gcol = g0 + kc
            elif W is w_out and kc >= 4:
                gcol = 16
            S.op("sp", lambda e, a=a, W=W, kc=kc, N=N: e.dma_start(out=a[:, 0:N], in_=W[kc * 128:(kc + 1) * 128, :]),
                 writes=[ab], dsem="wl%d" % ak)
            if wi[0] % 2 == 0:
                S.op("act", lambda e, a=a, o=o, N=N, gcol=gcol: e.activation(out=o[:, 0:N], in_=a[:, 0:N], func=AF.Copy,
                                                                                 scale=gt[:, gcol:gcol + 1]),
                     reads=[ab, bconst], writes=[ob])
            else:
                S.op("dve", lambda e, a=a, o=o, N=N, gcol=gcol: e.tensor_scalar(out=o[:, 0:N], in0=a[:, 0:N],
                                                                                    scalar1=gt[:, gcol:gcol + 1], scalar2=None,
                                                                                    op0=ALU.mult),
                     reads=[ab, bconst], writes=[ob])
            wi[0] += 1
            S.op("pool", lambda e, o=o, Wb=Wb, kc=kc, N=N: e.dma_start(out=Wb[kc * 128:(kc + 1) * 128, :], in_=o[:, 0:N]),
                 reads=[ob], dsem="ws%d" % ok)
    S.barrier()

    A = Arena()
    winb = A.take(8 * PW * 2, BF, [128, 8, PW])
    Ssnap = A.take(32 * 2 * 128 * 2, BF, [128, 32, 2, 128])
    Xr = A.ring(2, F32, [128, D])
    Hr = A.ring(2, BF, [128, D])
    hTr = A.ring(3, BF, [128, 8, 512])
    junk = A.take(D * 2, BF, [128, D])
    sprs = [[[A.take(1024, F32, [128, 256]) for _ in range(2)] for _ in range(4)] for _ in range(2)]
    spbs = [[[Buf() for _ in range(2)] for _ in range(4)] for _ in range(2)]
    etmp = A.ring(1, F32, [128, 256])
    eDr = A.ring(1, F32, [128, 256])
    kdr = A.ring(8, BF, [128, 256])
    vbr = A.ring(8, BF, [128, 512])
    decr = A.ring(8, F32, [128, 4])
    Er = A.ring(4, F32, [128, 512])
    qin = [[A.take(1024, BF, [128, 512]) for _ in range(2)] for _ in range(2)]
    kin = [[A.take(1024, BF, [128, 512]) for _ in range(2)] for _ in range(2)]
    qinb = [[Buf() for _ in range(2)] for _ in range(2)]
    kinb = [[Buf() for _ in range(2)] for _ in range(2)]
    attr = A.ring(6, BF, [128, 512])
    sqr = A.ring(2, F32, [128, 512])
    onr = A.ring(2, F32, [128, 512])
    err = A.ring(2, F32, [128, 512])
    ogr = A.ring(2, BF, [128, 512])
    rotr = A.ring(1, F32, [128, 2, 512])
    qrawr = A.ring(2, BF, [128, 512])
    t1r = A.ring(1, F32, [128, 512])
    t2r = A.ring(1, F32, [128, 512])
    qfr = A.ring(2, BF, [128, 512])
    vaugr = A.ring(2, BF, [128, 8, 65])
    validt = A.take(48 * 4, F32, [128, 48])
    zt = Ssnap.rearrange("p a b c -> p (a b c)")[:, 0:1024]
    bwin = Buf(); bvalid = Buf()
    bS = [Buf(), Buf()]; bSq = [[Buf() for _ in range(4)] for _ in range(2)]; bSsnap = Buf(); bzt = bSsnap; bFsnaps = [Buf(), Buf()]; bgaug = [Buf(), Buf()]

    for kc in range(8):
        S.op("sp", lambda e, kc=kc: e.dma_start(out=winb[:, kc, :], in_=win_b[kc * 128:(kc + 1) * 128, :]),
             writes=[bwin], dsem="wi%d" % (kc % 2))
    S.op("pool", lambda e: e.memset(zt[:], 0.0), writes=[bzt])

    def mm_group(out, pairs, ps_b, reads):
        n = len(pairs)
        for i, (l, r) in enumerate(pairs):
            S.op("pe", lambda e, l=l, r=r, i=i: e.matmul(out, l, r, start=(i == 0), stop=(i == n - 1)),
                 reads=reads, writes=[ps_b])

    def rstd_of(src, src_b, width):
        ss, ssb = smallcol()
        S.op("act", lambda e: e.activation(out=junk[:, 0:width], in_=src, func=AF.Square, accum_out=ss),
             reads=[src_b], writes=[ssb, bjunk])
        ln, lnb = smallcol()
        S.op("act", lambda e: e.activation(out=ln, in_=ss, func=AF.Ln, scale=1.0 / width, bias=epsc[:, 0:1]),
             reads=[ssb, bconst], writes=[lnb])
        rs, rsb = smallcol()
        S.op("act", lambda e: e.activation(out=rs, in_=ln, func=AF.Exp, scale=-0.5), reads=[lnb], writes=[rsb])
        return rs, rsb

    bjunk = Buf()

    dbg = {}
    if DEBUG:
        dbg["x"] = nc.dram_tensor("dbgx", [32, 128, D], F32, kind="ExternalOutput").ap()
        dbg["h"] = nc.dram_tensor("dbgh", [32, 128, D], BF, kind="ExternalOutput").ap()
        dbg["r"] = nc.dram_tensor("dbgr", [32, 128, 4], F32, kind="ExternalOutput").ap()
        dbg["hT"] = nc.dram_tensor("dbghT", [8, 128, 8, 512], BF, kind="ExternalOutput").ap()

    def prep_tile(xsrc_rows, hT, hTb, col, dbi=None):
        xt, xb, xk = Xr.next()
        S.op("sp", lambda e: e.dma_start(out=xt, in_=xsrc_rows), writes=[xb], dsem="x%d" % xk)
        rs, rsb = rstd_of(xt, xb, D)
        hb, hbb, _ = Hr.next()
        S.op("dve", lambda e: e.tensor_scalar(out=hb, in0=xt, scalar1=rs, scalar2=None, op0=ALU.mult),
             reads=[xb, rsb], writes=[hbb])
        pt, ptb, _ = PS.next()
        ptv = psbf(pt)
        for kc in range(8):
            S.op("pe", lambda e, kc=kc: e.transpose(ptv[:, kc * 128:(kc + 1) * 128], hb[:, kc * 128:(kc + 1) * 128], identb[:]),
                 reads=[hbb, bconst], writes=[ptb])
        S.op("act", lambda e: e.activation(out=hT[:, :, col:col + 128], in_=ptv.rearrange("p (a b) -> p a b", b=128),
                                           func=AF.Copy),
             reads=[ptb], writes=[hTb])
        if DEBUG and dbi is not None:
            S.op("sp", lambda e: e.dma_start(out=dbg["x"][dbi], in_=xt), reads=[xb], dsem="dbg0")
            S.op("sp", lambda e: e.dma_start(out=dbg["h"][dbi], in_=hb), reads=[hbb], dsem="dbg1")
            S.op("sp", lambda e: e.dma_start(out=dbg["r"][dbi][:, 0:1], in_=rs, allow_slow_non_contiguous=True), reads=[rsb], dsem="dbg2")
            if col == 384:
                S.op("sp", lambda e: e.dma_start(out=dbg["hT"][dbi // 4], in_=hT), reads=[hTb], dsem="dbg3")

    def interleave(gens, width):
        pending = list(gens)
        active = []
        while pending or active:
            while pending and len(active) < width:
                active.append(pending.pop(0))
            for g in list(active):
                try:
                    next(g)
                except StopIteration:
                    active.remove(g)
                yield

    def run_streams(gens, width):
        for _ in interleave(gens, width):
            pass

    def gla_tokmajor(hT, hTb, ti, col, dirs, scan, snap_dst, res, gset, split_state=False):
        spr, spb = sprs[gset], spbs[gset]
        bFsnap = bFsnaps[gset]
        lh = [hT[:, kc, col:col + 128] for kc in range(8)]
        pv, pvb, _ = PS.next()
        mm_group(pv[:, :], [(lh[kc], winb[:, kc, C_VA:C_VA + 512]) for kc in range(8)], pvb, [hTb, bwin])
        vb, vbb, _ = vbr.next()
        S.op("dve", lambda e: e.tensor_copy(vb, pv[:, :]), reads=[pvb], writes=[vbb])
        res["vb"] = (vb, vbb)
        for d in dirs:
            pz, pzb, _ = PS.next()
            S.op("pe", lambda e, d=d: e.matmul(pz[:, 0:256], gaug[0:17, d, col:col + 128], wga[0:17, d, :], start=True, stop=True),
                 reads=[bgaug[d], bconst], writes=[pzb])
            et_, etb, _ = etmp.next()
            S.op("act", lambda e, et_=et_: e.activation(out=et_, in_=pz[:, 0:256], func=AF.Exp, scale=-1.0),
                 reads=[pzb], writes=[etb])
            sp_, spb_ = spr[ti][d], spb[ti][d]
            S.op("act", lambda e, et_=et_, sp_=sp_: e.activation(out=sp_, in_=et_, func=AF.Ln, bias=onec[:, 0:1]),
                 reads=[etb, bconst], writes=[spb_])
        yield
        d = scan
        sp_, spb_ = spr[ti][d], spb[ti][d]
        pd, pdb, _ = PS.next()
        S.op("pe", lambda e: e.matmul(pd[:, 0:256], Mm[:, 2 + d, :], sp_, start=True, stop=True),
             reads=[spb_, bconst], writes=[pdb])
        for j in range(2):
            S.op("pe", lambda e, j=j: e.matmul(pd[:, 256 + 2 * j:258 + 2 * j], sp_[:, j * 128:(j + 1) * 128], negcol[:, 0:2],
                                               start=True, stop=True),
                 reads=[spb_, bconst], writes=[pdb])
        eD, eDb, _ = eDr.next()
        S.op("act", lambda e: e.activation(out=eD, in_=pd[:, 0:256], func=AF.Exp), reads=[pdb], writes=[eDb])
        dec, decb, _ = decr.next()
        S.op("act", lambda e: e.activation(out=dec, in_=pd[:, 256:260], func=AF.Exp), reads=[pdb], writes=[decb])
        pk, pkb, _ = PS.next()
        mm_group(pk[:, 0:256], [(lh[kc], winb[:, kc, C_KA:C_KA + 256]) for kc in range(8)], pkb, [hTb, bwin])
        kd, kdb, _ = kdr.next()
        S.op("dve", lambda e: e.tensor_tensor(out=kd, in0=pk[:, 0:256], in1=eD, op=ALU.mult), reads=[pkb, eDb], writes=[kdb])
        yield

        def state_gen():
            if snap_dst is not None:
                S.op("act", lambda e: e.activation(out=snap_dst, in_=Sst[:, d, :, :], func=AF.Copy), reads=bSq[d],
                     writes=[bSsnap if d == 1 else bFsnap])
            pkv, pkvb, _ = PS.next()
            for j in range(2):
                S.op("pe", lambda e, j=j: e.matmul(pkv[:, j * 256:(j + 1) * 256], kd[:, j * 128:(j + 1) * 128],
                                                   vb[:, j * 256:(j + 1) * 256], start=True, stop=True),
                     reads=[kdb, vbb], writes=[pkvb])
            for j in range(2):
                for par in range(2):
                    rows = slice(par * 64, (par + 1) * 64)
                    sbuf_ = bSq[d][2 * j + par]
                    S.op("dve", lambda e, j=j, par=par, rows=rows: e.scalar_tensor_tensor(
                        out=Sst[rows, d, j, :], in0=Sst[rows, d, j, :], scalar=dec[rows, 2 * j:2 * j + 1],
                        in1=pkv[rows, j * 256 + par * 128:j * 256 + (par + 1) * 128], op0=ALU.mult, op1=ALU.add),
                        reads=[pkvb, decb, sbuf_], writes=[sbuf_])
            yield
        if split_state:
            res["state"] = state_gen()
        else:
            for _ in state_gen():
                yield

    def prep_group(job, tiles, full):
        hT, hTb, _ = hTr.next()

        def gen():
            xsrc = xin[job]
            for ti in range(4):
                et = tiles[ti]
                prep_tile(xsrc[et * 128:(et + 1) * 128, :], hT, hTb, ti * 128, dbi=(et if (full and job == 0) else None))
                yield
        return (hT, hTb), gen()

    def proc_group(job, tiles, scan, full, dilkv, gset, hTctx):
        Fsnap = Fsnaps[gset]
        base = tiles[0]
        hT, hTb = hTctx
        order = list(range(4)) if scan == 0 else [3, 2, 1, 0]
        dirs = [0, 1] if full else [scan]

        def gproj_gen():
            for d in dirs:
                pg, pgb, _ = PS.next()
                mm_group(pg[0:16, :], [(winb[:, kc, C_G + 16 * d:C_G + 16 * d + 16], hT[:, kc, :]) for kc in range(8)], pgb, [hTb, bwin])
                S.op("act", lambda e, d=d, pg=pg: e.activation(out=gaug[0:16, d, :], in_=pg[0:16, :], func=AF.Copy),
                     reads=[pgb], writes=[bgaug[d]])
            yield
        tm = {}
        gens = []
        for ti in order:
            et = tiles[ti]
            is_main = (job == 0) or (32 <= et < 64)
            mt = et - (32 if job == 1 else 0)
            if not is_main:
                snap_dst = None
            elif scan == 1:
                snap_dst = Ssnap[:, mt, :, :]
            else:
                snap_dst = Fsnap[:, ti, :, :]
            tm[ti] = {}
            gens.append(gla_tokmajor(hT, hTb, ti, ti * 128, dirs, scan, snap_dst, tm[ti], gset, split_state=(not full)))

        def dil_gen():
            det0 = (base + 8) if job == 0 else (base - 24)
            ecol = det0 * 128
            rt, rtb, rk = rotr.next()
            S.op("sp", lambda e: e.dma_start(out=rt, in_=rot[job][:, :, ecol:ecol + 512].rearrange("a p n -> p a n")),
                 writes=[rtb], dsem="rot%d" % rk)
            which = [("k", C_KD)] + ([("q", C_QD)] if full else [])

            def finish(nm, hp, qr, qrb):
                pp, ppb, _ = PS.next()
                S.op("pe", lambda e: e.matmul(pp[:, :], pm[:], qr, start=True, stop=True),
                     reads=[qrb, bconst], writes=[ppb])
                t1, t1b, _ = t1r.next()
                S.op("dve", lambda e: e.tensor_tensor(out=t1, in0=pp[:, :], in1=rt[:, 1, :], op=ALU.mult),
                     reads=[ppb, rtb], writes=[t1b])
                t2, t2b, _ = t2r.next()
                S.op("pool", lambda e: e.tensor_tensor(out=t2, in0=qr, in1=rt[:, 0, :], op=ALU.mult),
                     reads=[qrb, rtb], writes=[t2b])
                qf, qfb, qfk = qfr.next()
                S.op("dve", lambda e: e.tensor_tensor(out=qf, in0=t1, in1=t2, op=ALU.add),
                     reads=[t1b, t2b], writes=[qfb])
                if nm == "q":
                    mcol = (base - (32 if job == 1 else 0)) * 128
                    dst = qT_s[job][hp][:, mcol:mcol + 512]
                else:
                    dst = kT_s[job][hp][:, ecol:ecol + 512]
                S.op("pool", lambda e: e.dma_start(out=dst, in_=qf), reads=[qfb], dsem="qf%d" % qfk)

            prev = None
            for hp in range(4):
                for nm, c0 in which:
                    pq, pqb, _ = PS.next()
                    mm_group(pq[:, :], [(winb[:, kc, c0 + hp * 128:c0 + (hp + 1) * 128], hT[:, kc, :]) for kc in range(8)],
                             pqb, [hTb, bwin])
                    qr, qrb, _ = qrawr.next()
                    S.op("act", lambda e, qr=qr, pq=pq: e.activation(out=qr, in_=pq[:, :], func=AF.Copy), reads=[pqb], writes=[qrb])
                    if prev is not None:
                        finish(*prev)
                    prev = (nm, hp, qr, qrb)
                    yield
            for ti in range(4):
                col = ti * 128
                pv, pvb, _ = PS.next()
                mm_group(pv[:, :], [(hT[:, kc, col:col + 128], winb[:, kc, C_VD:C_VD + 512]) for kc in range(8)], pvb, [hTb, bwin])
                if prev is not None:
                    finish(*prev)
                    prev = None
                va, vab, vak = vaugr.next()
                S.op("act", lambda e, va=va, pv=pv: e.activation(out=va[:, :, 0:64], in_=pv[:, :].rearrange("p (a b) -> p a b", b=64),
                                                                   func=AF.Copy), reads=[pvb], writes=[vab])
                det = det0 + ti
                S.op("dve", lambda e, va=va, det=det: e.tensor_copy(va[:, :, 64:65],
                                                                    validt[:, det:det + 1].broadcast_to([128, 8]).rearrange("p (a b) -> p a b", b=1)),
                     reads=[bvalid], writes=[vab])
                S.op("pool", lambda e, va=va, det=det: e.dma_start(out=v_s[job][det * 128:(det + 1) * 128, :],
                                                                   in_=va.rearrange("p a b -> p (a b)")),
                     reads=[vab], dsem="va%d" % vak)
                yield

        if dilkv:
            gens.insert(2, dil_gen())

        def xgen():
            for _ in gproj_gen():
                yield
            for _ in interleave(gens, 3):
                yield
        def xbgen():
            for ti in order:
                for _ in tm[ti]["state"]:
                    yield
        return xgen(), ((job, tiles, hT, hTb, tm, gset) if full else None), (None if full else xbgen())

    def y_gen(job, tiles, hT, hTb, tm, gset):
        spr, spb = sprs[gset], spbs[gset]
        Fsnap, bFsnap = Fsnaps[gset], bFsnaps[gset]
        for j in range(2):
            EE = []
            for d in range(2):
                pb_, pbb, _ = PS.next()
                for ti in range(4):
                    S.op("pe", lambda e, ti=ti: e.matmul(pb_[:, ti * 128:(ti + 1) * 128], spr[ti][d][:, j * 128:(j + 1) * 128],
                                                         Mm[:, d, :], start=True, stop=True),
                         reads=[spb[ti][d], bconst], writes=[pbb])
                Ep, Epb, _ = Er.next()
                S.op("act", lambda e: e.activation(out=Ep, in_=pb_[:, :], func=AF.Exp), reads=[pbb], writes=[Epb])
                En, Enb, _ = Er.next()
                S.op("act", lambda e: e.activation(out=En, in_=pb_[:, :], func=AF.Exp, scale=-1.0), reads=[pbb], writes=[Enb])
                EE.append((Ep, Epb, En, Enb))
            yield
            pqa, pqab, _ = PS.next()
            mm_group(pqa[:, :], [(winb[:, kc, C_QA + j * 128:C_QA + (j + 1) * 128], hT[:, kc, :]) for kc in range(8)], pqab, [hTb, bwin])
            pka, pkab, _ = PS.next()
            mm_group(pka[:, :], [(winb[:, kc, C_KA + j * 128:C_KA + (j + 1) * 128], hT[:, kc, :]) for kc in range(8)], pkab, [hTb, bwin])
            for d in range(2):
                Ep, Epb, En, Enb = EE[d]
                S.op("dve", lambda e: e.scalar_tensor_tensor(out=qin[d][j], in0=pqa[:, :], scalar=0.125, in1=Ep,
                                                             op0=ALU.mult, op1=ALU.mult),
                     reads=[pqab, Epb], writes=[qinb[d][j]])
                S.op("dve", lambda e: e.tensor_tensor(out=kin[d][j], in0=pka[:, :], in1=En, op=ALU.mult),
                     reads=[pkab, Enb], writes=[kinb[d][j]])
            yield

        def y_tile(ti):
            et = tiles[ti]
            mt = et - (32 if job == 1 else 0)
            col = ti * 128
            vb, vbb = tm[ti]["vb"]
            atts = []
            for d in range(2):
                at, atb, _ = attr.next()
                atv = at.rearrange("p (j q t) -> p j q t", q=2, t=128)
                for par in range(2):
                    pa, pab, _ = PS.next()
                    rows = slice(par * 64, (par + 1) * 64)
                    for j in range(2):
                        S.op("pe", lambda e, j=j: e.matmul(pa[:, j * 128:(j + 1) * 128], kin[d][j][rows, col:col + 128],
                                                           qin[d][j][rows, col:col + 128], start=True, stop=True),
                             reads=[kinb[d][j], qinb[d][j]], writes=[pab])
                    S.op("dve", lambda e: e.tensor_tensor(out=atv[:, :, par, :], in0=pa[:, 0:256].rearrange("p (j t) -> p j t", t=128),
                                                          in1=Gm[:, d, 0:256].rearrange("p (j t) -> p j t", t=128), op=ALU.mult),
                         reads=[pab, bconst], writes=[atb])
                atts.append((at, atb))
            pr, prb, _ = PS.next()
            mm_group(pr[:, :], [(hT[:, kc, col:col + 128], winb[:, kc, C_RA:C_RA + 512]) for kc in range(8)], prb, [hTb, bwin])
            er, erb, _ = err.next()
            S.op("act", lambda e: e.activation(out=er, in_=pr[:, :], func=AF.Exp, scale=-1.0), reads=[prb], writes=[erb])
            S.op("pool", lambda e: e.tensor_scalar(out=er, in0=er, scalar1=1.0, scalar2=1.0, op0=ALU.add, op1=ALU.mult),
                 reads=[erb], writes=[erb])
            S.op("dve", lambda e: e.reciprocal(out=er, in_=er), reads=[erb], writes=[erb])
            S.op("dve", lambda e: e.tensor_tensor(out=er, in0=pr[:, :], in1=er, op=ALU.mult), reads=[prb, erb], writes=[erb])
            yield
            po, pob, _ = PS.next()
            for h in range(4):
                j, par = h // 2, h % 2
                rows = slice(par * 64, (par + 1) * 64)
                hc = slice(h * 128, (h + 1) * 128)
                pairs = [(atts[0][0][:, hc], vb[:, hc]), (atts[1][0][:, hc], vb[:, hc]),
                         (qin[0][j][rows, col:col + 128], Fsnap[rows, ti, j, :]),
                         (qin[1][j][rows, col:col + 128], Ssnap[rows, mt, j, :])]
                mm_group(po[:, hc], pairs, pob, [atts[0][1], atts[1][1], vbb, qinb[0][j], qinb[1][j], bFsnap, bSsnap])
            sq, sqb, _ = sqr.next()
            S.op("act", lambda e: e.activation(out=sq, in_=po[:, :], func=AF.Square), reads=[pob], writes=[sqb])
            ss, ssb = smallcol(4)
            S.op("dve", lambda e: e.tensor_reduce(out=ss, in_=sq.rearrange("p (a b) -> p a b", b=128), axis=AX.X, op=ALU.add),
                 reads=[sqb], writes=[ssb])
            ln, lnb = smallcol(4)
            S.op("act", lambda e: e.activation(out=ln, in_=ss, func=AF.Ln, scale=1.0 / 128, bias=epsc[:, 0:1]),
                 reads=[ssb, bconst], writes=[lnb])
            rs, rsb = smallcol(4)
            S.op("act", lambda e: e.activation(out=rs, in_=ln, func=AF.Exp, scale=-0.5), reads=[lnb], writes=[rsb])
            on, onb, _ = onr.next()
            for h in range(4):
                S.op("act" if h % 2 else "dve", (lambda e, h=h: e.activation(out=on[:, h * 128:(h + 1) * 128], in_=po[:, h * 128:(h + 1) * 128],
                                                                            func=AF.Copy, scale=rs[:, h:h + 1])) if h % 2 else
                     (lambda e, h=h: e.tensor_scalar(out=on[:, h * 128:(h + 1) * 128], in0=po[:, h * 128:(h + 1) * 128],
                                                     scalar1=rs[:, h:h + 1], scalar2=None, op0=ALU.mult)),
                     reads=[pob, rsb], writes=[onb])
            og, ogb, ogk = ogr.next()
            S.op("pool", lambda e: e.tensor_tensor(out=og, in0=on, in1=er, op=ALU.mult), reads=[onb, erb], writes=[ogb])
            S.op("pool", lambda e: e.dma_start(out=o_s[job][mt * 128:(mt + 1) * 128, 512:1024], in_=og),
                 reads=[ogb], dsem="og%d" % ogk)
            yield

        for _ in interleave([y_tile(ti) for ti in range(4)], 2):
            yield

    def zero_state(d):
        S.op("dve", lambda e: e.memset(Sst[:, d, :, :], 0.0), reads=bSq[d], writes=bSq[d])

    for job in range(2):
        S.op("sp", lambda e, job=job: e.dma_start(out=validt[:], in_=valid[job][:, :]), writes=[bvalid], dsem="vld")
        if job == 0:
            for hp in range(4):
                for c0 in (0, HALO + SEG):
                    S.op("pool", lambda e, hp=hp, c0=c0: e.dma_start(out=kT_s[0][hp][:, c0:c0 + HALO], in_=zt[:, 0:HALO]),
                         reads=[bzt], dsem="z0")
            for r0 in (0, HALO + SEG):
                for k in range(8):
                    S.op("pool", lambda e, r0=r0, k=k: e.dma_start(out=v_s[0][r0 + k * 128:r0 + (k + 1) * 128, :], in_=zt[:, 0:520]),
                         reads=[bzt], dsem="z1")
            moff = 0
        else:
            moff = 32
        descs = []
        if job == 1:
            for g in range(23, 15, -1):
                descs.append(([4 * g + i for i in range(4)], 1, False, g < 18))
        for g in range(7, -1, -1):
            descs.append(([moff + 4 * g + i for i in range(4)], 1, False, False))
        nback = len(descs)
        if job == 1:
            for g in range(0, 8):
                descs.append(([4 * g + i for i in range(4)], 0, False, g >= 6))
        for g in range(8):
            descs.append(([moff + 4 * g + i for i in range(4)], 0, True, True))
        zero_state(1)
        hctx, pg0 = prep_group(job, descs[0][0], descs[0][2])
        run_streams([pg0], 1)
        prev_y = None
        prev_xb = None
        nfull = 0
        for i, (tiles, scan, full, dilkv) in enumerate(descs):
            if i == nback:
                zero_state(0)
            gset = nfull % 2
            xg, yctx, xb = proc_group(job, tiles, scan, full, dilkv, gset, hctx)
            streams = [xg]
            if prev_xb is not None:
                streams.insert(0, prev_xb)
            prev_xb = xb
            if prev_y is not None:
                streams.append(y_gen(*prev_y))
            if i + 1 < len(descs):
                hctx, pgn = prep_group(job, descs[i + 1][0], descs[i + 1][2])
                streams.append(pgn)
            run_streams(streams, 4)
            prev_y = yctx
            if full:
                nfull += 1
        run_streams([y_gen(*prev_y)], 1)
    S.barrier()

    A = Arena()
    qT = A.take(SEG * 2, BF, [128, SEG]); q4 = A.take(SEG * 2, BF, [128, 4, SEG // 4]); q16 = A.take(SEG * 2, BF, [128, 16, SEG // 16])
    kT = A.take(EXT * 2, BF, [128, EXT]); k4 = A.take(EXT * 2, BF, [128, 4, EXT // 4]); k16 = A.take(EXT * 2, BF, [128, 16, EXT // 16])
    NT1, NT4, NT16 = 33, 9, 3
    vg1 = A.take(NT1 * 260, BF, [128, NT1, 130])
    vg4 = A.take(4 * NT4 * 260, BF, [128, 4 * NT4, 130])
    vg16 = A.take(16 * NT16 * 260, BF, [128, 16 * NT16, 130])
    accTs = [[A.take(SEG * 4, F32, [128, SEG]) for _ in range(2)] for _ in range(2)]
    Pr = A.ring(4, BF, [128, 256])
    Pmr = A.ring(10, BF, [128, 256])
    ostr = A.ring(2, BF, [128, 128])
    bq = Buf(); bk = Buf(); bqp = [None, Buf(), Buf()]; bkp = [None, Buf(), Buf()]; bvgs = [Buf(), Buf(), Buf()]; baccs = [[Buf(), Buf()], [Buf(), Buf()]]
    bqp[0] = bq; bkp[0] = bk
    PSs = Ring([psb[i][:, 0:256] for i in range(4)])
    PSu = [Ring([psb[4][:], psb[5][:]]), Ring([psb[6][:], psb[7][:]])]

    for job in range(2):
        for hp in range(4):
            accT = accTs[hp % 2]; bacc = baccs[hp % 2]
            S.op("sp", lambda e, job=job, hp=hp: e.dma_start(out=qT[:], in_=qT_s[job][hp]), writes=[bq], dsem="bq")
            S.op("sp", lambda e, job=job, hp=hp: e.dma_start(out=kT[:], in_=kT_s[job][hp]), writes=[bk], dsem="bk")
            S.op("dve", lambda e: e.tensor_copy(q4[:], qT.rearrange("p (l r) -> p r l", r=4)), reads=[bq], writes=[bqp[1]])
            S.op("dve", lambda e: e.tensor_copy(k4[:], kT.rearrange("p (l r) -> p r l", r=4)), reads=[bk], writes=[bkp[1]])
            S.op("dve", lambda e: e.tensor_copy(q16[:], qT.rearrange("p (l r) -> p r l", r=16)), reads=[bq], writes=[bqp[2]])
            S.op("act", lambda e: e.activation(out=k16[:], in_=kT.rearrange("p (l r) -> p r l", r=16), func=AF.Copy), reads=[bk], writes=[bkp[2]])
            vcols = slice(hp * 130, (hp + 1) * 130)
            vsrc = v_s[job]
            for t0 in (0, 11, 22):
                S.op("sp", lambda e: e.dma_start(out=vg1[:, t0:t0 + 11, :], in_=vsrc[960 + t0 * 128:960 + (t0 + 11) * 128, vcols].rearrange("(t p) c -> p t c", p=128)),
                     writes=[bvgs[0]], dsem="bv0")
            for r in range(4):
                S.op("sp", lambda e, vsrc=vsrc, vcols=vcols, r=r: e.dma_start(
                    out=vg4[:, r * NT4:(r + 1) * NT4, :],
                    in_=vsrc[r + 768:r + 768 + 4 * (NT4 * 128 - 1) + 1:4, vcols].rearrange("(t p) c -> p t c", p=128)),
                    writes=[bvgs[1]], dsem="bv1")
            for r in range(16):
                S.op("sp", lambda e, vsrc=vsrc, vcols=vcols, r=r: e.dma_start(
                    out=vg16[:, r * NT16:(r + 1) * NT16, :],
                    in_=vsrc[r:r + 16 * (NT16 * 128 - 1) + 1:16, vcols].rearrange("(t p) c -> p t c", p=128)),
                    writes=[bvgs[2]], dsem="bv2")
            pats = [(1, 32, qT.rearrange("p (r l) -> p r l", r=1), kT.rearrange("p (r l) -> p r l", r=1), vg1, 1024),
                    (4, 8, q4, k4, vg4, 256), (16, 2, q16, k16, vg16, 64)]
            units = []
            for pi, pat in enumerate(pats):
                for r in range(pat[0]):
                    curs = [{}, {}]
                    for kt in range(pat[1] + 1):
                        for par in range(2):
                            units.append((pi, pat, r, par, kt, curs[par]))
            staged = {}

            def stage1(ui):
                pi, (dil, nqt, qp, kp, vg, Q0), r, par, kt, cur = units[ui]
                rows = slice(par * 64, (par + 1) * 64)
                qa = max(0, kt - 1); qb_ = min(nqt - 1, kt)
                n = (qb_ - qa + 1) * 128
                jlo = 128 * (qa - (kt - 1))
                k0 = Q0 + 128 * kt - 64
                ps_, psb_, _ = PSs.next()
                mode = MASK_MODES[ui % len(MASK_MODES)]
                P2, P2b, _ = Pmr.next()
                if mode == "pe":
                    S.op("pe", lambda e: e.matmul(ps_[:, 0:n], identb[:], mask01[:, jlo:jlo + n], start=True, stop=False),
                         reads=[bconst], writes=[psb_])
                    S.op("pe", lambda e: e.matmul(ps_[:, 0:n], kp[rows, r, k0:k0 + 128], qp[rows, r, 128 * qa:128 * qa + n], start=False, stop=True),
                         reads=[bqp[pi], bkp[pi]], writes=[psb_])
                    S.op("act", lambda e: e.activation(out=P2[:, 0:n], in_=ps_[:, 0:n], func=AF.Exp, scale=0.125),
                         reads=[psb_], writes=[P2b])
                else:
                    S.op("pe", lambda e: e.matmul(ps_[:, 0:n], kp[rows, r, k0:k0 + 128], qp[rows, r, 128 * qa:128 * qa + n], start=True, stop=True),
                         reads=[bqp[pi], bkp[pi]], writes=[psb_])
                    P, Pb, _ = Pr.next()
                    S.op("act", lambda e: e.activation(out=P[:, 0:n], in_=ps_[:, 0:n], func=AF.Exp, scale=0.125),
                         reads=[psb_], writes=[Pb])
                    S.op(mode, lambda e: e.tensor_tensor(out=P2[:, 0:n], in0=P[:, 0:n], in1=mask01m[:, jlo:jlo + n], op=ALU.mult),
                         reads=[Pb, bconst], writes=[P2b])
                staged[ui] = (P2, P2b, qa)

            def stage2(ui):
                pi, (dil, nqt, qp, kp, vg, Q0), r, par, kt, cur = units[ui]
                ntile = nqt + 1
                acc = accT[par]
                P2, P2b, qa = staged.pop(ui)
                vgt = vg[:, r * ntile + kt, par * 65:(par + 1) * 65]
                qts = [qt for qt in (kt - 1, kt) if 0 <= qt <= nqt - 1]
                if len(qts) == 2 and kt % 4 != 0:
                    groups = [qts]
                else:
                    groups = [[qt] for qt in qts]
                for grp in groups:
                    qt0 = grp[0]
                    bank = qt0 // 4
                    first = (bank not in cur)
                    if first:
                        cur[bank] = PSu[par].next()
                    pu, pub, _ = cur[bank]
                    last = (kt == grp[-1] + 1) and (grp[-1] % 4 == 3 or grp[-1] == nqt - 1)
                    c0 = (qt0 % 4) * 128
                    nn = 128 * len(grp)
                    p0 = (qt0 - qa) * 128
                    S.op("pe", lambda e: e.matmul(pu[0:65, c0:c0 + nn], vgt, P2[:, p0:p0 + nn], start=first, stop=last),
                         reads=[P2b, bvgs[pi]], writes=[pub])
                    if last:
                        qt = grp[-1]
                        q0 = (qt // 4) * 4
                        ncol = (qt - q0 + 1) * 128
                        s0 = r + dil * 128 * q0
                        dst = acc[0:65, s0:s0 + dil * (ncol - 1) + 1:dil]
                        if pi == 0:
                            S.op("dve", lambda e: e.tensor_copy(dst, pu[0:65, 0:ncol]),
                                 reads=[pub], writes=[bacc[par]])
                        else:
                            S.op("dve", lambda e: e.tensor_tensor(out=dst, in0=pu[0:65, 0:ncol], in1=dst, op=ALU.add),
                                 reads=[pub, bacc[par]], writes=[bacc[par]])

            LA = 3
            for i in range(len(units) + LA):
                if i < len(units):
                    stage1(i)
                if i - LA >= 0:
                    stage2(i - LA)
            for i in range(32):
                os_, osb, osk = ostr.next()
                for par in range(2):
                    pf, pfb, _ = PSs.next()
                    S.op("pe", lambda e, pf=pf, par=par, i=i: e.transpose(pf[:, 0:65], accT[par][0:65, i * 128:(i + 1) * 128], identf[0:65, 0:65]),
                         reads=[bacc[par], bconst], writes=[pfb])
                    rc, rcb = smallcol()
                    S.op("dve", lambda e, rc=rc, pf=pf: e.reciprocal(out=rc, in_=pf[:, 64:65]), reads=[pfb], writes=[rcb])
                    S.op("act", lambda e, os_=os_, pf=pf, rc=rc, par=par: e.activation(out=os_[:, par * 64:(par + 1) * 64], in_=pf[:, 0:64], func=AF.Copy, scale=rc),
                         reads=[pfb, rcb], writes=[osb])
                S.op("pool", lambda e, os_=os_, job=job, hp=hp, i=i: e.dma_start(out=o_s[job][i * 128:(i + 1) * 128, hp * 128:(hp + 1) * 128], in_=os_),
                     reads=[osb], dsem="os%d" % osk)
    S.barrier()

    A = Arena()
    woutb = A.take(8 * D * 2, BF, [128, 8, D])
    wdb = A.take(22 * D * 2, BF, [128, 22, D])
    wgr = A.ring(3, BF, [128, 8, 128])
    wur = A.ring(3, BF, [128, 8, 128])
    x1s = [A.take(4 * D * 4, F32, [128, 4, D]) for _ in range(2)]
    h2Ts = [A.take(8 * 512 * 2, BF, [128, 8, 512]) for _ in range(2)]
    aT = A.take(22 * 512 * 2, BF, [128, 22, 512])
    otr = A.ring(2, BF, [128, D])
    oTr = A.ring(2, BF, [128, 8, 128])
    xr2 = A.ring(2, F32, [128, D])
    h2r = A.ring(2, BF, [128, D])
    sgr = A.ring(2, F32, [128, 512])
    yr = A.ring(2, F32, [128, D])
    junk2 = A.take(D * 2, BF, [128, D])
    bwo = Buf(); bwd = Buf(); bx1s = [[Buf() for _ in range(4)] for _ in range(2)]; bh2Ts = [Buf(), Buf()]; baT = [Buf() for _ in range(22)]
    for kc in range(8):
        S.op("sp", lambda e, kc=kc: e.dma_start(out=woutb[:, kc, :], in_=wout_b[kc * 128:(kc + 1) * 128, :]), writes=[bwo], dsem="cw0")
    for c in range(22):
        S.op("sp", lambda e, c=c: e.dma_start(out=wdb[:, c, :], in_=wd_b[c * 128:(c + 1) * 128, :]), writes=[bwd], dsem="cw1")
    junk = junk2

    def c_prologue(job, blk, cset):
        x1, h2T, bx1, bh2T = x1s[cset], h2Ts[cset], bx1s[cset], bh2Ts[cset]
        xoff = 0 if job == 0 else SEG
        for i in range(4):
            row0 = blk * 512 + i * 128
            ot, otb, otk = otr.next()
            S.op("sp", lambda e: e.dma_start(out=ot, in_=o_s[job][row0:row0 + 128, :]), writes=[otb], dsem="co%d" % otk)
            xt, xb, xk = xr2.next()
            S.op("sp", lambda e: e.dma_start(out=xt, in_=xin[job][xoff + row0:xoff + row0 + 128, :]), writes=[xb], dsem="cx%d" % xk)
            pt, ptb, _ = PS.next()
            ptv = psbf(pt)
            for kc in range(8):
                S.op("pe", lambda e, kc=kc: e.transpose(ptv[:, kc * 128:(kc + 1) * 128], ot[:, kc * 128:(kc + 1) * 128], identb[:]),
                     reads=[otb, bconst], writes=[ptb])
            oT, oTb, _ = oTr.next()
            S.op("act", lambda e: e.activation(out=oT, in_=ptv.rearrange("p (a b) -> p a b", b=128), func=AF.Copy),
                 reads=[ptb], writes=[oTb])
            yield
            for half in range(2):
                py, pyb, _ = PS.next()
                mm_group(py[:, :], [(oT[:, kc, :], woutb[:, kc, half * 512:(half + 1) * 512]) for kc in range(8)], pyb, [oTb, bwo])
                S.op("dve", lambda e, py=py, half=half: e.tensor_tensor(out=x1[:, i, half * 512:(half + 1) * 512], in0=py[:, :],
                                                                       in1=xt[:, half * 512:(half + 1) * 512], op=ALU.add),
                     reads=[pyb, xb], writes=[bx1[i]])
            rs, rsb = rstd_of(x1[:, i, :], bx1[i], D)
            h2, h2b, _ = h2r.next()
            S.op("pool", lambda e: e.tensor_scalar(out=h2, in0=x1[:, i, :], scalar1=rs, scalar2=1.0, op0=ALU.mult, op1=ALU.mult),
                 reads=[bx1[i], rsb], writes=[h2b])
            yield
            pt2, pt2b, _ = PS.next()
            ptv2 = psbf(pt2)
            for kc in range(8):
                S.op("pe", lambda e, kc=kc: e.transpose(ptv2[:, kc * 128:(kc + 1) * 128], h2[:, kc * 128:(kc + 1) * 128], identb[:]),
                     reads=[h2b, bconst], writes=[pt2b])
            S.op("act", lambda e: e.activation(out=h2T[:, :, i * 128:(i + 1) * 128], in_=ptv2.rearrange("p (a b) -> p a b", b=128), func=AF.Copy),
                 reads=[pt2b], writes=[bh2T])
            yield

    def c_main(job, blk, cset):
        x1, h2T, bx1, bh2T = x1s[cset], h2Ts[cset], bx1s[cset], bh2Ts[cset]
        for c in range(22):
            wgc, wgb_, wgk = wgr.next()
            S.op("sp", lambda e: e.dma_start(out=wgc, in_=wg_b[:, c * 128:(c + 1) * 128].rearrange("(kc p) n -> p kc n", p=128)),
                 writes=[wgb_], dsem="cg%d" % wgk)
            wuc, wub_, wuk = wur.next()
            S.op("sp", lambda e: e.dma_start(out=wuc, in_=wu_b[:, c * 128:(c + 1) * 128].rearrange("(kc p) n -> p kc n", p=128)),
                 writes=[wub_], dsem="cu%d" % wuk)
            pg, pgb, _ = PS.next()
            mm_group(pg[:, :], [(wgc[:, kc, :], h2T[:, kc, :]) for kc in range(8)], pgb, [wgb_, bh2T])
            pu, pub, _ = PS.next()
            mm_group(pu[:, :], [(wuc[:, kc, :], h2T[:, kc, :]) for kc in range(8)], pub, [wub_, bh2T])
            sg, sgb, _ = sgr.next()
            S.op("act", lambda e: e.activation(out=sg, in_=pg[:, :], func=AF.Silu), reads=[pgb], writes=[sgb])
            S.op("dve", lambda e: e.tensor_tensor(out=aT[:, c, :], in0=pu[:, :], in1=sg, op=ALU.mult),
                 reads=[pub, sgb], writes=[baT[c]])
            yield
        for i in range(4):
            row0 = blk * 512 + i * 128
            yt, ytb, yk = yr.next()
            for half in range(2):
                pd, pdb, _ = PS.next()
                mm_group(pd[:, :], [(aT[:, c, i * 128:(i + 1) * 128], wdb[:, c, half * 512:(half + 1) * 512]) for c in range(22)],
                         pdb, baT + [bwd])
                S.op("dve", lambda e, pd=pd, half=half: e.tensor_tensor(out=yt[:, half * 512:(half + 1) * 512], in0=pd[:, :],
                                                                       in1=x1[:, i, half * 512:(half + 1) * 512], op=ALU.add),
                     reads=[pdb, bx1[i]], writes=[ytb])
                yield
            rs, rsb = rstd_of(yt, ytb, D)
            S.op("dve", lambda e: e.scalar_tensor_tensor(out=yt, in0=yt, scalar=rs, in1=gfin[:], op0=ALU.mult, op1=ALU.mult),
                 reads=[ytb, rsb, bconst], writes=[ytb])
            S.op("pool", lambda e: e.dma_start(out=yout[job][row0:row0 + 128, :], in_=yt), reads=[ytb], dsem="y%d" % yk)
            yield

    cblocks = [(job, blk) for job in range(2) for blk in range(8)]
    run_streams([c_prologue(cblocks[0][0], cblocks[0][1], 0)], 1)
    for bi, (job, blk) in enumerate(cblocks):
        streams = [c_main(job, blk, bi % 2)]
        if bi + 1 < len(cblocks):
            streams.append(c_prologue(cblocks[bi + 1][0], cblocks[bi + 1][1], (bi + 1) % 2))
        run_streams(streams, 2)
    S.emit(st)
    st.close()
    return nc, None


def _consts():
    identb = np.eye(128, dtype=np.float32).astype(ml_dtypes.bfloat16)
    identf = np.eye(128, dtype=np.float32)
    pm = np.zeros((128, 128), np.float32)
    for par in range(2):
        b = par * 64
        for i in range(8):
            pm[b + i + 8, b + i] = -1.0
            pm[b + i, b + 8 + i] = 1.0
    p = np.arange(128)[:, None]
    j = np.arange(256)[None, :]
    mask = np.where((j - p >= 0) & (j - p <= 128), 0.0, -30000.0).astype(np.float32)
    maskm = ((j - p >= 0) & (j - p <= 128)).astype(np.float32)
    s = np.arange(128)[:, None]
    t = np.arange(128)[None, :]
    c = -1.0 / 16.0
    M = np.stack([(s <= t) * c, (s >= t) * c, (s > t) * c, (s < t) * c]).astype(np.float32)
    G = np.stack([np.tile((s <= t).astype(np.float32), (1, 4)), np.tile((s > t).astype(np.float32), (1, 4))])
    return dict(c_identb=identb, c_identf=identf, c_pm=pm.astype(ml_dtypes.bfloat16), c_mask=mask.astype(ml_dtypes.bfloat16), c_maskm=maskm.astype(ml_dtypes.bfloat16),
                c_M=M, c_G=G.astype(ml_dtypes.bfloat16))


def _rot_table(pos0):
    inv_freq = np.power(np.float32(500000.0), -np.arange(0, 16, 2, dtype=np.float32) / np.float32(16))
    pos = (np.arange(EXT, dtype=np.float32) + np.float32(pos0 - HALO)).astype(np.float32)
    ang = pos[None, :] * inv_freq[:, None].astype(np.float32)
    cos = np.cos(ang).astype(np.float32)
    sin = np.sin(ang).astype(np.float32)
    C = np.ones((128, EXT), np.float32)
    Sn = np.zeros((128, EXT), np.float32)
    for par in range(2):
        b = par * 64
        C[b:b + 8] = cos
        C[b + 8:b + 16] = cos
        Sn[b:b + 8] = sin
        Sn[b + 8:b + 16] = sin
    return np.stack([C, Sn]).astype(np.float32)


_CACHE = {}


def make_in_maps(x_prompt, x_sample, g_mix, w_in, w_gate_f, b_gate_f, w_gate_b, b_gate_b, g_gla, w_out,
                 g_ffn, w_ffn_gate, w_ffn_up, w_ffn_down, g_final):
    f = lambda a: np.ascontiguousarray(np.asarray(a, dtype=np.float32))
    x_prompt = f(x_prompt); x_sample = f(x_sample)
    consts = _consts()
    shared = dict(g_mix=f(g_mix)[0], g_ffn=f(g_ffn)[0], g_gla=f(g_gla)[0], g_final=f(g_final),
                  w_in=f(w_in)[0], w_out=f(w_out)[0], w_ffn_gate=f(w_ffn_gate)[0], w_ffn_up=f(w_ffn_up)[0],
                  w_ffn_down=f(w_ffn_down)[0], w_gate_f=f(w_gate_f)[0], w_gate_b=f(w_gate_b)[0],
                  b_gate_f=f(b_gate_f)[0], b_gate_b=f(b_gate_b)[0])
    shared.update(consts)
    rotP = _rot_table(0)
    validP = np.zeros((EXT,), np.float32); validP[HALO:HALO + SEG] = 1.0
    in_maps = []
    for c in range(8):
        sq, half = c // 2, c % 2
        xs = np.zeros((3 * SEG, D), np.float32)
        xs[SEG:2 * SEG] = x_sample[sq, half * SEG:(half + 1) * SEG]
        vs = np.zeros((EXT,), np.float32); vs[HALO:HALO + SEG] = 1.0
        if half == 0:
            xs[2 * SEG:] = x_sample[sq, SEG:]
            vs[HALO + SEG:] = 1.0
        else:
            xs[:SEG] = x_sample[sq, :SEG]
            vs[:HALO] = 1.0
        m = dict(shared)
        m.update(xP=x_prompt[c], xS=xs, rotP=rotP, rotS=_rot_table(half * SEG),
                 validP=np.ascontiguousarray(validP.reshape(EXT // 128, 128).T),
                 validS=np.ascontiguousarray(vs.reshape(EXT // 128, 128).T))
        in_maps.append(m)
    return in_maps


def kernel(**inputs):
    if "nc" not in _CACHE:
        _CACHE["nc"] = build_program()
    nc, _st = _CACHE["nc"]
    in_maps = make_in_maps(**inputs)
    res = run_bass_kernel_spmd(nc, in_maps, core_ids=list(range(8)))
    yp = np.stack([np.asarray(res.results[c]["yP"], dtype=np.float32) for c in range(8)])
    ys = np.stack([np.concatenate([np.asarray(res.results[2 * s]["yS"], dtype=np.float32),
                                   np.asarray(res.results[2 * s + 1]["yS"], dtype=np.float32)], axis=0) for s in range(4)])
    return yp, ys
```
